# Optimizing a Trainium2 kernel written in Bass

```python
import math
import jax, jax.numpy as jnp
from jax import lax
import numpy as np

D_MODEL = 1024
BATCH = 1
SEQ = 16384
DEPTH = 4

HD = 64
BLK = 128
MEM_LEN = 256
NEG_INF = -1e30
TINY = 1e-30
EPS = 1e-6

A_HEADS = 8
A_KV = 2
CMP_LEN = 32
CMP_STRIDE = 16
CMP_HIDDEN = 128
SEL_BLK = 64
SEL_TOPK = 16
A_WINDOW = 512
SEL_FORCE = 1e4
B_HEADS = 8
B_HEADDIM = 64
B_GROUPS = 2
B_STATE = 128
B_CONV = 4
B_CHUNK = 128
B_INNER = B_HEADS * B_HEADDIM
B_CONV_DIM = B_INNER + 2 * B_GROUPS * B_STATE
C_HEADS = 8
C_KV = 2
C_WINDOW = 128
ROPE_THETA = 150000.0
D_SLOTS = 8
D_PATTERNS = ((128, 1), (512, 4), (2048, 16))
D_NPAT = 3
M_HEADS = 4
N_BUCKETS = 32
MAX_DIST = 2048
N_BIAS_HEADS = A_HEADS + D_NPAT * D_SLOTS

IN_SPLITS = (
    A_HEADS * HD, 6 * A_KV * HD, 3 * A_HEADS, A_HEADS * HD,
    B_CONV_DIM, B_HEADS, B_INNER,
    C_HEADS * HD, 2 * C_KV * HD, C_HEADS * HD,
    D_NPAT * D_SLOTS * HD, 2 * D_SLOTS * HD, D_SLOTS * HD,
    M_HEADS * HD, M_HEADS * HD,
)
D_IN = 8224
MIX_WIDTH = A_HEADS * HD + B_INNER + C_HEADS * HD + D_SLOTS * HD + M_HEADS * HD

kernel_name = "hymba_parallel_hybrid_nsa_ssd_swa_dilated"


def rms_norm(x, w):
    xf = x.astype(jnp.float32)
    y = xf * lax.rsqrt(jnp.mean(xf * xf, axis=-1, keepdims=True) + EPS)
    return (y * w.astype(jnp.float32)).astype(x.dtype)


def t5_bucket(dist):
    dist = jnp.maximum(dist, 0)
    exact = N_BUCKETS // 2
    rel = jnp.maximum(dist, exact).astype(jnp.float32)
    large = exact + (jnp.log(rel / exact) / math.log(MAX_DIST / exact) * (N_BUCKETS - exact)).astype(jnp.int32)
    return jnp.where(dist < exact, dist, jnp.minimum(large, N_BUCKETS - 1))


def masked_softmax(s, mask):
    s = jnp.where(mask, s, NEG_INF)
    p = jnp.where(mask, jnp.exp(s - jnp.max(s, axis=-1, keepdims=True)), 0.0)
    return p / jnp.maximum(jnp.sum(p, axis=-1, keepdims=True), TINY)


def rope(t, positions):
    half = HD // 2
    inv = ROPE_THETA ** (-jnp.arange(half, dtype=jnp.float32) / half)
    ang = positions.astype(jnp.float32)[..., None] * inv
    cos, sin = jnp.cos(ang)[:, :, None, :], jnp.sin(ang)[:, :, None, :]
    t1, t2 = t[..., :half].astype(jnp.float32), t[..., half:].astype(jnp.float32)
    return jnp.concatenate([t1 * cos - t2 * sin, t2 * cos + t1 * sin], axis=-1).astype(t.dtype)


def banded_attention(q, k, v, max_dist, rel_cols=None, dist_scale=1, sinks=None):
    b, L, hq, hd = q.shape
    hkv = k.shape[2]
    g = hq // hkv
    nb = L // BLK
    n_prev = -(-max_dist // BLK)
    kw = (n_prev + 1) * BLK
    qb = q.reshape(b, nb, BLK, hkv, g, hd)

    def windows(t):
        tp = jnp.pad(t, ((0, 0), (n_prev * BLK, 0), (0, 0), (0, 0))).reshape(b, nb + n_prev, BLK, hkv, hd)
        return jnp.concatenate([tp[:, o:o + nb] for o in range(n_prev + 1)], axis=2)

    kb, vb = windows(k), windows(v)
    s = jnp.einsum('bnqhgd,bnkhd->bnhgqk', qb, kb, preferred_element_type=jnp.float32) * hd ** -0.5
    qi = jnp.arange(BLK)[:, None]
    kj = jnp.arange(kw)[None, :]
    dist = n_prev * BLK + qi - kj
    kpos = (jnp.arange(nb)[:, None, None] - n_prev) * BLK + kj[None]
    mask = (dist >= 0) & (dist <= max_dist) & (kpos >= 0)
    if rel_cols is not None:
        bias = jnp.transpose(rel_cols[t5_bucket(dist * dist_scale)], (2, 0, 1)).reshape(hkv, g, BLK, kw)
        s = s + bias.astype(jnp.float32)
    s = jnp.where(mask[None, :, None, None], s, NEG_INF)
    m = jnp.max(s, axis=-1, keepdims=True)
    if sinks is not None:
        sk = sinks.astype(jnp.float32).reshape(1, 1, hkv, g, 1, 1)
        m = jnp.maximum(m, sk)
    p = jnp.where(mask[None, :, None, None], jnp.exp(s - m), 0.0)
    denom = jnp.sum(p, axis=-1, keepdims=True)
    if sinks is not None:
        denom = denom + jnp.exp(sk - m)
    o = jnp.einsum('bnhgqk,bnkhd->bnqhgd', (p / denom).astype(v.dtype), vb)
    lse = (m + jnp.log(denom))[..., 0]
    return o.reshape(b, L, hq, hd), jnp.transpose(lse, (0, 1, 4, 2, 3)).reshape(b, L, hq)


def compress(t, pe, w1, w2):
    b, s, hkv, hd = t.shape
    r = CMP_LEN // CMP_STRIDE
    ncmp = s // CMP_STRIDE
    u = jnp.pad(t, ((0, 0), (0, (r - 1) * CMP_STRIDE), (0, 0), (0, 0))).reshape(b, ncmp + r - 1, CMP_STRIDE, hkv, hd)
    blocks = jnp.concatenate([u[:, i:i + ncmp] for i in range(r)], axis=2) + pe[:, None, :]
    flat = jnp.transpose(blocks, (0, 1, 3, 2, 4)).reshape(b, ncmp, hkv, CMP_LEN * hd)
    return jax.nn.gelu(flat @ w1) @ w2


def nsa_mixer(q, k_c, v_c, k_s, v_s, k_w, v_w, gate_logits, pe, w1, w2, rel_a):
    b, s, hq, hd = q.shape
    g = hq // A_KV
    nb = s // BLK
    nsel = s // SEL_BLK
    n_top = min(SEL_TOPK, nsel)
    r = CMP_LEN // CMP_STRIDE
    scale = hd ** -0.5
    kc = compress(k_c, pe[0], w1[0], w2[0])
    vc = compress(v_c, pe[1], w1[1], w2[1])
    ncmp = kc.shape[1]
    cmp_end = jnp.arange(ncmp) * CMP_STRIDE + (CMP_LEN - 1)
    ksb = jnp.transpose(k_s.reshape(b, nsel, SEL_BLK, A_KV, hd), (0, 3, 1, 2, 4))
    vsb = jnp.transpose(v_s.reshape(b, nsel, SEL_BLK, A_KV, hd), (0, 3, 1, 2, 4))
    kwp = jnp.pad(k_w, ((0, 0), (A_WINDOW, 0), (0, 0), (0, 0)))
    vwp = jnp.pad(v_w, ((0, 0), (A_WINDOW, 0), (0, 0), (0, 0)))
    qg = q.reshape(b, s, A_KV, g, hd)
    gates = jax.nn.sigmoid(gate_logits.astype(jnp.float32)).reshape(b, s, A_KV, g, 3)
    rel3 = rel_a.reshape(N_BUCKETS, A_KV, g)
    gather = jax.vmap(jax.vmap(lambda tb, ix: tb[ix]))
    lookup = jax.vmap(lambda tab, bk: tab[bk], in_axes=(1, 1), out_axes=1)

    def head_bias(dist):
        return jnp.transpose(rel_a[t5_bucket(dist)], (2, 0, 1)).reshape(A_KV, g, *dist.shape).astype(jnp.float32)

    def block(n):
        t0 = n * BLK
        pos = t0 + jnp.arange(BLK)
        qb = lax.dynamic_slice_in_dim(qg, t0, BLK, axis=1)
        dist_c = pos[:, None] - cmp_end[None, :]
        s_c = jnp.einsum('bqhgd,bjhd->bhgqj', qb, kc, preferred_element_type=jnp.float32) * scale + head_bias(dist_c)
        p_c = masked_softmax(s_c, dist_c >= 0)
        o_c = jnp.einsum('bhgqj,bjhd->bqhgd', p_c.astype(vc.dtype), vc)
        pcs = jnp.pad(jnp.sum(p_c, axis=2), ((0, 0), (0, 0), (0, 0), (r - 1, 0)))
        unit = sum(pcs[..., r - 1 - i:r - 1 - i + ncmp] for i in range(r))
        imp = unit.reshape(b, A_KV, BLK, nsel, SEL_BLK // CMP_STRIDE).sum(-1)
        blk_ids = jnp.arange(nsel)[None, :]
        forced = (blk_ids == (pos // SEL_BLK)[:, None]) | (blk_ids == 0)
        valid = blk_ids * SEL_BLK <= pos[:, None]
        imp = jnp.where(forced, SEL_FORCE, jnp.where(valid, imp, -1.0))
        top_val, idx = lax.top_k(imp, n_top)
        kg = gather(ksb, idx)
        vg = gather(vsb, idx)
        kpos = idx[..., None] * SEL_BLK + jnp.arange(SEL_BLK)
        dist_s = pos[:, None, None] - kpos
        mask_s = (dist_s >= 0) & (top_val >= 0)[..., None]
        bias_s = jnp.transpose(lookup(rel3, t5_bucket(dist_s)), (0, 1, 5, 2, 3, 4)).astype(jnp.float32)
        s_s = jnp.einsum('bqhgd,bhqnld->bhgqnl', qb, kg, preferred_element_type=jnp.float32) * scale + bias_s
        nk = n_top * SEL_BLK
        p_s = masked_softmax(s_s.reshape(b, A_KV, g, BLK, nk), mask_s.reshape(b, A_KV, 1, BLK, nk))
        o_s = jnp.einsum('bhgqk,bhqkd->bqhgd', p_s.astype(vg.dtype), vg.reshape(b, A_KV, BLK, nk, hd))
        kw = lax.dynamic_slice_in_dim(kwp, t0, BLK + A_WINDOW, axis=1)
        vw = lax.dynamic_slice_in_dim(vwp, t0, BLK + A_WINDOW, axis=1)
        kpos_w = t0 - A_WINDOW + jnp.arange(BLK + A_WINDOW)
        dist_w = pos[:, None] - kpos_w[None, :]
        mask_w = (dist_w >= 0) & (dist_w < A_WINDOW) & (kpos_w[None, :] >= 0)
        s_w = jnp.einsum('bqhgd,bkhd->bhgqk', qb, kw, preferred_element_type=jnp.float32) * scale + head_bias(dist_w)
        p_w = masked_softmax(s_w, mask_w)
        o_w = jnp.einsum('bhgqk,bkhd->bqhgd', p_w.astype(vw.dtype), vw)
        gb = lax.dynamic_slice_in_dim(gates, t0, BLK, axis=1)
        o = gb[..., 0:1] * o_c + gb[..., 1:2] * o_s + gb[..., 2:3] * o_w
        return o.reshape(b, BLK, hq, hd).astype(q.dtype)

    out = lax.map(block, jnp.arange(nb))
    return jnp.transpose(out, (1, 0, 2, 3, 4)).reshape(b, s, hq, hd)


def causal_depthwise_conv(x, w, bias):
    out = lax.conv_general_dilated(x, w.astype(x.dtype)[:, None, :], window_strides=(1,),
                                   padding=[(B_CONV - 1, 0)], dimension_numbers=('NWC', 'WIO', 'NWC'),
                                   feature_group_count=x.shape[-1])
    return out + bias


def ssd_scan(xdt, adt, bm, cm):
    b, s, G, J, P = xdt.shape
    N = bm.shape[-1]
    L = B_CHUNK
    nc = s // L
    x = xdt.reshape(b, nc, L, G, J, P)
    bm = bm.reshape(b, nc, L, G, N)
    cm = cm.reshape(b, nc, L, G, N)
    acs = jnp.cumsum(jnp.transpose(adt.reshape(b, nc, L, G, J), (0, 1, 3, 4, 2)).astype(jnp.float32), axis=-1)
    causal = jnp.tril(jnp.ones((L, L), dtype=bool))
    decay = jnp.exp(jnp.where(causal, acs[..., :, None] - acs[..., None, :], NEG_INF))
    cb = jnp.einsum('bclgn,bcsgn->bcgls', cm, bm, preferred_element_type=jnp.float32)
    y_diag = jnp.einsum('bcgls,bcgjls,bcsgjp->bclgjp', cb, decay, x)
    states = jnp.einsum('bclgn,bcgjl,bclgjp->bcgjpn', bm, jnp.exp(acs[..., -1:] - acs), x).astype(jnp.float32)

    def step(h, inp):
        st, dec = inp
        return h * dec[..., None, None] + st, h

    init = jnp.zeros((b, G, J, P, N), jnp.float32)
    _, prev = lax.scan(step, init, (jnp.moveaxis(states, 1, 0), jnp.moveaxis(jnp.exp(acs[..., -1]), 1, 0)))
    y_off = jnp.einsum('bclgn,bcgjpn,bcgjl->bclgjp', cm, jnp.moveaxis(prev, 0, 1), jnp.exp(acs))
    return (y_diag + y_off).reshape(b, s, G, J, P)


def ssd_mixer(xbc, dt_raw, z, conv_w, conv_b, dt_bias, a_log, d_skip, norm_w):
    b, s, _ = xbc.shape
    J = B_HEADS // B_GROUPS
    xbc = jax.nn.silu(causal_depthwise_conv(xbc, conv_w, conv_b))
    xs = xbc[..., :B_INNER].reshape(b, s, B_GROUPS, J, B_HEADDIM)
    bm = xbc[..., B_INNER:B_INNER + B_GROUPS * B_STATE].reshape(b, s, B_GROUPS, B_STATE)
    cm = xbc[..., B_INNER + B_GROUPS * B_STATE:].reshape(b, s, B_GROUPS, B_STATE)
    dt = jax.nn.softplus(dt_raw.astype(jnp.float32) + dt_bias).reshape(b, s, B_GROUPS, J)
    a = -jnp.exp(a_log.astype(jnp.float32)).reshape(B_GROUPS, J)
    y = ssd_scan(xs * dt[..., None], dt * a, bm, cm) + xs * d_skip.reshape(B_GROUPS, J)[:, :, None]
    yz = y.reshape(b, s, B_GROUPS, J * B_HEADDIM) * jax.nn.silu(z.reshape(b, s, B_GROUPS, J * B_HEADDIM))
    return rms_norm(yz, norm_w.reshape(B_GROUPS, J * B_HEADDIM)).reshape(b, s, B_INNER)


def strided_window_attention(q, k, v, dil, steps, rel_cols):
    b, s, h, hd = q.shape
    L = s // dil
    Lp = -(-L // BLK) * BLK

    def to_sub(t):
        t = jnp.transpose(t.reshape(b, L, dil, t.shape[2], hd), (0, 2, 1, 3, 4)).reshape(b * dil, L, t.shape[2], hd)
        return jnp.pad(t, ((0, 0), (0, Lp - L), (0, 0), (0, 0)))

    o, lse = banded_attention(to_sub(q), to_sub(k), to_sub(v), steps, rel_cols=rel_cols, dist_scale=dil)
    o = jnp.transpose(o[:, :L].reshape(b, dil, L, h, hd), (0, 2, 1, 3, 4)).reshape(b, s, h, hd)
    lse = jnp.transpose(lse[:, :L].reshape(b, dil, L, h), (0, 2, 1, 3)).reshape(b, s, h)
    return o, lse


def dilated_mixer(q, k, v, rel_d):
    outs, lses = [], []
    for gi, (window, dil) in enumerate(D_PATTERNS):
        o, lse = strided_window_attention(q[:, :, gi], k, v, dil, window // dil,
                                          rel_d[:, gi * D_SLOTS:(gi + 1) * D_SLOTS])
        outs.append(o)
        lses.append(lse)
    alpha = jax.nn.softmax(jnp.stack(lses, axis=2), axis=2)
    return jnp.einsum('bsph,bsphd->bshd', alpha.astype(q.dtype), jnp.stack(outs, axis=2))


def memory_attention(q, mem, norm_w, w_kv):
    b, s, h, hd = q.shape
    kv = jnp.einsum('bmd,de->bme', rms_norm(mem, norm_w), w_kv).reshape(b, mem.shape[1], 2, h, hd)
    sc = jnp.einsum('bshd,bmhd->bhsm', q, kv[:, :, 0], preferred_element_type=jnp.float32) * hd ** -0.5
    p = jax.nn.softmax(sc, axis=-1)
    return jnp.einsum('bhsm,bmhd->bshd', p.astype(kv.dtype), kv[:, :, 1])


def setup_inputs(seed: int = 0) -> dict:
    key = jax.random.key(seed)
    ks = jax.random.split(key, 20)
    f32 = jnp.float32

    def nrm(k, shape, scale):
        return jax.random.normal(k, shape, f32) * scale

    dt = jnp.exp(jax.random.uniform(ks[12], (DEPTH, B_HEADS), f32, math.log(1e-3), math.log(1e-1)))
    return {
        "x": nrm(ks[0], (BATCH, SEQ, D_MODEL), 1.0),
        "mem": nrm(ks[1], (BATCH, MEM_LEN, D_MODEL), 1.0),
        "positions": (jax.random.randint(ks[2], (BATCH, 1), 0, 4096) + jnp.arange(SEQ, dtype=jnp.int32)[None, :]).astype(jnp.int32),
        "pre_norm": 1.0 + nrm(ks[3], (DEPTH, D_MODEL), 0.02),
        "post_norm": 1.0 + nrm(ks[4], (DEPTH, D_MODEL), 0.02),
        "w_in": nrm(ks[5], (DEPTH, D_MODEL, D_IN), D_MODEL ** -0.5),
        "w_out": nrm(ks[6], (DEPTH, MIX_WIDTH, D_MODEL), MIX_WIDTH ** -0.5),
        "rel_bias": nrm(ks[7], (N_BUCKETS, N_BIAS_HEADS), 0.1),
        "a_cmp_pos": nrm(ks[8], (DEPTH, 2, CMP_LEN, HD), 0.02),
        "a_cmp_w1": nrm(ks[9], (DEPTH, 2, CMP_LEN * HD, CMP_HIDDEN), (CMP_LEN * HD) ** -0.5),
        "a_cmp_w2": nrm(ks[10], (DEPTH, 2, CMP_HIDDEN, HD), CMP_HIDDEN ** -0.5),
        "b_conv_w": nrm(ks[11], (DEPTH, B_CONV, B_CONV_DIM), B_CONV ** -0.5),
        "b_conv_b": nrm(ks[13], (DEPTH, B_CONV_DIM), 0.01),
        "b_dt_bias": dt + jnp.log(-jnp.expm1(-dt)),
        "b_a_log": jnp.log(jax.random.uniform(ks[14], (DEPTH, B_HEADS), f32, 1.0, 16.0)),
        "b_d": 1.0 + nrm(ks[15], (DEPTH, B_HEADS), 0.1),
        "b_norm": 1.0 + nrm(ks[16], (DEPTH, B_INNER), 0.02),
        "c_sinks": nrm(ks[17], (DEPTH, C_HEADS), 0.5),
        "m_norm": 1.0 + nrm(ks[18], (DEPTH, D_MODEL), 0.02),
        "m_w_kv": nrm(ks[19], (DEPTH, D_MODEL, 2 * M_HEADS * HD), D_MODEL ** -0.5),
    }


def reference(x, mem, positions, pre_norm, post_norm, w_in, w_out, rel_bias, a_cmp_pos, a_cmp_w1, a_cmp_w2,
              b_conv_w, b_conv_b, b_dt_bias, b_a_log, b_d, b_norm, c_sinks, m_norm, m_w_kv):
    b, s, _ = x.shape
    offsets = [int(o) for o in np.cumsum(IN_SPLITS)[:-1]]
    rel_a = rel_bias[:, :A_HEADS]
    rel_d = rel_bias[:, A_HEADS:]
    silu = jax.nn.silu
    for layer in range(DEPTH):
        h = rms_norm(x, pre_norm[layer])
        proj = jnp.einsum('bsd,de->bse', h, w_in[layer])
        (a_q, a_kv, a_gate, a_z, b_xbc, b_dt, b_z, c_q, c_kv, c_z,
         d_q, d_kv, d_z, m_q, m_z) = jnp.split(proj, offsets, axis=-1)
        a_kv = a_kv.reshape(b, s, 6, A_KV, HD)
        a_out = nsa_mixer(a_q.reshape(b, s, A_HEADS, HD), a_kv[:, :, 0], a_kv[:, :, 1], a_kv[:, :, 2],
                          a_kv[:, :, 3], a_kv[:, :, 4], a_kv[:, :, 5], a_gate.reshape(b, s, A_HEADS, 3),
                          a_cmp_pos[layer], a_cmp_w1[layer], a_cmp_w2[layer], rel_a)
        b_out = ssd_mixer(b_xbc, b_dt, b_z, b_conv_w[layer], b_conv_b[layer], b_dt_bias[layer],
                          b_a_log[layer], b_d[layer], b_norm[layer])
        c_kv = c_kv.reshape(b, s, 2, C_KV, HD)
        c_out, _ = banded_attention(rope(c_q.reshape(b, s, C_HEADS, HD), positions), rope(c_kv[:, :, 0], positions),
                                    c_kv[:, :, 1], C_WINDOW - 1, sinks=c_sinks[layer])
        d_kv = d_kv.reshape(b, s, 2, D_SLOTS, HD)
        d_out = dilated_mixer(d_q.reshape(b, s, D_NPAT, D_SLOTS, HD), d_kv[:, :, 0], d_kv[:, :, 1], rel_d)
        m_out = memory_attention(m_q.reshape(b, s, M_HEADS, HD), mem, m_norm[layer], m_w_kv[layer])
        mix = jnp.concatenate([
            a_out.reshape(b, s, -1) * silu(a_z),
            b_out,
            c_out.reshape(b, s, -1) * silu(c_z),
            d_out.reshape(b, s, -1) * silu(d_z),
            m_out.reshape(b, s, -1) * silu(m_z),
        ], axis=-1)
        x = x + rms_norm(jnp.einsum('bse,ed->bsd', mix, w_out[layer]), post_norm[layer])
    return x
```

```python
import math
import os
from contextlib import ExitStack
import numpy as np
import ml_dtypes
import concourse.bass as bass
import concourse.mybir as mybir
from concourse.bass_utils import run_bass_kernel_spmd

F32 = mybir.dt.float32
BF16 = mybir.dt.bfloat16
I32 = mybir.dt.int32
AF = mybir.ActivationFunctionType
ALU = mybir.AluOpType
AX = mybir.AxisListType
NPBF = ml_dtypes.bfloat16

NCORE = 8
TPC = 2048
NT = 16
NTT = NT + 1
EXT = NTT * 128
DM = 1024
EPS = 1e-6
NEG = -30000.0
SEM_LIMIT = 30000


class Sync:
    def __init__(self, nc, n_dma_sems=12):
        self.nc = nc
        self.engs = {"pe": nc.tensor, "dve": nc.vector, "act": nc.scalar, "pool": nc.gpsimd, "sp": nc.sync}
        self.nsem = 0
        self.sem = {}
        self.semkey = {}
        self.cnt = {}
        for k in self.engs:
            self._newsem(k)
        self.waited = {k: {} for k in self.engs}
        self.dma_sems = [nc.alloc_semaphore(f"sdma{i}") for i in range(n_dma_sems)]
        self.dma_uses = [0] * n_dma_sems
        self.dma_i = 0
        self.bufs = {}
        self.nops = 0

    def _newsem(self, k):
        self.nsem += 1
        self.sem[k] = self.nc.alloc_semaphore(f"s{k}{self.nsem}")
        self.semkey[k] = f"{k}{self.nsem}"
        self.cnt[k] = 0

    def _wait(self, eng, tok):
        if tok is None:
            return
        semkey, sem, val, src = tok
        if eng == "pe" and src == "pe":
            return
        if self.waited[eng].get(semkey, 0) >= val:
            return
        self.engs[eng].wait_ge(sem, val)
        self.waited[eng][semkey] = val

    def _deps(self, eng, reads, writes):
        for k in reads:
            b = self.bufs.get(k)
            if b:
                self._wait(eng, b["w"])
                if k.startswith("ps"):
                    for t in b["r"]:
                        if t[3] != eng:
                            self._wait(eng, t)
        for k in writes:
            b = self.bufs.get(k)
            if b:
                self._wait(eng, b["w"])
                for t in b["r"]:
                    self._wait(eng, t)

    def _commit(self, tok, reads, writes):
        for k in reads:
            b = self.bufs.setdefault(k, {"w": None, "r": []})
            b["r"].append(tok)
            if len(b["r"]) > 24:
                last = {}
                for t in b["r"]:
                    if t[0] not in last or last[t[0]][2] < t[2]:
                        last[t[0]] = t
                b["r"] = list(last.values())
        for k in writes:
            self.bufs[k] = {"w": tok, "r": []}

    def op(self, eng, fn, reads=(), writes=()):
        self._deps(eng, reads, writes)
        if self.cnt[eng] >= SEM_LIMIT:
            self._newsem(eng)
        ins = fn(self.engs[eng])
        self.cnt[eng] += 1
        ins.then_inc(self.sem[eng], 1)
        tok = (self.semkey[eng], self.sem[eng], self.cnt[eng], eng)
        self._commit(tok, reads, writes)
        self.nops += 1
        return tok

    def dma(self, eng, out, in_, reads=(), writes=()):
        self._deps(eng, reads, writes)
        i = self.dma_i % len(self.dma_sems)
        self.dma_i += 1
        sem = self.dma_sems[i]
        if self.dma_uses[i] > 0:
            self._wait(eng, (f"dma{i}", sem, 16 * self.dma_uses[i], "dma"))
        if self.dma_uses[i] * 16 >= SEM_LIMIT:
            sem = self.dma_sems[i] = self.nc.alloc_semaphore(f"sdma{i}_{self.dma_i}")
            self.dma_uses[i] = 0
            self.waited_reset(f"dma{i}")
        self.dma_uses[i] += 1
        self.engs[eng].dma_start(out=out, in_=in_).then_inc(sem, 16)
        tok = (f"dma{i}", sem, 16 * self.dma_uses[i], "dma")
        self._commit(tok, reads, writes)
        self.nops += 1
        return tok

    def waited_reset(self, semkey):
        for e in self.waited:
            self.waited[e].pop(semkey, None)
        for b in self.bufs.values():
            if b["w"] is not None and b["w"][0] == semkey:
                b["w"] = None
            b["r"] = [t for t in b["r"] if t[0] != semkey]

    def release(self, keys):
        for eng in self.engs:
            for k in keys:
                b = self.bufs.get(k)
                if b:
                    self._wait(eng, b["w"])
                    for t in b["r"]:
                        self._wait(eng, t)
        for k in keys:
            self.bufs.pop(k, None)

    def finish(self, eng="sp"):
        for k, b in self.bufs.items():
            self._wait(eng, b["w"])
            for t in b["r"]:
                self._wait(eng, t)


OFF = {}
_o = 0
for _n, _w in [("a_q", 512), ("a_kc", 128), ("a_vc", 128), ("a_ks", 128), ("a_vs", 128), ("a_kw", 128), ("a_vw", 128),
               ("a_gate", 24), ("a_z", 512), ("b_x", 512), ("b_B", 256), ("b_C", 256), ("b_dt", 8), ("b_z", 512),
               ("c_q", 512), ("c_k", 128), ("c_v", 128), ("c_z", 512), ("d_q", 1536), ("d_k", 512), ("d_v", 512),
               ("d_z", 512), ("m_q", 256), ("m_z", 256)]:
    OFF[_n] = (_o, _w)
    _o += _w
assert _o == 8224


def cols(*names):
    out = []
    for n in names:
        o, w = OFF[n]
        out.extend(range(o, o + w))
    return out


A_FM = ["a_kc", "a_vc", "a_ks", "a_kw", "c_k", "d_k", "b_x", "b_B", "b_C"]
A_TM = ["a_vs", "a_vw", "c_v", "d_v", "b_dt"]
A_COLS = cols(*A_FM) + cols(*A_TM)
A_NFM = 17
A_TM0 = A_NFM * 128


class Scope:
    def __init__(self, S, nc):
        self.S, self.nc, self.es, self.names = S, nc, ExitStack(), []

    def __enter__(self):
        self.es.__enter__()
        return self

    def __exit__(self, *a):
        self.S.release(self.names)
        return self.es.__exit__(*a)

    def enter_context(self, cm):
        return self.es.enter_context(cm)


_SB_COUNT = [0]


def sb(es, nc, name, shape, dt):
    if isinstance(es, Scope):
        es.names.append(name)
    _SB_COUNT[0] += 1
    return es.enter_context(nc.sbuf_tensor(f"{name}__{_SB_COUNT[0]}", shape, dt))


def emit_consts(S, nc, es):
    C = {}
    C["identb"] = sb(es, nc, "identb", [128, 128], BF16)
    C["identf"] = sb(es, nc, "identf", [128, 128], F32)
    for nm in ("identb", "identf"):
        t = C[nm]
        S.op("pool", lambda e: e.memset(t[:], 1.0), writes=[nm])
        S.op("pool", lambda e: e.affine_select(out=t[:], in_=t[:], pattern=[[-1, 128]], compare_op=ALU.is_equal,
                                                fill=0.0, base=0, channel_multiplier=1), reads=[nm], writes=[nm])
    return C


def emit_hT(S, nc, C, x_ext, prenorm_b, hT, ps_bf, ntt=NTT):
    with Scope(S, nc) as es:
        pn = sb(es, nc, "pn", [128, DM], F32)
        S.dma("sp", pn[:], prenorm_b[:, :], writes=["pn"])
        xt = [sb(es, nc, f"xt{i}", [128, DM], F32) for i in range(2)]
        hb = [sb(es, nc, f"hb{i}", [128, DM], BF16) for i in range(2)]
        junk = sb(es, nc, "junk", [128, DM], F32)
        ss = [sb(es, nc, f"ss{i}", [128, 1], F32) for i in range(2)]
        for t in range(ntt):
            b = t % 2
            S.dma("sp", xt[b][:], x_ext[t * 128:(t + 1) * 128, :], writes=[f"xt{b}"])
            S.op("pool", lambda e: e.memset(ss[b][:], 0.0), writes=[f"ss{b}"])
            S.op("act", lambda e: e.activation(out=junk[:], in_=xt[b][:], func=AF.Square, accum_out=ss[b][:]),
                 reads=[f"xt{b}", f"ss{b}"], writes=["junk", f"ss{b}"])
            S.op("act", lambda e: e.activation(out=ss[b][:], in_=ss[b][:], func=AF.Sqrt, bias=C["epsc"][:, 0:1], scale=1.0 / DM),
                 reads=[f"ss{b}"], writes=[f"ss{b}"])
            S.op("dve", lambda e: e.reciprocal(out=ss[b][:], in_=ss[b][:]), reads=[f"ss{b}"], writes=[f"ss{b}"])
            S.op("dve", lambda e: e.scalar_tensor_tensor(out=hb[b][:], in0=xt[b][:], scalar=ss[b][:, 0:1], in1=pn[:],
                                                         op0=ALU.mult, op1=ALU.mult),
                 reads=[f"xt{b}", f"ss{b}", "pn"], writes=[f"hb{b}"])
            for k in range(8):
                S.op("pe", lambda e: e.transpose(out=ps_bf[:, k * 128:(k + 1) * 128], in_=hb[b][:, k * 128:(k + 1) * 128],
                                                 identity=C["identb"][:]),
                     reads=[f"hb{b}", "identb"], writes=["ps_bf"])
            S.op("act", lambda e: e.copy(out=hT[:, :, t * 128:(t + 1) * 128],
                                         in_=ps_bf[:, :].rearrange("p (k t) -> p k t", k=8)),
                 reads=["ps_bf"], writes=["hT"])


class WLoader:
    def __init__(self, S, nc, es, W, width, name="w"):
        self.S, self.nc, self.W, self.width, self.name = S, nc, W, width, name
        self.wf = [sb(es, nc, f"{name}f{i}", [128, 8, width], F32) for i in range(2)]
        self.wb = [sb(es, nc, f"{name}b{i}", [128, 8, width], BF16) for i in range(2)]
        self.i = 0

    def load(self, c0, w=None):
        w = w or self.width
        b = self.i % 2
        self.i += 1
        S = self.S
        S.dma("sp", self.wf[b][:, :, :w], self.W[:, c0:c0 + w].rearrange("(k p) c -> p k c", p=128),
              writes=[f"{self.name}f{b}"])
        S.op("pool", lambda e: e.tensor_copy(out=self.wb[b][:, :, :w], in_=self.wf[b][:, :, :w]),
             reads=[f"{self.name}f{b}"], writes=[f"{self.name}b{b}"])
        return self.wb[b], f"{self.name}b{b}"


def tok_groups(t0, t1, g=512):
    out = []
    while t0 < t1:
        w = min(g, t1 - t0)
        out.append((t0, w))
        t0 += w
    return out


def mm_fm(S, ps, pskey, wb, wkey, hT, t0, w, m=128, mo=0):
    for k in range(8):
        S.op("pe", lambda e: e.matmul(ps[:m, :w], lhsT=wb[:, k, mo:mo + m], rhs=hT[:, k, t0:t0 + w], start=(k == 0), stop=(k == 7)),
             reads=[wkey, "hT"], writes=[pskey])


def mm_tm(S, ps, pskey, wb, wkey, hT, t, c0, w):
    for k in range(8):
        S.op("pe", lambda e: e.matmul(ps[:, :w], lhsT=hT[:, k, t * 128:(t + 1) * 128], rhs=wb[:, k, c0:c0 + w], start=(k == 0), stop=(k == 7)),
             reads=[wkey, "hT"], writes=[pskey])


def build_A():
    nc = bass.Bass("TRN2", target_bir_lowering=False)
    D = lambda name, shape, dt, kind: nc.dram_tensor(name, shape, dt, kind=kind).ap()
    x_ext = D("x_ext", [EXT, DM], F32, "ExternalInput")
    prenorm_b = D("prenorm_b", [128, DM], F32, "ExternalInput")
    WA = D("WA", [DM, len(A_COLS)], F32, "ExternalInput")
    pos_b = D("pos_b", [128, TPC], I32, "ExternalInput")
    ropec = D("ropec", [128, 2], F32, "ExternalInput")
    ropeR = D("ropeR", [128, 128], F32, "ExternalInput")
    w1d = D("w1d", [2, 128, 32 * 128], F32, "ExternalInput")
    w2 = D("w2", [2, 128, 64], F32, "ExternalInput")
    peT = D("peT", [2, 128, 32], F32, "ExternalInput")
    o_fm = D("o_fm", [7, 128, TPC], BF16, "ExternalOutput")
    o_tm = D("o_tm", [TPC, 896], BF16, "ExternalOutput")
    o_kcT = D("o_kcT", [128, 128], BF16, "ExternalOutput")
    o_vc = D("o_vc", [128, 128], BF16, "ExternalOutput")
    o_S = D("o_S", [128, 512], F32, "ExternalOutput")
    o_D = D("o_D", [128, 8], F32, "ExternalOutput")
    g = PB()
    g.nc = nc
    g.b_cw = D("b_cw", [128, 8, 4], F32, "ExternalInput")
    g.b_cb = D("b_cb", [128, 8], F32, "ExternalInput")
    g.b_par = D("b_par", [128, 3, 8], F32, "ExternalInput")

    with ExitStack() as es:
        S = Sync(nc)
        C = emit_consts(S, nc, es)
        C["epsc"] = sb(es, nc, "epsc", [128, 1], F32)
        S.op("pool", lambda e: e.memset(C["epsc"][:], EPS), writes=["epsc"])
        hT = sb(es, nc, "hT", [128, 8, EXT], BF16)
        ps = [es.enter_context(nc.psum_tensor(f"ps{i}", [128, 512], F32)) for i in range(7)]
        ps_bf = es.enter_context(nc.psum_tensor("ps_bf", [128, 1024], BF16))
        emit_hT(S, nc, C, x_ext, prenorm_b, hT, ps_bf)
        g.S, g.C, g.hT, g.ps, g.ps_bf = S, C, hT, ps, ps_bf

        dtraw = sb(es, nc, "dtraw", [128, NTT, 8], F32)
        with Scope(S, nc) as e1:
            cosT = sb(e1, nc, "cosT", [128, TPC], F32)
            sinT = sb(e1, nc, "sinT", [128, TPC], F32)
            rc = sb(e1, nc, "rc", [128, 2], F32)
            rR = sb(e1, nc, "rR", [128, 128], F32)
            S.dma("sp", rc[:], ropec[:, :], writes=["rc"])
            S.dma("sp", rR[:], ropeR[:, :], writes=["rR"])
            emit_rope_tables(S, nc, pos_b, rc, cosT, sinT)

            wl = WLoader(S, nc, e1, WA, 128)
            kcT = sb(e1, nc, "kcT_raw", [128, 2, EXT], BF16)
            stage = [sb(e1, nc, f"stage{i}", [128, TPC], BF16) for i in range(2)]
            xf = sb(e1, nc, "xf", [128, 512], F32)
            pi = 0
            for ti in range(9):
                wb, wkey = wl.load(ti * 128)
                if ti < 2:
                    for (t0, w) in tok_groups(0, EXT):
                        p, pk = ps[pi % 4], f"ps{pi % 4}"
                        pi += 1
                        mm_fm(S, p, pk, wb, wkey, hT, t0, w)
                        S.op("act", lambda e: e.copy(out=kcT[:, ti, t0:t0 + w], in_=p[:, :w]), reads=[pk], writes=["kcT_raw"])
                else:
                    sbuf, skey = stage[ti % 2], f"stage{ti % 2}"
                    for (t0, w) in tok_groups(128, EXT):
                        p, pk = ps[pi % 4], f"ps{pi % 4}"
                        pi += 1
                        mm_fm(S, p, pk, wb, wkey, hT, t0, w)
                        l0 = t0 - 128
                        if ti == 4:
                            emit_rope_apply(S, nc, p, pk, w, xf, rR, cosT, sinT, l0, ps[4], "ps4", sbuf[:, l0:l0 + w], skey, scale=1.0)
                        else:
                            S.op("act", lambda e: e.copy(out=sbuf[:, l0:l0 + w], in_=p[:, :w]), reads=[pk], writes=[skey])
                    S.dma("sp", o_fm[ti - 2, :, :], sbuf[:, :], reads=[skey], writes=[f"o_fm{ti}"])
            wlt = WLoader(S, nc, e1, WA, 512, name="wt")
            vst = [sb(e1, nc, f"vst{i}", [128, 896], BF16) for i in range(2)]
            wbs = []
            wA, kA = wlt.load(A_TM0, 512)
            wB, kB = wlt.load(A_TM0 + 512, 384)
            for t in range(1, NTT):
                b = t % 2
                for (wb, wk, c0, w) in ((wA, kA, 0, 512), (wB, kB, 512, 384)):
                    p, pk = ps[pi % 4], f"ps{pi % 4}"
                    pi += 1
                    mm_tm(S, p, pk, wb, wk, hT, t, 0, w)
                    S.op("act", lambda e: e.copy(out=vst[b][:, c0:c0 + w], in_=p[:, :w]), reads=[pk], writes=[f"vst{b}"])
                S.dma("sp", o_tm[(t - 1) * 128:t * 128, :], vst[b][:, :], reads=[f"vst{b}"], writes=[f"o_tm{t}"])

            emit_compress(S, nc, e1, C, kcT, w1d, w2, peT, ps, o_kcT, o_vc)
            wd, kd_ = wlt.load(A_TM0 + 896, 8)
            for t in range(NTT):
                p, pk = ps[pi % 4], f"ps{pi % 4}"
                pi += 1
                mm_tm(S, p, pk, wd, kd_, hT, t, 0, 8)
                S.op("act", lambda e: e.copy(out=dtraw[:, t, :], in_=p[:, 0:8]), reads=[pk], writes=["dtraw"])
        with Scope(S, nc) as e2:
            emit_ssd(g, e2, WA, 9, dtraw, False, out_S=o_S, out_D=o_D)
        S.finish("sp")
    return nc


def emit_rope_tables(S, nc, pos_b, rc, cosT, sinT):
    with Scope(S, nc) as es:
        pi_ = sb(es, nc, "pos_i", [128, TPC], I32)
        ang = sb(es, nc, "ang", [128, TPC], F32)
        tmp = sb(es, nc, "rtmp", [128, TPC], F32)
        kf = sb(es, nc, "rkf", [128, TPC], F32)
        S.dma("sp", pi_[:], pos_b[:, :], writes=["pos_i"])
        S.op("dve", lambda e: e.tensor_copy(out=ang[:], in_=pi_[:]), reads=["pos_i"], writes=["ang"])
        S.op("dve", lambda e: e.tensor_scalar(out=ang[:], in0=ang[:], scalar1=rc[:, 0:1], scalar2=None, op0=ALU.mult),
             reads=["ang", "rc"], writes=["ang"])
        for (dst, dk, shift) in ((sinT, "sinT", 0.0), (cosT, "cosT", 0.25)):
            S.op("dve", lambda e: e.tensor_scalar(out=tmp[:], in0=ang[:], scalar1=shift, scalar2=None, op0=ALU.add),
                 reads=["ang"], writes=["rtmp"])
            S.op("dve", lambda e: e.tensor_copy(out=pi_[:], in_=tmp[:]), reads=["rtmp"], writes=["pos_i"])
            S.op("dve", lambda e: e.tensor_copy(out=kf[:], in_=pi_[:]), reads=["pos_i"], writes=["rkf"])
            S.op("dve", lambda e: e.tensor_tensor(out=tmp[:], in0=tmp[:], in1=kf[:], op=ALU.subtract),
                 reads=["rtmp", "rkf"], writes=["rtmp"])
            S.op("act", lambda e: e.activation(out=dst[:], in_=tmp[:], func=AF.Sin, scale=2.0 * math.pi),
                 reads=["rtmp"], writes=[dk])


def emit_rope_apply(S, nc, p, pk, w, xf, rR, cosT, sinT, l0, p2, p2k, dst, dkey, scale=1.0):
    S.op("act", lambda e: e.copy(out=xf[:, :w], in_=p[:, :w]), reads=[pk], writes=["xf"])
    S.op("pe", lambda e: e.matmul(p2[:, :w], lhsT=rR[:, :], rhs=xf[:, :w], start=True, stop=True), reads=["rR", "xf"], writes=[p2k])
    S.op("dve", lambda e: e.tensor_tensor(out=xf[:, :w], in0=xf[:, :w], in1=cosT[:, l0:l0 + w], op=ALU.mult),
         reads=["xf", "cosT"], writes=["xf"])
    S.op("dve", lambda e: e.tensor_tensor(out=p[:, :w], in0=p2[:, :w], in1=sinT[:, l0:l0 + w], op=ALU.mult),
         reads=[p2k, "sinT"], writes=[pk])
    if scale == 1.0:
        S.op("dve", lambda e: e.tensor_tensor(out=dst, in0=p[:, :w], in1=xf[:, :w], op=ALU.add), reads=[pk, "xf"], writes=[dkey])
    else:
        S.op("dve", lambda e: e.scalar_tensor_tensor(out=dst, in0=p[:, :w], scalar=scale, in1=xf[:, :w], op0=ALU.mult, op1=ALU.add),
             reads=[pk, "xf"], writes=[dkey])


def emit_compress(S, nc, es0, C, kcT, w1d, w2, peT, ps, o_kcT, o_vc):
    with Scope(S, nc) as es:
        w1f = sb(es, nc, "w1f", [128, 32 * 128], F32)
        w1b = [sb(es, nc, f"w1b{i}", [128, 32, 128], BF16) for i in range(2)]
        w2f = sb(es, nc, "w2f", [128, 2, 64], F32)
        w2b = sb(es, nc, "w2b", [128, 2, 64], BF16)
        pef = sb(es, nc, "pef", [128, 2, 32], F32)
        peb = sb(es, nc, "peb", [128, 2, 32], BF16)
        biasc = sb(es, nc, "biasc", [128, 2], F32)
        g1 = sb(es, nc, "g1", [128, 128], F32)
        g2 = sb(es, nc, "g2", [128, 128], F32)
        hid = sb(es, nc, "hid", [128, 128], BF16)
        okc = sb(es, nc, "okc", [128, 128], BF16)
        ovc = sb(es, nc, "ovc", [128, 128], BF16)
        for kv in range(2):
            S.dma("sp", w1f[:], w1d[kv, :, :], writes=["w1f"])
            S.op("pool", lambda e: e.tensor_copy(out=w1b[kv][:, :, :], in_=w1f[:, :].rearrange("p (l m) -> p l m", l=32)),
                 reads=["w1f"], writes=[f"w1b{kv}"])
            S.dma("sp", w2f[:, kv, :], w2[kv, :, :], writes=["w2f"])
            S.dma("sp", pef[:, kv, :], peT[kv, :, :], writes=["pef"])
        S.op("dve", lambda e: e.tensor_copy(out=w2b[:], in_=w2f[:]), reads=["w2f"], writes=["w2b"])
        S.op("dve", lambda e: e.tensor_copy(out=peb[:], in_=pef[:]), reads=["pef"], writes=["peb"])
        for kv in range(2):
            for l in range(32):
                S.op("pe", lambda e: e.matmul(ps[5][:, 0:1], lhsT=w1b[kv][0:64, l, :], rhs=peb[0:64, kv, l:l + 1], start=(l == 0), stop=(l == 31)),
                     reads=[f"w1b{kv}", "peb"], writes=["ps5"])
            S.op("act", lambda e: e.copy(out=biasc[:, kv:kv + 1], in_=ps[5][:, 0:1]), reads=["ps5"], writes=["biasc"])
            for hh in range(2):
                p, pk = ps[hh], f"ps{hh}"
                r0 = 64 * hh
                for l in range(32):
                    S.op("pe", lambda e: e.matmul(p[:, 0:128], lhsT=w1b[kv][r0:r0 + 64, l, :],
                                                  rhs=kcT[r0:r0 + 64, kv, 112 + l:112 + l + 16 * 127 + 1:16], start=(l == 0), stop=(l == 31)),
                         reads=[f"w1b{kv}", "kcT_raw"], writes=[pk])
                S.op("act", lambda e: e.activation(out=g1[:], in_=p[:, 0:128], func=AF.Identity, bias=biasc[:, kv:kv + 1], scale=1.0),
                     reads=[pk, "biasc"], writes=["g1"])
                S.op("dve", lambda e: e.tensor_tensor(out=g2[:], in0=g1[:], in1=g1[:], op=ALU.mult), reads=["g1"], writes=["g2"])
                S.op("dve", lambda e: e.tensor_scalar(out=g2[:], in0=g2[:], scalar1=0.044715, scalar2=1.0, op0=ALU.mult, op1=ALU.add),
                     reads=["g2"], writes=["g2"])
                S.op("dve", lambda e: e.tensor_tensor(out=g2[:], in0=g2[:], in1=g1[:], op=ALU.mult), reads=["g1", "g2"], writes=["g2"])
                S.op("act", lambda e: e.activation(out=g2[:], in_=g2[:], func=AF.Tanh, scale=math.sqrt(2.0 / math.pi)),
                     reads=["g2"], writes=["g2"])
                S.op("dve", lambda e: e.scalar_tensor_tensor(out=g2[:], in0=g2[:], scalar=1.0, in1=g1[:], op0=ALU.add, op1=ALU.mult),
                     reads=["g1", "g2"], writes=["g2"])
                S.op("act", lambda e: e.activation(out=hid[:], in_=g2[:], func=AF.Copy, scale=0.5), reads=["g2"], writes=["hid"])
                if kv == 0:
                    S.op("pe", lambda e: e.matmul(ps[2][0:64, 0:128], lhsT=w2b[:, 0, :], rhs=hid[:], start=True, stop=True),
                         reads=["w2b", "hid"], writes=["ps2"])
                    S.op("act", lambda e: e.copy(out=okc[r0:r0 + 64, :], in_=ps[2][0:64, 0:128]), reads=["ps2"], writes=["okc"])
                else:
                    S.op("pe", lambda e: e.matmul(ps[2][:, 0:64], lhsT=hid[:], rhs=w2b[:, 1, :], start=True, stop=True),
                         reads=["w2b", "hid"], writes=["ps2"])
                    S.op("act", lambda e: e.copy(out=ovc[:, r0:r0 + 64], in_=ps[2][:, 0:64]), reads=["ps2"], writes=["ovc"])
        S.dma("sp", o_kcT[:, :], okc[:], reads=["okc"], writes=["o_kcT"])
        S.dma("sp", o_vc[:, :], ovc[:], reads=["ovc"], writes=["o_vc"])


def host_A_inputs(c, x_cur, layer, P):
    lo = c * TPC
    x_ext = np.zeros((EXT, DM), np.float32)
    if c > 0:
        x_ext[:128] = x_cur[lo - 128:lo]
    x_ext[128:] = x_cur[lo:lo + TPC]
    inv = (150000.0 ** (-np.arange(32, dtype=np.float32) / 32)).astype(np.float32)
    ropec = np.zeros((128, 2), np.float32)
    ropec[:, 0] = inv[np.arange(128) % 32] / np.float32(2 * math.pi)
    R = np.zeros((128, 128), np.float32)
    for m in range(128):
        if m % 64 < 32:
            R[m + 32, m] = -1.0
        else:
            R[m - 32, m] = 1.0
    w1 = P["a_cmp_w1"][layer]
    w1d = np.ascontiguousarray(w1.reshape(2, 32, 64, 128).transpose(0, 2, 1, 3))
    w1d = np.concatenate([w1d, w1d], axis=1).reshape(2, 128, 32 * 128)
    peT = np.ascontiguousarray(P["a_cmp_pos"][layer].transpose(0, 2, 1))
    peT = np.concatenate([peT, peT], axis=1)
    return {
        "x_ext": x_ext,
        "prenorm_b": np.ascontiguousarray(np.broadcast_to(P["pre_norm"][layer][None, :], (128, DM))),
        "WA": np.ascontiguousarray(P["w_in"][layer][:, A_COLS]),
        "pos_b": np.ascontiguousarray(np.broadcast_to(P["positions"][0, lo:lo + TPC][None, :], (128, TPC))).astype(np.int32),
        "ropec": ropec, "ropeR": R,
        "w1d": np.ascontiguousarray(w1d), "w2": np.ascontiguousarray(P["a_cmp_w2"][layer]), "peT": np.ascontiguousarray(peT),
        **{k: v for k, v in host_ssd_params(layer, P).items() if k in ("b_cw", "b_cb", "b_par")},
    }


_NC_CACHE = {}


B_FM = ["a_q", "c_q", "d_q", "m_q", "b_x", "b_B", "b_C"]
B_TM = ["a_z", "b_z", "c_z", "d_z", "m_z", "a_gate", "b_dt"]
B_COLS = cols(*B_FM) + cols(*B_TM)
B_NFM = 30
B_TM0 = B_NFM * 128
FMT = {"a_q": 0, "c_q": 4, "d_q": 8, "m_q": 20, "b_x": 22, "b_B": 26, "b_C": 28}
MIXW = 2304
VW = 128


class PB:
    pass


def attn_unit(S, spe, spek, spo, spok, regions, PT, ptk, bias_ap, bias_reads=()):
    R = len(regions)
    half = (R + 1) // 2
    for r, mms in enumerate(regions):
        n = len(mms)
        sp, spk = (spe, spek) if r % 2 == 0 else (spo, spok)
        c0 = (r // 2) * 128
        for j, (lhsT, rhs, rd) in enumerate(mms):
            S.op("pe", lambda e: e.matmul(sp[:, c0:c0 + 128], lhsT=lhsT, rhs=rhs, start=(j == 0), stop=(j == n - 1)),
                 reads=rd, writes=[spk])
    S.op("act", lambda e: e.activation(out=PT[:, 0:half * 128], in_=spe[:, 0:half * 128], func=AF.Exp, bias=bias_ap, scale=1.0),
         reads=[spek] + list(bias_reads), writes=[ptk])
    if R > 1:
        S.op("act", lambda e: e.activation(out=PT[:, half * 128:R * 128], in_=spo[:, 0:(R - half) * 128], func=AF.Exp, bias=bias_ap, scale=1.0),
             reads=[spok] + list(bias_reads), writes=[ptk])


def ptcol(h, R=4):
    return (h % 2) * ((R + 1) // 2) + h // 2


def finalize_T(S, nc, C, po, pok, R, osb, osbk, ptr, ptrk, otm, otmk):
    if po is not None:
        S.op("act", lambda e: e.copy(out=osb[:65, :R * 128], in_=po[:65, :R * 128]), reads=[pok], writes=[osbk])
    for r in range(R):
        S.op("pe", lambda e: e.matmul(ptr[:, r * 128:r * 128 + 65], lhsT=osb[:65, r * 128:(r + 1) * 128], rhs=C["identf"][:65, :65], start=True, stop=True),
             reads=[osbk, "identf"], writes=[ptrk])
    S.op("dve", lambda e: e.tensor_copy(out=otm[:, :R, :], in_=ptr[:, :R * 128].rearrange("p (r d) -> p r d", r=R)[:, :, 0:65]),
         reads=[ptrk], writes=[otmk])


def build_B(debug=False):
    nc = bass.Bass("TRN2", target_bir_lowering=False)
    D = lambda name, shape, dt, kind="ExternalInput": nc.dram_tensor(name, shape, dt, kind=kind).ap()
    g = PB()
    g.nc = nc
    g.x_ext = D("x_ext", [EXT, DM], F32)
    g.prenorm_b = D("prenorm_b", [128, DM], F32)
    g.postnorm_b = D("postnorm_b", [128, DM], F32)
    g.WB = D("WB", [DM, len(B_COLS)], F32)
    g.Wout = D("Wout", [MIXW, DM], F32)
    g.mem = D("mem", [256, DM], F32)
    g.mnorm_b = D("mnorm_b", [128, DM], F32)
    g.Wkv = D("Wkv", [DM, 512], F32)
    g.D = D
    g.pos_b = D("pos_b", [128, TPC], I32)
    g.ropec = D("ropec", [128, 2], F32)
    g.ropeR = D("ropeR", [128, 128], F32)
    g.kval_d = D("kval", [128, 128], F32)
    g.C_kT = D("C_kT", [2, 128, 17 * 128], BF16)
    g.C_v = D("C_v", [2, 128, 17 * VW], BF16)
    g.C_mask = D("C_mask", [2, 128, 128], F32)
    g.sinks_b = D("sinks_b", [128, 8], F32)
    g.A_ksT = D("A_ksT", [2, 128, 128 * 128], BF16)
    g.A_vs = D("A_vs", [2, 128, 128 * VW], BF16)
    g.A_kwT = D("A_kwT", [2, 128, 20 * 128], BF16)
    g.A_vw = D("A_vw", [2, 128, 20 * VW], BF16)
    g.A_kcT = D("A_kcT", [2, 128, 1024], BF16)
    g.A_vc = D("A_vc", [2, 128, 8 * VW], BF16)
    g.A_kvalc = D("A_kvalc", [128, 8], F32)
    g.A_selB = D("A_selB", [104, 128, 128], F32)
    g.A_winB = D("A_winB", [40, 128, 128], F32)
    g.A_cmpB = D("A_cmpB", [16, 16, 128, 128], F32)
    g.A_farc = D("A_farc", [128, 8], F32)
    g.A_farrow = D("A_farrow", [1, 8, 128], F32)
    g.A_wimp = D("A_wimp", [16, 128, 128], F32)
    g.A_f0 = D("A_f0", [128, 256], F32)
    g.A_mkak = D("A_mkak", [128, 4], F32)
    g.A_ewide = D("A_ewide", [128, 8256], BF16)
    g.D_kT = D("D_kT", [4, 128, 4096], BF16)
    g.D_v = D("D_v", [4, 128, 69 * 2 * VW], BF16)
    g.D_bias = D("D_bias", [48, 128, 128], F32)
    g.b_cw = D("b_cw", [128, 8, 4], F32)
    g.b_cb = D("b_cb", [128, 8], F32)
    g.b_par = D("b_par", [128, 3, 8], F32)
    g.b_dskip = D("b_dskip", [128, 512], F32)
    g.b_norm = D("b_norm", [128, 512], F32)
    g.b_Sprev = D("b_Sprev", [128, 7 * 512], F32)
    g.b_Dprev = D("b_Dprev", [128, 56], F32)
    g.x_out = D("x_out", [TPC, DM], F32, "ExternalOutput")
    g.zs = D("zs_scr", [TPC, MIXW], F32, "ExternalOutput")
    g.mix = D("mix_scr", [TPC, MIXW], BF16, "ExternalOutput")

    with ExitStack() as es:
        S = Sync(nc)
        g.S = S
        C = emit_consts(S, nc, es)
        g.C = C
        C["epsc"] = sb(es, nc, "epsc", [128, 1], F32)
        S.op("pool", lambda e: e.memset(C["epsc"][:], EPS), writes=["epsc"])
        C["zeroc"] = sb(es, nc, "zeroc", [128, 1], F32)
        S.op("pool", lambda e: e.memset(C["zeroc"][:], 0.0), writes=["zeroc"])
        g.ps = [es.enter_context(nc.psum_tensor(f"ps{i}", [128, 512], F32)) for i in range(7)]
        g.ps_bf = es.enter_context(nc.psum_tensor("ps_bf", [128, 1024], BF16))
        g.gates = sb(es, nc, "gates", [128, NT, 24], F32)
        g.dtraw = sb(es, nc, "dtraw", [128, NTT, 8], F32)
        g.kval = sb(es, nc, "kval_sb", [128, 128], F32)
        S.dma("sp", g.kval[:], g.kval_d[:, :], writes=["kval"])
        C["onesb"] = sb(es, nc, "onesb", [1, 128], BF16)
        S.op("pool", lambda e: e.memset(C["onesb"][:], 1.0), writes=["onesb"])
        farf = sb(es, nc, "farf", [1, 8, 128], F32)
        g.farrow = sb(es, nc, "farrow", [1, 8, 128], BF16)
        S.dma("sp", farf[:], g.A_farrow[:, :, :], writes=["farf"])
        S.op("dve", lambda e: e.tensor_copy(out=g.farrow[:], in_=farf[:]), reads=["farf"], writes=["farrow"])
        import os
        ph = os.environ.get("PHASES", "ZCMDBAO")
        qa = sb(es, nc, "qa", [128, 4, TPC], BF16) if "A" in ph else None
        with Scope(S, nc) as eh:
            g.hT = sb(eh, nc, "hT", [128, 8, EXT], BF16)
            emit_hT(S, nc, C, g.x_ext, g.prenorm_b, g.hT, g.ps_bf)
            if "Z" in ph:
                phase_Z(g)
            if "C" in ph:
                phase_C(g)
            if "M" in ph:
                phase_M(g)
            if "D" in ph:
                phase_D(g)
            if "B" in ph:
                phase_B(g)
            if "A" in ph:
                with Scope(S, nc) as eq:
                    proj_q(g, eq, "qa", FMT["a_q"], 4, qT=qa)
        if "A" in ph:
            phase_A(g, qa)
        if "O" in ph:
            phase_out(g)
        S.finish("sp")
    return nc


def phase_Z(g):
    S, nc = g.S, g.nc
    with Scope(S, nc) as es:
        wl = WLoader(S, nc, es, g.WB, 512, name="wz")
        zst = [sb(es, nc, f"zst{i}", [128, 512], F32) for i in range(2)]
        pi = 0
        import os
        for cb in [int(c) for c in os.environ.get('ZCB', '01234')]:
            c0 = B_TM0 + cb * 512
            w = 512 if cb < 4 else 256 + 32
            wb, wk = wl.load(c0, w)
            for t in range(NTT):
                if cb < 4 and t == 0:
                    continue
                p, pk = g.ps[pi % 2], f"ps{pi % 2}"
                b = pi % 2
                pi += 1
                mm_tm(S, p, pk, wb, wk, g.hT, t, 0, w)
                if cb == 4:
                    S.op("dve", lambda e: e.tensor_copy(out=g.dtraw[:, t, :], in_=p[:, 280:288]), reads=[pk], writes=["dtraw"])
                    if t == 0:
                        continue
                    S.op("act", lambda e: e.activation(out=g.gates[:, t - 1, :], in_=p[:, 256:280], func=AF.Tanh, scale=0.5), reads=[pk], writes=["gates"])
                    S.op("dve", lambda e: e.tensor_scalar(out=g.gates[:, t - 1, :], in0=g.gates[:, t - 1, :], scalar1=0.5, scalar2=0.5, op0=ALU.mult, op1=ALU.add),
                         reads=["gates"], writes=["gates"])
                wz = 512 if cb < 4 else 256
                S.op("act", lambda e: e.activation(out=zst[b][:, :wz], in_=p[:, :wz], func=AF.Tanh, scale=0.5), reads=[pk], writes=[f"zst{b}"])
                S.op("dve", lambda e: e.tensor_scalar(out=zst[b][:, :wz], in0=zst[b][:, :wz], scalar1=0.5, scalar2=0.5, op0=ALU.mult, op1=ALU.add),
                     reads=[f"zst{b}"], writes=[f"zst{b}"])
                S.op("dve", lambda e: e.tensor_tensor(out=zst[b][:, :wz], in0=p[:, :wz], in1=zst[b][:, :wz], op=ALU.mult),
                     reads=[f"zst{b}", pk], writes=[f"zst{b}"])
                S.dma("sp", g.zs[(t - 1) * 128:t * 128, cb * 512:cb * 512 + wz], zst[b][:, :wz], reads=[f"zst{b}"], writes=["zs_scr"])


def proj_q(g, es, name, tile0, ntiles, scale=0.125, pbase=0, qT=None):
    S, nc = g.S, g.nc
    if qT is None:
        qT = sb(es, nc, name, [128, ntiles, TPC], BF16)
    wl = WLoader(S, nc, es, g.WB, 128, name=name + "w")
    pi = 0
    for ti in range(ntiles):
        wb, wk = wl.load((tile0 + ti) * 128)
        for (t0, w) in tok_groups(128, EXT):
            p, pk = g.ps[pbase + pi % 2], f"ps{pbase + pi % 2}"
            pi += 1
            mm_fm(S, p, pk, wb, wk, g.hT, t0, w)
            S.op("act", lambda e: e.activation(out=qT[:, ti, t0 - 128:t0 - 128 + w], in_=p[:, :w], func=AF.Copy, scale=scale),
                 reads=[pk], writes=[name])
    return qT


def phase_M(g):
    S, nc, C = g.S, g.nc, g.C
    with Scope(S, nc) as es:
        memT = sb(es, nc, "memT", [128, 8, 256], BF16)
        emit_hT_named(S, nc, C, g.mem, g.mnorm_b, memT, "memT", g.ps_bf, ntt=2)
        wl = WLoader(S, nc, es, g.Wkv, 512, name="wkv")
        wb, wk = wl.load(0, 512)
        kmT = sb(es, nc, "kmT", [128, 2, 256], BF16)
        vm = sb(es, nc, "vm", [128, 2, 4, VW], BF16)
        S.op("pool", lambda e: e.memset(vm[:], 1.0), writes=["vm"])
        for ti in range(2):
            p, pk = g.ps[0], "ps0"
            for k in range(8):
                S.op("pe", lambda e: e.matmul(p[:, :256], lhsT=wb[:, k, ti * 128:(ti + 1) * 128], rhs=memT[:, k, :], start=(k == 0), stop=(k == 7)),
                     reads=[wk, "memT"], writes=[pk])
            S.op("act", lambda e: e.copy(out=kmT[:, ti, :], in_=p[:, :256]), reads=[pk], writes=["kmT"])
        for mt in range(2):
            p, pk = g.ps[1], "ps1"
            for k in range(8):
                S.op("pe", lambda e: e.matmul(p[:, :256], lhsT=memT[:, k, mt * 128:(mt + 1) * 128], rhs=wb[:, k, 256:512], start=(k == 0), stop=(k == 7)),
                     reads=[wk, "memT"], writes=[pk])
            S.op("act", lambda e: e.copy(out=vm[:, mt, :, 0:64], in_=p[:, :256].rearrange("p (h d) -> p h d", h=4)), reads=[pk], writes=["vm"])
        import os
        MSTOP = int(os.environ.get("MSTOP", "9"))
        if MSTOP < 1:
            return
        qT = proj_q(g, es, "qm", FMT["m_q"], 2)
        if MSTOP < 2:
            return
        PT = [sb(es, nc, f"PT{i}", [128, 512], BF16) for i in range(2)]
        osb = sb(es, nc, "osb", [128, 512], F32)
        otm = sb(es, nc, "otm", [128, 4, 65], F32)
        rden = sb(es, nc, "rden", [128, 4], F32)
        zt = sb(es, nc, "zt", [128, 256], F32)
        mst = sb(es, nc, "mst", [128, 256], BF16)
        ptr, ptrk = g.ps[0], "ps0"
        ui = 0
        for i in range(int(os.environ.get("MI", NT))):
            for mt in range(2):
                pt, ptk = PT[ui % 2], f"PT{ui % 2}"
                ui += 1
                regions = []
                for h in range(4):
                    r0 = 64 * (h % 2)
                    regions.append([(kmT[r0:r0 + 64, h // 2, mt * 128:(mt + 1) * 128], qT[r0:r0 + 64, h // 2, i * 128:(i + 1) * 128], ["kmT", "qm"])])
                attn_unit(S, g.ps[1], "ps1", g.ps[2], "ps2", regions, pt, ptk, C["zeroc"][:, 0:1], ["zeroc"])
                for h in range(4):
                    pc = ptcol(h)
                    S.op("pe", lambda e: e.matmul(g.ps[3 + h][:, 0:128], lhsT=vm[:, mt, h, :], rhs=pt[:, pc * 128:(pc + 1) * 128],
                                                  start=(mt == 0), stop=(mt == 1)), reads=["vm", ptk], writes=[f"ps{3 + h}"])
            if MSTOP < 3:
                continue
            for h in range(4):
                S.op("act", lambda e: e.copy(out=osb[:65, h * 128:(h + 1) * 128], in_=g.ps[3 + h][:65, 0:128]), reads=[f"ps{3 + h}"], writes=["osb"])
            finalize_T(S, nc, C, None, None, 4, osb, "osb", ptr, ptrk, otm, "otm")
            if MSTOP < 4:
                continue
            S.op("dve", lambda e: e.reciprocal(out=rden[:], in_=otm[:, :, 64]), reads=["otm"], writes=["rden"])
            S.dma("sp", zt[:], g.zs[i * 128:(i + 1) * 128, 2048:2304], reads=["zs_scr"], writes=["zt"])
            for h in range(4):
                S.op("dve", lambda e: e.scalar_tensor_tensor(out=mst[:, h * 64:(h + 1) * 64], in0=otm[:, h, 0:64], scalar=rden[:, h:h + 1],
                                                             in1=zt[:, h * 64:(h + 1) * 64], op0=ALU.mult, op1=ALU.mult),
                     reads=["otm", "rden", "zt"], writes=["mst"])
            S.dma("sp", g.mix[i * 128:(i + 1) * 128, 2048:2304], mst[:], reads=["mst"], writes=["mix_scr"])


def emit_hT_named(S, nc, C, x_ext, prenorm_b, hT, hkey, ps_bf, ntt):
    with Scope(S, nc) as es:
        pn = sb(es, nc, "pn2", [128, DM], F32)
        S.dma("sp", pn[:], prenorm_b[:, :], writes=["pn2"])
        xt = sb(es, nc, "xt2", [128, DM], F32)
        hb = sb(es, nc, "hb2", [128, DM], BF16)
        junk = sb(es, nc, "junk2", [128, DM], F32)
        ss = sb(es, nc, "ss2", [128, 1], F32)
        for t in range(ntt):
            S.dma("sp", xt[:], x_ext[t * 128:(t + 1) * 128, :], writes=["xt2"])
            S.op("pool", lambda e: e.memset(ss[:], 0.0), writes=["ss2"])
            S.op("act", lambda e: e.activation(out=junk[:], in_=xt[:], func=AF.Square, accum_out=ss[:]), reads=["xt2", "ss2"], writes=["junk2", "ss2"])
            S.op("act", lambda e: e.activation(out=ss[:], in_=ss[:], func=AF.Sqrt, bias=C["epsc"][:, 0:1], scale=1.0 / DM), reads=["ss2"], writes=["ss2"])
            S.op("dve", lambda e: e.reciprocal(out=ss[:], in_=ss[:]), reads=["ss2"], writes=["ss2"])
            S.op("dve", lambda e: e.scalar_tensor_tensor(out=hb[:], in0=xt[:], scalar=ss[:, 0:1], in1=pn[:], op0=ALU.mult, op1=ALU.mult),
                 reads=["xt2", "ss2", "pn2"], writes=["hb2"])
            for k in range(8):
                S.op("pe", lambda e: e.transpose(out=ps_bf[:, k * 128:(k + 1) * 128], in_=hb[:, k * 128:(k + 1) * 128], identity=C["identb"][:]),
                     reads=["hb2", "identb"], writes=["ps_bf"])
            S.op("act", lambda e: e.copy(out=hT[:, :, t * 128:(t + 1) * 128], in_=ps_bf[:, :].rearrange("p (k t) -> p k t", k=8)),
                 reads=["ps_bf"], writes=[hkey])


def phase_out(g):
    S, nc, C = g.S, g.nc, g.C
    with Scope(S, nc) as es:
        wo = sb(es, nc, "wo", [128, 18, DM], BF16)
        wof = [sb(es, nc, f"wof{i}", [128, DM], F32) for i in range(2)]
        for k in range(18):
            b = k % 2
            S.dma("sp", wof[b][:], g.Wout[k * 128:(k + 1) * 128, :], writes=[f"wof{b}"])
            S.op("pool", lambda e: e.tensor_copy(out=wo[:, k, :], in_=wof[b][:]), reads=[f"wof{b}"], writes=["wo"])
        pnb = sb(es, nc, "pnb", [128, DM], F32)
        S.dma("sp", pnb[:], g.postnorm_b[:, :], writes=["pnb"])
        mt_ = [sb(es, nc, f"mixt{i}", [128, MIXW], BF16) for i in range(2)]
        mT = [sb(es, nc, f"mixT{i}", [128, 18, 128], BF16) for i in range(2)]
        xt = [sb(es, nc, f"xo{i}", [128, DM], F32) for i in range(2)]
        y = [sb(es, nc, f"yo{i}", [128, DM], F32) for i in range(2)]
        junk = sb(es, nc, "junko", [128, DM], F32)
        ss = [sb(es, nc, f"sso{i}", [128, 1], F32) for i in range(2)]
        for t in range(NT):
            b = t % 2
            S.dma("sp", mt_[b][:], g.mix[t * 128:(t + 1) * 128, :], reads=["mix_scr"], writes=[f"mixt{b}"])
            S.dma("sp", xt[b][:], g.x_ext[(t + 1) * 128:(t + 2) * 128, :], writes=[f"xo{b}"])
            for half in range(3):
                k0 = half * 8
                nk = min(8, 18 - k0)
                for k in range(nk):
                    S.op("pe", lambda e: e.transpose(out=g.ps_bf[:, k * 128:(k + 1) * 128], in_=mt_[b][:, (k0 + k) * 128:(k0 + k + 1) * 128],
                                                     identity=C["identb"][:]), reads=[f"mixt{b}", "identb"], writes=["ps_bf"])
                S.op("act", lambda e: e.copy(out=mT[b][:, k0:k0 + nk, :], in_=g.ps_bf[:, :nk * 128].rearrange("p (k t) -> p k t", k=nk)),
                     reads=["ps_bf"], writes=[f"mixT{b}"])
            for nh in range(2):
                p, pk = g.ps[nh], f"ps{nh}"
                for k in range(18):
                    S.op("pe", lambda e: e.matmul(p[:, :512], lhsT=mT[b][:, k, :], rhs=wo[:, k, nh * 512:(nh + 1) * 512], start=(k == 0), stop=(k == 17)),
                         reads=[f"mixT{b}", "wo"], writes=[pk])
                S.op("act", lambda e: e.copy(out=y[b][:, nh * 512:(nh + 1) * 512], in_=p[:, :512]), reads=[pk], writes=[f"yo{b}"])
            S.op("pool", lambda e: e.memset(ss[b][:], 0.0), writes=[f"sso{b}"])
            S.op("act", lambda e: e.activation(out=junk[:], in_=y[b][:], func=AF.Square, accum_out=ss[b][:]), reads=[f"yo{b}", f"sso{b}"], writes=["junko", f"sso{b}"])
            S.op("act", lambda e: e.activation(out=ss[b][:], in_=ss[b][:], func=AF.Sqrt, bias=C["epsc"][:, 0:1], scale=1.0 / DM), reads=[f"sso{b}"], writes=[f"sso{b}"])
            S.op("dve", lambda e: e.reciprocal(out=ss[b][:], in_=ss[b][:]), reads=[f"sso{b}"], writes=[f"sso{b}"])
            S.op("dve", lambda e: e.scalar_tensor_tensor(out=y[b][:], in0=y[b][:], scalar=ss[b][:, 0:1], in1=pnb[:], op0=ALU.mult, op1=ALU.mult),
                 reads=[f"yo{b}", f"sso{b}", "pnb"], writes=[f"yo{b}"])
            S.op("pool", lambda e: e.tensor_tensor(out=y[b][:], in0=y[b][:], in1=xt[b][:], op=ALU.add), reads=[f"yo{b}", f"xo{b}"], writes=[f"yo{b}"])
            S.dma("sp", g.x_out[t * 128:(t + 1) * 128, :], y[b][:], reads=[f"yo{b}"], writes=[f"x_out{t}"])


def host_B_inputs(c, x_cur, layer, P, Aout):
    lo = c * TPC
    x_ext = np.zeros((EXT, DM), np.float32)
    if c > 0:
        x_ext[:128] = x_cur[lo - 128:lo]
    x_ext[128:] = x_cur[lo:lo + TPC]
    bc = lambda v: np.ascontiguousarray(np.broadcast_to(v[None, :], (128, v.shape[0])))
    return {
        "x_ext": x_ext,
        "prenorm_b": bc(P["pre_norm"][layer]), "postnorm_b": bc(P["post_norm"][layer]),
        "WB": np.ascontiguousarray(P["w_in"][layer][:, B_COLS]),
        "Wout": np.ascontiguousarray(P["w_out"][layer]),
        "mem": np.ascontiguousarray(P["mem"][0]), "mnorm_b": bc(P["m_norm"][layer]),
        "Wkv": np.ascontiguousarray(P["m_w_kv"][layer]),
        **host_B_attn_inputs(c, layer, P, Aout),
        **host_B_A_inputs(c, layer, P, Aout),
        **host_B_D_inputs(c, layer, P, Aout),
        **host_ssd_params(layer, P),
        **host_B_ssd_chain(c, Aout),
    }


def t5_bucket_np(dist):
    dist = np.maximum(dist, 0)
    rel = np.maximum(dist, 16).astype(np.float32)
    large = 16 + (np.log(rel / np.float32(16)) / np.float32(math.log(2048 / 16)) * np.float32(16)).astype(np.int32)
    return np.where(dist < 16, dist, np.minimum(large, 31)).astype(np.int64)


def toeplitz_tile(table_col, dist, allowed):
    out = np.full(dist.shape, NEG, np.float32)
    if table_col is None:
        out[allowed] = 0.0
    else:
        out[allowed] = table_col[t5_bucket_np(dist)][allowed]
    return out


_KI = np.arange(128)[:, None]
_QI = np.arange(128)[None, :]


def gqa_run(g, units, PT, po, pok, first_bufs=None):
    S = g.S
    n = len(units)
    for u, un in enumerate(units):
        b = g.ui % 2
        g.ui += 1
        spe, spek, spo, spok = g.ps[2 * b], f"ps{2 * b}", g.ps[2 * b + 1], f"ps{2 * b + 1}"
        pt, ptk = PT[b], f"PT{b}"
        attn_unit(S, spe, spek, spo, spok, un["regions"], pt, ptk, un["bias"][0], un["bias"][1])
        if "post" in un:
            un["post"](pt, ptk)
        vap, vreads = un["v"]
        S.op("pe", lambda e: e.matmul(po[:, 0:512], lhsT=vap, rhs=pt[:, 0:512], start=(u == 0), stop=(u == n - 1)),
             reads=list(vreads) + [ptk], writes=[pok])


def load_bias_tiles(g, es, name, dram, n):
    S, nc = g.S, g.nc
    t = sb(es, nc, name, [128, n, 128], BF16)
    with Scope(S, nc) as es2:
        st = [sb(es2, nc, f"{name}_st{i}", [128, 8, 128], F32) for i in range(2)]
        for j0 in range(0, n, 8):
            b = (j0 // 8) % 2
            m = min(8, n - j0)
            S.dma("sp", st[b][:, :m, :], dram[j0:j0 + m, :, :].rearrange("n p q -> p n q"), writes=[f"{name}_st{b}"])
            S.op("pool", lambda e: e.tensor_copy(out=t[:, j0:j0 + m, :], in_=st[b][:, :m, :]), reads=[f"{name}_st{b}"], writes=[name])
    return t


def load_bf(g, es, name, dram, shape):
    t = sb(es, g.nc, name, shape, BF16)
    g.S.dma("sp", t[:], dram, writes=[name])
    return t


def head_regions(kT, kkey, kcols, qT, qkey, gi, i, extra=None):
    regions = []
    for h in range(4):
        r0 = 64 * (h % 2)
        mm = [(kT[r0:r0 + 64, kcols], qT[r0:r0 + 64, 2 * gi + h // 2, i * 128:(i + 1) * 128], [kkey, qkey])]
        if extra is not None:
            mm += extra(h)
        regions.append(mm)
    return regions


def phase_C(g):
    S, nc, C = g.S, g.nc, g.C
    D = g.D
    with Scope(S, nc) as es:
        cosT = sb(es, nc, "cosT", [128, TPC], F32)
        sinT = sb(es, nc, "sinT", [128, TPC], F32)
        rc = sb(es, nc, "rc", [128, 2], F32)
        rR = sb(es, nc, "rR", [128, 128], F32)
        S.dma("sp", rc[:], g.ropec[:, :], writes=["rc"])
        S.dma("sp", rR[:], g.ropeR[:, :], writes=["rR"])
        emit_rope_tables(S, nc, g.pos_b, rc, cosT, sinT)
        qT = sb(es, nc, "qc", [128, 4, TPC], BF16)
        wl = WLoader(S, nc, es, g.WB, 128, name="qcw")
        xf = sb(es, nc, "xf", [128, 512], F32)
        pi = 0
        for ti in range(4):
            wb, wk = wl.load((FMT["c_q"] + ti) * 128)
            for (t0, w) in tok_groups(128, EXT):
                p, pk = g.ps[pi % 2], f"ps{pi % 2}"
                pi += 1
                mm_fm(S, p, pk, wb, wk, g.hT, t0, w)
                l0 = t0 - 128
                emit_rope_apply(S, nc, p, pk, w, xf, rR, cosT, sinT, l0, g.ps[2], "ps2", qT[:, ti, l0:l0 + w], "qc", scale=1.0)
        S.op("pool", lambda e: e.tensor_scalar(out=qT[:], in0=qT[:], scalar1=0.125, scalar2=None, op0=ALU.mult), reads=["qc"], writes=["qc"])
        kT = [load_bf(g, es, f"kcT{gi}", g.C_kT[gi, :, :], [128, 17 * 128]) for gi in range(2)]
        vv = [load_bf(g, es, f"vcc{gi}", g.C_v[gi, :, :], [128, 17 * VW]) for gi in range(2)]
        cm = load_bias_tiles(g, es, "cmask", g.C_mask, 2)
        sinkf = sb(es, nc, "sinkf", [128, 8], F32)
        S.dma("sp", sinkf[:], g.sinks_b[:, :], writes=["sinkf"])
        S.op("act", lambda e: e.activation(out=sinkf[:], in_=sinkf[:], func=AF.Exp), reads=["sinkf"], writes=["sinkf"])
        PT = [sb(es, nc, f"PT{i}", [128, 512], BF16) for i in range(2)]
        osb = sb(es, nc, "osb", [128, 512], F32)
        otm = sb(es, nc, "otm", [128, 4, 65], F32)
        rden = sb(es, nc, "rden", [128, 4], F32)
        zt = sb(es, nc, "zt", [128, 512], F32)
        mst = sb(es, nc, "mst", [128, 512], BF16)
        po, pok, ptr, ptrk = g.ps[4], "ps4", g.ps[5], "ps5"
        g.ui = 0
        for i in range(NT):
            S.dma("sp", zt[:], g.zs[i * 128:(i + 1) * 128, 1024:1536], reads=["zs_scr"], writes=["zt"])
            for gi in range(2):
                units = []
                for w in (1, 0):
                    T = 1 + i - w
                    ext = lambda h, w=w: [(C["identb"][:, :], cm[:, w, :], ["identb", "cmask"])]
                    units.append(dict(regions=head_regions(kT[gi], f"kcT{gi}", slice(T * 128, (T + 1) * 128), qT, "qc", gi, i, ext),
                                      bias=(g.kval[:, 111 + T:112 + T], ["kval"]),
                                      v=(vv[gi][:, T * VW:(T + 1) * VW], [f"vcc{gi}"])))
                gqa_run(g, units, PT, po, pok)
                finalize_T(S, nc, C, po, pok, 4, osb, "osb", ptr, ptrk, otm, "otm")
                for h in range(4):
                    pc = ptcol(h)
                    hh = 4 * gi + h
                    S.op("dve", lambda e: e.tensor_tensor(out=rden[:, h:h + 1], in0=otm[:, pc, 64:65], in1=sinkf[:, hh:hh + 1], op=ALU.add),
                         reads=["otm", "sinkf"], writes=["rden"])
                S.op("dve", lambda e: e.reciprocal(out=rden[:], in_=rden[:]), reads=["rden"], writes=["rden"])
                for h in range(4):
                    pc = ptcol(h)
                    hh = 4 * gi + h
                    S.op("dve", lambda e: e.scalar_tensor_tensor(out=mst[:, hh * 64:(hh + 1) * 64], in0=otm[:, pc, 0:64], scalar=rden[:, h:h + 1],
                                                                 in1=zt[:, hh * 64:(hh + 1) * 64], op0=ALU.mult, op1=ALU.mult),
                         reads=["otm", "rden", "zt"], writes=["mst"])
            S.dma("sp", g.mix[i * 128:(i + 1) * 128, 1024:1536], mst[:], reads=["mst"], writes=["mix_scr"])


def gather_A(res):
    G = {}
    G["fm"] = np.concatenate([np.asarray(r["o_fm"]) for r in res], axis=2)
    G["tm"] = np.concatenate([np.asarray(r["o_tm"]) for r in res], axis=0)
    G["kcT"] = np.concatenate([np.asarray(r["o_kcT"]) for r in res], axis=1)
    G["vc"] = np.concatenate([np.asarray(r["o_vc"]) for r in res], axis=0)
    G["S"] = [np.asarray(r["o_S"]) for r in res]
    G["D"] = [np.asarray(r["o_D"]) for r in res]
    return G


def rope_consts():
    inv = (150000.0 ** (-np.arange(32, dtype=np.float32) / 32)).astype(np.float32)
    ropec = np.zeros((128, 2), np.float32)
    ropec[:, 0] = inv[np.arange(128) % 32] / np.float32(2 * math.pi)
    R = np.zeros((128, 128), np.float32)
    for m in range(128):
        if m % 64 < 32:
            R[m + 32, m] = -1.0
        else:
            R[m - 32, m] = 1.0
    return ropec, R


def kT_window(G, fm_idx, gi, c, first_slot, dup=True):
    n = 128 - first_slot
    out = np.zeros((128, n * 128), NPBF)
    for s in range(n):
        Tg = first_slot + s - 16 * (7 - c)
        if Tg < 0:
            continue
        blk = G["fm"][fm_idx][:, Tg * 128:(Tg + 1) * 128]
        if dup:
            out[0:64, s * 128:(s + 1) * 128] = blk[64 * gi:64 * gi + 64]
            out[64:128, s * 128:(s + 1) * 128] = blk[64 * gi:64 * gi + 64]
        else:
            out[:, s * 128:(s + 1) * 128] = blk
    return out


def v_window(G, col0, c, first_slot):
    n = 128 - first_slot
    out = np.zeros((128, n, VW), NPBF)
    out[:, :, 64] = 1.0
    for s in range(n):
        Tg = first_slot + s - 16 * (7 - c)
        if Tg < 0:
            continue
        out[:, s, 0:64] = G["tm"][Tg * 128:(Tg + 1) * 128, col0:col0 + 64]
    return out


def host_B_attn_inputs(c, layer, P, G):
    lo = c * TPC
    ropec, R = rope_consts()
    d = {}
    d["pos_b"] = np.ascontiguousarray(np.broadcast_to(P["positions"][0, lo:lo + TPC][None, :], (128, TPC))).astype(np.int32)
    d["ropec"], d["ropeR"] = ropec, R
    kval = np.zeros((128, 128), np.float32)
    kval[:, :16 * (7 - c)] = NEG
    d["kval"] = kval
    d["C_kT"] = np.stack([kT_window(G, 2, gi, c, 111) for gi in range(2)])
    d["C_v"] = np.stack([v_window(G, 256 + 64 * gi, c, 111).reshape(128, -1) for gi in range(2)])
    cm = np.stack([toeplitz_tile(None, _QI - _KI, (_QI - _KI) >= 0), toeplitz_tile(None, 128 + _QI - _KI, (128 + _QI - _KI) <= 127)])
    d["C_mask"] = cm
    d["sinks_b"] = np.ascontiguousarray(np.broadcast_to(P["c_sinks"][layer][None, :], (128, 8)))
    return d


def phase_A(g, qa):
    S, nc, C = g.S, g.nc, g.C
    with Scope(S, nc) as es:
        ew = load_bf(g, es, "ewide", g.A_ewide[:, :], [128, 8256])
        wimp = load_bias_tiles(g, es, "wimp", g.A_wimp, 16)
        farc = sb(es, nc, "farc", [128, 8], F32)
        S.dma("sp", farc[:], g.A_farc[:, :], writes=["farc"])
        f0 = sb(es, nc, "f0", [128, 256], F32)
        S.dma("sp", f0[:], g.A_f0[:, :], writes=["f0"])
        mkak = sb(es, nc, "mkak", [128, 4], F32)
        S.dma("sp", mkak[:], g.A_mkak[:, :], writes=["mkak"])
        kvalc = sb(es, nc, "kvalc", [128, 8], F32)
        S.dma("sp", kvalc[:], g.A_kvalc[:, :], writes=["kvalc"])
        PT = [sb(es, nc, f"PT{i}", [128, 512], BF16) for i in range(2)]
        PTc = sb(es, nc, "PTc", [128, 8, 512], BF16)
        osb = sb(es, nc, "osb", [128, 512], F32)
        otm = [sb(es, nc, f"otm{b}", [128, 4, 65], F32) for b in range(3)]
        rden = sb(es, nc, "rden", [128, 3, 4], F32)
        acc = sb(es, nc, "acc_a", [128, 64], F32)
        imp = sb(es, nc, "imp", [128, 256], F32)
        imp2 = sb(es, nc, "imp2", [128, 256], F32)
        mx = sb(es, nc, "mx", [128, 16], F32)
        negm = sb(es, nc, "negm", [128, 256], F32)
        nmn = sb(es, nc, "nmn", [128, 2, 128], BF16)
        nmf = sb(es, nc, "nmf", [128, 2, 4, 128], BF16)
        cst = sb(es, nc, "cst", [128, 8, 128], F32)
        cbt = sb(es, nc, "cbt", [128, 8, 128], BF16)
        zt = sb(es, nc, "zt", [128, 512], F32)
        mst = sb(es, nc, "mst", [128, 512], BF16)
        po, pok, ptr, ptrk, pu, puk = g.ps[4], "ps4", g.ps[5], "ps5", g.ps[6], "ps6"
        g.ui = 0
        for gi in range(2):
            with Scope(S, nc) as eg:
                ksT = load_bf(g, eg, f"ksT{gi}", g.A_ksT[gi, :, :], [128, 128 * 128])
                vs = load_bf(g, eg, f"vs{gi}", g.A_vs[gi, :, :], [128, 128 * VW])
                kwT = load_bf(g, eg, f"kwT{gi}", g.A_kwT[gi, :, :], [128, 20 * 128])
                vw = load_bf(g, eg, f"vw{gi}", g.A_vw[gi, :, :], [128, 20 * VW])
                kcT = load_bf(g, eg, f"kcT{gi}", g.A_kcT[gi, :, :], [128, 1024])
                vc = load_bf(g, eg, f"vc{gi}", g.A_vc[gi, :, :], [128, 8 * VW])
                selB = load_bias_tiles(g, eg, f"selB{gi}", g.A_selB[gi * 52:(gi + 1) * 52, :, :], 52)
                winB = load_bias_tiles(g, eg, f"winB{gi}", g.A_winB[gi * 20:(gi + 1) * 20, :, :], 20)
                for i in range(int(os.environ.get("AI", NT))):
                    S.dma("sp", cst[:], g.A_cmpB[i, gi * 8:(gi + 1) * 8, :, :].rearrange("n p q -> p n q"), writes=["cst"])
                    S.op("pool", lambda e: e.tensor_copy(out=cbt[:], in_=cst[:]), reads=["cst"], writes=["cbt"])
                    for m in range(8):
                        b = g.ui % 2
                        g.ui += 1
                        if m >= 6:
                            ext = lambda h, m=m: [(C["identb"][:, :], cbt[:, h * 2 + (m - 6), :], ["identb", "cbt"])]
                        else:
                            ext = lambda h: [(C["onesb"][0:1, :], g.farrow[0:1, gi * 4 + h, :], ["onesb", "farrow"])]
                        regions = head_regions(kcT, f"kcT{gi}", slice(m * 128, (m + 1) * 128), qa, "qa", gi, i, ext)
                        attn_unit(S, g.ps[2 * b], f"ps{2 * b}", g.ps[2 * b + 1], f"ps{2 * b + 1}", regions, PTc[:, m, :], "PTc",
                                  kvalc[:, m:m + 1], ["kvalc"])
                        S.op("pe", lambda e: e.matmul(po[:, 0:512], lhsT=vc[:, m * VW:(m + 1) * VW], rhs=PTc[:, m, :], start=(m == 0), stop=(m == 7)),
                             reads=[f"vc{gi}", "PTc"], writes=[pok])
                    finalize_T(S, nc, C, po, pok, 4, osb, "osb", ptr, ptrk, otm[0], "otm0")
                    S.op("dve", lambda e: e.tensor_scalar(out=rden[:, 0, :], in0=otm[0][:, :, 64], scalar1=1e-30, scalar2=None, op0=ALU.max), reads=["otm0"], writes=["rden"])
                    S.op("dve", lambda e: e.reciprocal(out=rden[:, 0, :], in_=rden[:, 0, :]), reads=["rden"], writes=["rden"])
                    for pair in range(2):
                        for hp in range(2):
                            pc = pair * 2 + hp
                            for m in range(8):
                                S.op("pe", lambda e: e.matmul(pu[:, hp * 256:(hp + 1) * 256], lhsT=PTc[:, m, pc * 128:(pc + 1) * 128],
                                                              rhs=wimp[:, 2 * m:2 * m + 2, :].rearrange("p a b -> p (a b)"), start=(m == 0), stop=(m == 7)),
                                     reads=["PTc", "wimp"], writes=[puk])
                        for hp in range(2):
                            pc = pair * 2 + hp
                            if pc == 0:
                                S.op("dve", lambda e: e.scalar_tensor_tensor(out=imp[:], in0=pu[:, 0:256], scalar=rden[:, 0, 0:1], in1=f0[:],
                                                                             op0=ALU.mult, op1=ALU.add), reads=[puk, "rden", "f0"], writes=["imp"])
                            else:
                                S.op("dve", lambda e: e.scalar_tensor_tensor(out=imp[:], in0=pu[:, hp * 256:(hp + 1) * 256], scalar=rden[:, 0, pc:pc + 1],
                                                                             in1=imp[:], op0=ALU.mult, op1=ALU.add), reads=[puk, "rden", "imp"], writes=["imp"])
                    c0 = 224 + 2 * i
                    S.op("dve", lambda e: e.tensor_tensor(out=imp[:, c0:c0 + 2], in0=imp[:, c0:c0 + 2], in1=mkak[:, 0:2], op=ALU.mult),
                         reads=["imp", "mkak"], writes=["imp"])
                    S.op("dve", lambda e: e.tensor_tensor(out=imp[:, c0:c0 + 2], in0=imp[:, c0:c0 + 2], in1=mkak[:, 2:4], op=ALU.add),
                         reads=["imp", "mkak"], writes=["imp"])
                    if c0 + 2 < 256:
                        S.op("dve", lambda e: e.memset(imp[:, c0 + 2:256], -1.0), reads=["imp"], writes=["imp"])
                    S.op("dve", lambda e: e.max(out=mx[:, 0:8], in_=imp[:]), reads=["imp"], writes=["mx"])
                    S.op("dve", lambda e: e.match_replace(out=imp2[:], in_to_replace=mx[:, 0:8], in_values=imp[:], imm_value=-1e9),
                         reads=["imp", "mx"], writes=["imp2"])
                    S.op("dve", lambda e: e.max(out=mx[:, 8:16], in_=imp2[:]), reads=["imp2"], writes=["mx"])
                    S.op("dve", lambda e: e.tensor_scalar(out=negm[:], in0=imp[:], scalar1=mx[:, 15:16], scalar2=1.0, op0=ALU.is_ge, op1=ALU.subtract),
                         reads=["imp", "mx"], writes=["negm"])
                    for half in range(2):
                        S.op("pe", lambda e: e.matmul(pu[:, half * 128:(half + 1) * 128], lhsT=negm[:, half * 128:(half + 1) * 128], rhs=C["identf"][:, :],
                                                      start=True, stop=True), reads=["negm", "identf"], writes=[puk])
                    S.op("act", lambda e: e.activation(out=nmn[:].rearrange("p a b -> p (a b)"), in_=pu[:, 0:256], func=AF.Copy, scale=-NEG),
                         reads=[puk], writes=["nmn"])
                    for h in range(4):
                        pc = ptcol(h)
                        S.op("act", lambda e: e.activation(out=nmf[:, :, h, :], in_=pu[:, 0:256].rearrange("p (a b) -> p a b", a=2), func=AF.Identity,
                                                           bias=farc[:, gi * 4 + h:gi * 4 + h + 1], scale=-NEG), reads=[puk, "farc"], writes=["nmf"])
                    units = []
                    nkt = 113 + i
                    for kt in range(int(os.environ.get("KT0", 0)), nkt):
                        delta = 112 + i - kt
                        half, r = (2 * kt) // 128, (2 * kt) % 128
                        if delta <= 12:
                            ext = lambda h, delta=delta, half=half, r=r: [(ew[:, 64 * r:64 * r + 128], nmn[:, half, :], ["ewide", "nmn"]),
                                                                           (C["identb"][:, :], selB[:, h * 13 + delta, :], ["identb", f"selB{gi}"])]
                        else:
                            ext = lambda h, half=half, r=r: [(ew[:, 64 * r:64 * r + 128], nmf[:, half, h, :], ["ewide", "nmf"])]
                        units.append(dict(regions=head_regions(ksT, f"ksT{gi}", slice(kt * 128, (kt + 1) * 128), qa, "qa", gi, i, ext),
                                          bias=(g.kval[:, kt:kt + 1], ["kval"]), v=(vs[:, kt * VW:(kt + 1) * VW], [f"vs{gi}"])))
                    gqa_run(g, units, PT, po, pok)
                    finalize_T(S, nc, C, po, pok, 4, osb, "osb", ptr, ptrk, otm[1], "otm1")
                    S.op("dve", lambda e: e.tensor_scalar(out=rden[:, 1, :], in0=otm[1][:, :, 64], scalar1=1e-30, scalar2=None, op0=ALU.max), reads=["otm1"], writes=["rden"])
                    S.op("dve", lambda e: e.reciprocal(out=rden[:, 1, :], in_=rden[:, 1, :]), reads=["rden"], writes=["rden"])
                    units = []
                    for w in (4, 3, 2, 1, 0):
                        T = 4 + i - w
                        ext = lambda h, w=w: [(C["identb"][:, :], winB[:, h * 5 + w, :], ["identb", f"winB{gi}"])]
                        units.append(dict(regions=head_regions(kwT, f"kwT{gi}", slice(T * 128, (T + 1) * 128), qa, "qa", gi, i, ext),
                                          bias=(g.kval[:, 108 + T:109 + T], ["kval"]), v=(vw[:, T * VW:(T + 1) * VW], [f"vw{gi}"])))
                    gqa_run(g, units, PT, po, pok)
                    finalize_T(S, nc, C, po, pok, 4, osb, "osb", ptr, ptrk, otm[2], "otm2")
                    S.op("dve", lambda e: e.tensor_scalar(out=rden[:, 2, :], in0=otm[2][:, :, 64], scalar1=1e-30, scalar2=None, op0=ALU.max), reads=["otm2"], writes=["rden"])
                    S.op("dve", lambda e: e.reciprocal(out=rden[:, 2, :], in_=rden[:, 2, :]), reads=["rden"], writes=["rden"])
                    S.dma("sp", zt[:, 0:256], g.zs[i * 128:(i + 1) * 128, gi * 256:(gi + 1) * 256], reads=["zs_scr"], writes=["zt"])
                    for h in range(4):
                        pc = ptcol(h)
                        hh = 4 * gi + h
                        for br in range(3):
                            S.op("dve", lambda e: e.tensor_tensor(out=rden[:, br, pc:pc + 1], in0=rden[:, br, pc:pc + 1],
                                                                  in1=g.gates[:, i, hh * 3 + br:hh * 3 + br + 1], op=ALU.mult),
                                 reads=["rden", "gates"], writes=["rden"])
                        S.op("dve", lambda e: e.tensor_scalar(out=acc[:], in0=otm[0][:, pc, 0:64], scalar1=rden[:, 0, pc:pc + 1], scalar2=None, op0=ALU.mult),
                             reads=["otm0", "rden"], writes=["acc_a"])
                        for br in (1, 2):
                            S.op("dve", lambda e: e.scalar_tensor_tensor(out=acc[:], in0=otm[br][:, pc, 0:64], scalar=rden[:, br, pc:pc + 1], in1=acc[:],
                                                                         op0=ALU.mult, op1=ALU.add), reads=[f"otm{br}", "rden", "acc_a"], writes=["acc_a"])
                        S.op("dve", lambda e: e.tensor_tensor(out=mst[:, h * 64:(h + 1) * 64], in0=acc[:], in1=zt[:, h * 64:(h + 1) * 64], op=ALU.mult),
                             reads=["acc_a", "zt"], writes=["mst"])
                    S.dma("sp", g.mix[i * 128:(i + 1) * 128, gi * 256:(gi + 1) * 256], mst[:, 0:256], reads=["mst"], writes=["mix_scr"])


_HOST_CACHE = {}


def _host_A_static(rel_a):
    d = {}
    selB = np.zeros((8, 13, 128, 128), np.float32)
    winB = np.zeros((8, 5, 128, 128), np.float32)
    for h in range(8):
        for dl in range(13):
            dist = 128 * dl + _QI - _KI
            selB[h, dl] = toeplitz_tile(rel_a[:, h], dist, dist >= 0)
        for w in range(5):
            dist = 128 * w + _QI - _KI
            winB[h, w] = toeplitz_tile(rel_a[:, h], dist, (dist >= 0) & (dist < 512))
    d["A_selB"] = selB.reshape(104, 128, 128)
    d["A_winB"] = winB.reshape(40, 128, 128)
    cmpB = np.zeros((16, 8, 2, 128, 128), np.float32)
    for i in range(16):
        for mi, m in enumerate((6, 7)):
            dist = 14336 + 128 * i + _QI - 2048 * m - 16 * _KI - 15
            for h in range(8):
                cmpB[i, h, mi] = toeplitz_tile(rel_a[:, h], dist, dist >= 0)
    d["A_cmpB"] = cmpB.reshape(16, 16, 128, 128)
    d["A_farc"] = np.ascontiguousarray(np.broadcast_to(rel_a[31][None, :], (128, 8)))
    d["A_farrow"] = np.ascontiguousarray(np.broadcast_to(rel_a[31][None, :, None], (1, 8, 128))).astype(np.float32)
    wimp = np.zeros((1024, 256), np.float32)
    wt = {-1: 1.0, 0: 2.0, 1: 2.0, 2: 2.0, 3: 1.0}
    for p_ in range(1024):
        j = p_ - 1
        for b in range(256):
            k = j - 4 * b
            if k in wt:
                wimp[p_, b] = wt[k]
    d["A_wimp"] = np.ascontiguousarray(wimp.reshape(8, 128, 2, 128).transpose(0, 2, 1, 3).reshape(16, 128, 128))
    return d


def host_B_A_inputs(c, layer, P, G):
    rel_a = P["rel_bias"][:, :8]
    d = {}
    if "A_static" not in _HOST_CACHE:
        _HOST_CACHE["A_static"] = _host_A_static(rel_a)
    d.update(_HOST_CACHE["A_static"])
    d["A_ksT"] = np.stack([kT_window(G, 0, gi, c, 0) for gi in range(2)])
    d["A_vs"] = np.stack([v_window(G, 0 + 64 * gi, c, 0).reshape(128, -1) for gi in range(2)])
    d["A_kwT"] = np.stack([kT_window(G, 1, gi, c, 108) for gi in range(2)])
    d["A_vw"] = np.stack([v_window(G, 128 + 64 * gi, c, 108).reshape(128, -1) for gi in range(2)])
    sh = 128 * (7 - c)
    kc = np.zeros((2, 128, 1024), NPBF)
    vc = np.zeros((2, 128, 8, VW), NPBF)
    vc[:, :, :, 64] = 1.0
    n = 1024 - sh
    for gi in range(2):
        kc[gi, 0:64, sh:] = G["kcT"][64 * gi:64 * gi + 64, :n]
        kc[gi, 64:128, sh:] = G["kcT"][64 * gi:64 * gi + 64, :n]
        vfull = np.zeros((1024, 64), NPBF)
        vfull[sh:] = G["vc"][:n, 64 * gi:64 * gi + 64]
        vc[gi, :, :, 0:64] = vfull.reshape(8, 128, 64).transpose(1, 0, 2)
    d["A_kcT"] = kc
    d["A_vc"] = vc.reshape(2, 128, -1)
    kvalc = np.zeros((128, 8), np.float32)
    pos = (np.arange(8)[None, :] * 128 + np.arange(128)[:, None])
    kvalc[pos <= sh] = NEG
    d["A_kvalc"] = kvalc
    f0 = np.zeros((128, 256), np.float32)
    f0[:, 32 * (7 - c)] = 1e4
    d["A_f0"] = f0
    mkak = np.zeros((128, 4), np.float32)
    mkak[64:, 0] = 1.0
    mkak[:64, 2] = 1e4
    mkak[:64, 3] = -1.0
    mkak[64:, 3] = 1e4
    d["A_mkak"] = mkak
    ewide = np.zeros((128, 8256), NPBF)
    xx = np.arange(8256)
    for b in range(128):
        ewide[b, (xx // 64) == b] = 1.0
    d["A_ewide"] = ewide
    return d


def phase_D(g):
    S, nc, C = g.S, g.nc, g.C
    with Scope(S, nc) as es:
        DB = load_bias_tiles(g, es, "DB", g.D_bias, 48)
        PT = [sb(es, nc, f"PT{i}", [128, 256], BF16) for i in range(2)]
        OT = sb(es, nc, "OT", [128, 2, TPC], F32)
        osb = sb(es, nc, "osb", [128, 128], F32)
        otm = sb(es, nc, "otm", [128, 65], F32)
        rden = sb(es, nc, "rden", [128, 1], F32)
        zt = sb(es, nc, "zt", [128, 128], F32)
        mst = sb(es, nc, "mst", [128, 128], BF16)
        ptr, ptrk = g.ps[6], "ps6"
        g.ui = 0
        for t in range(4):
            with Scope(S, nc) as et:
                qd = sb(et, nc, f"qd{t}", [128, 3, TPC], BF16)
                wl = WLoader(S, nc, et, g.WB, 128, name=f"qdw{t}")
                pi = 0
                for p_ in range(3):
                    wb, wk = wl.load((FMT["d_q"] + p_ * 4 + t) * 128)
                    for (t0, w) in tok_groups(128, EXT):
                        pp, pk = g.ps[pi % 2], f"ps{pi % 2}"
                        pi += 1
                        mm_fm(S, pp, pk, wb, wk, g.hT, t0, w)
                        S.op("act", lambda e: e.activation(out=qd[:, p_, t0 - 128:t0 - 128 + w], in_=pp[:, :w], func=AF.Copy, scale=0.125),
                             reads=[pk], writes=[f"qd{t}"])
                kd = load_bf(g, et, f"kd{t}", g.D_kT[t, :, :], [128, 4096])
                vd = load_bf(g, et, f"vd{t}", g.D_v[t, :, :], [128, 69 * 2 * VW])

                def run_set(p_, qsl, keys, first):
                    n = len(keys)
                    for u, (ksl, vt, w, kvc) in enumerate(keys):
                        b = g.ui % 2
                        g.ui += 1
                        regions = []
                        for h in range(2):
                            r0 = 64 * h
                            regions.append([(kd[r0:r0 + 64, ksl], qd[r0:r0 + 64, p_, qsl], [f"kd{t}", f"qd{t}"]),
                                            (C["identb"][:, :], DB[:, (p_ * 8 + 2 * t + h) * 2 + w, :], ["identb", "DB"])])
                        attn_unit(S, g.ps[2 * b], f"ps{2 * b}", g.ps[2 * b + 1], f"ps{2 * b + 1}", regions, PT[b], f"PT{b}",
                                  g.kval[:, kvc:kvc + 1], ["kval"])
                        for h in range(2):
                            v0 = (vt * 2 + h) * VW
                            S.op("pe", lambda e: e.matmul(g.ps[4 + h][:, 0:128], lhsT=vd[:, v0:v0 + VW], rhs=PT[b][:, h * 128:(h + 1) * 128],
                                                          start=(u == 0), stop=(u == n - 1)), reads=[f"vd{t}", f"PT{b}"], writes=[f"ps{4 + h}"])
                    for h in range(2):
                        if first:
                            S.op("act", lambda e: e.copy(out=OT[:, h, qsl], in_=g.ps[4 + h][:, 0:128]), reads=[f"ps{4 + h}"], writes=["OT"])
                        else:
                            S.op("dve", lambda e: e.tensor_tensor(out=OT[:, h, qsl], in0=g.ps[4 + h][:, 0:128], in1=OT[:, h, qsl], op=ALU.add),
                                 reads=[f"ps{4 + h}", "OT"], writes=["OT"])

                for i in range(NT):
                    keys = [(slice((16 + i - w) * 128, (17 + i - w) * 128), 1 + i - w, w, 112 + i - w) for w in (1, 0)]
                    run_set(0, slice(i * 128, (i + 1) * 128), keys, True)
                for U in range(4):
                    for rho in range(4):
                        q0 = 512 * U + rho
                        keys = []
                        for w in (1, 0):
                            k0 = 2048 + 512 * (U - w) + rho
                            keys.append((slice(k0, k0 + 4 * 127 + 1, 4), 17 + (U - w + 1) * 4 + rho, w, 112 if U - w >= 0 else 111))
                        run_set(1, slice(q0, q0 + 4 * 127 + 1, 4), keys, False)
                for r in range(16):
                    keys = []
                    for w in (1, 0):
                        k0 = 2048 * (1 - w) + r
                        keys.append((slice(k0, k0 + 16 * 127 + 1, 16), 37 + (1 - w) * 16 + r, w, 112 if w == 0 else 111))
                    run_set(2, slice(r, r + 16 * 127 + 1, 16), keys, False)
                for i in range(NT):
                    S.dma("sp", zt[:], g.zs[i * 128:(i + 1) * 128, 1536 + t * 128:1536 + (t + 1) * 128], reads=["zs_scr"], writes=["zt"])
                    for h in range(2):
                        S.op("pe", lambda e: e.matmul(ptr[:, 0:65], lhsT=OT[:65, h, i * 128:(i + 1) * 128], rhs=C["identf"][:65, :65], start=True, stop=True),
                             reads=["OT", "identf"], writes=[ptrk])
                        S.op("dve", lambda e: e.tensor_copy(out=otm[:], in_=ptr[:, 0:65]), reads=[ptrk], writes=["otm"])
                        S.op("dve", lambda e: e.reciprocal(out=rden[:], in_=otm[:, 64:65]), reads=["otm"], writes=["rden"])
                        S.op("dve", lambda e: e.scalar_tensor_tensor(out=mst[:, h * 64:(h + 1) * 64], in0=otm[:, 0:64], scalar=rden[:, 0:1],
                                                                     in1=zt[:, h * 64:(h + 1) * 64], op0=ALU.mult, op1=ALU.mult),
                             reads=["otm", "rden", "zt"], writes=["mst"])
                    S.dma("sp", g.mix[i * 128:(i + 1) * 128, 1536 + t * 128:1536 + (t + 1) * 128], mst[:], reads=["mst"], writes=["mix_scr"])


def host_B_D_inputs(c, layer, P, G):
    rel_d = P["rel_bias"][:, 8:]
    d = {}
    if "D_bias" in _HOST_CACHE:
        d["D_bias"] = _HOST_CACHE["D_bias"]
    d["D_kT"] = np.stack([kT_window(G, 3 + t, None, c, 96, dup=False) for t in range(4)])
    base = c * TPC
    dv = G["tm"][:, 384:896]

    def vtile(tok):
        out = np.zeros((128, 8, VW), NPBF)
        out[:, :, 64] = 1.0
        ok = tok >= 0
        if ok.any():
            out[ok, :, 0:64] = dv[tok[ok]].reshape(-1, 8, 64)
        return out

    tiles = []
    for T in range(111, 128):
        tiles.append(vtile(base + (T - 112) * 128 + np.arange(128)))
    for U in range(-1, 4):
        for rho in range(4):
            tiles.append(vtile(base + 512 * U + rho + 4 * np.arange(128)))
    for cs in range(2):
        for r in range(16):
            tiles.append(vtile(base + (cs - 1) * 2048 + r + 16 * np.arange(128)))
    V = np.stack(tiles, axis=1)
    d["D_v"] = np.stack([np.ascontiguousarray(V[:, :, 2 * t:2 * t + 2, :]).reshape(128, -1) for t in range(4)])
    if "D_bias" in d:
        return d
    DB = np.zeros((3, 8, 2, 128, 128), np.float32)
    for p_, dil in enumerate((1, 4, 16)):
        for s_ in range(8):
            for w in range(2):
                dist = 128 * w + _QI - _KI
                DB[p_, s_, w] = toeplitz_tile(rel_d[:, p_ * 8 + s_], dist * dil, (dist >= 0) & (dist <= 128))
    d["D_bias"] = _HOST_CACHE["D_bias"] = DB.reshape(48, 128, 128)
    return d


def emit_ssd(g, es, W, fm_tile0, dtraw, full, Hin=None, out_S=None, out_D=None):
    S, nc, C = g.S, g.nc, g.C
    triu = sb(es, nc, "triu", [128, 128], F32)
    ones = sb(es, nc, "onesf", [128, 128], F32)
    S.op("pool", lambda e: e.memset(ones[:], 1.0), writes=["onesf"])
    S.op("pool", lambda e: e.memset(triu[:], 1.0), writes=["triu"])
    S.op("pool", lambda e: e.affine_select(out=triu[:], in_=triu[:], pattern=[[1, 128]], compare_op=ALU.is_ge, fill=0.0, base=0,
                                            channel_multiplier=-1), reads=["triu"], writes=["triu"])
    cw = sb(es, nc, "cw", [128, 8, 4], F32)
    cbias = sb(es, nc, "cbias", [128, 8], F32)
    S.dma("sp", cw[:], g.b_cw[:, :, :], writes=["cw"])
    S.dma("sp", cbias[:], g.b_cb[:, :], writes=["cbias"])
    par = sb(es, nc, "bpar", [128, 3, 8], F32)
    S.dma("sp", par[:], g.b_par[:, :, :], writes=["bpar"])
    xsT = sb(es, nc, "xsT", [128, 4, TPC], F32)
    BT = sb(es, nc, "BT", [128, 2, TPC], BF16)
    CT = sb(es, nc, "CT", [128, 2, TPC], BF16)
    with Scope(S, nc) as e1:
        wl = WLoader(S, nc, e1, W, 128, name="wssd")
        raw = sb(e1, nc, "craw", [128, EXT], F32)
        acc = sb(e1, nc, "cacc", [128, TPC], F32)
        tmp = sb(e1, nc, "ctmp", [128, TPC], F32)
        pi = 0
        for ti in range(8):
            wb, wk = wl.load((fm_tile0 + ti) * 128)
            for (t0, w) in tok_groups(0, EXT):
                p, pk = g.ps[pi % 2], f"ps{pi % 2}"
                pi += 1
                mm_fm(S, p, pk, wb, wk, g.hT, t0, w)
                S.op("act", lambda e: e.copy(out=raw[:, t0:t0 + w], in_=p[:, :w]), reads=[pk], writes=["craw"])
            S.op("dve", lambda e: e.tensor_scalar(out=acc[:], in0=raw[:, 125:125 + TPC], scalar1=cw[:, ti, 0:1], scalar2=cbias[:, ti:ti + 1],
                                                  op0=ALU.mult, op1=ALU.add), reads=["craw", "cw", "cbias"], writes=["cacc"])
            for k in range(1, 4):
                S.op("dve", lambda e: e.scalar_tensor_tensor(out=acc[:], in0=raw[:, 125 + k:125 + k + TPC], scalar=cw[:, ti, k:k + 1], in1=acc[:],
                                                             op0=ALU.mult, op1=ALU.add), reads=["craw", "cw", "cacc"], writes=["cacc"])
            S.op("act", lambda e: e.activation(out=tmp[:], in_=acc[:], func=AF.Tanh, scale=0.5), reads=["cacc"], writes=["ctmp"])
            S.op("dve", lambda e: e.tensor_scalar(out=tmp[:], in0=tmp[:], scalar1=0.5, scalar2=0.5, op0=ALU.mult, op1=ALU.add), reads=["ctmp"], writes=["ctmp"])
            if ti < 4:
                S.op("dve", lambda e: e.tensor_tensor(out=xsT[:, ti, :], in0=tmp[:], in1=acc[:], op=ALU.mult), reads=["ctmp", "cacc"], writes=["xsT"])
            elif ti < 6:
                S.op("dve", lambda e: e.tensor_tensor(out=BT[:, ti - 4, :], in0=tmp[:], in1=acc[:], op=ALU.mult), reads=["ctmp", "cacc"], writes=["BT"])
            else:
                S.op("dve", lambda e: e.tensor_tensor(out=CT[:, ti - 6, :], in0=tmp[:], in1=acc[:], op=ALU.mult), reads=["ctmp", "cacc"], writes=["CT"])
    dt = sb(es, nc, "dt", [128, NT, 8], F32)
    adt = sb(es, nc, "adt", [128, NT, 8], F32)
    aexp = sb(es, nc, "aexp", [128, 8], F32)
    S.op("act", lambda e: e.activation(out=aexp[:], in_=par[:, 1, :], func=AF.Exp), reads=["bpar"], writes=["aexp"])
    for n in range(NT):
        S.op("dve", lambda e: e.tensor_tensor(out=dt[:, n, :], in0=dtraw[:, n + 1, :], in1=par[:, 0, :], op=ALU.add), reads=["dtraw", "bpar"], writes=["dt"])
    S.op("act", lambda e: e.activation(out=dt[:], in_=dt[:], func=AF.Exp), reads=["dt"], writes=["dt"])
    S.op("dve", lambda e: e.tensor_scalar(out=dt[:], in0=dt[:], scalar1=1.0, scalar2=None, op0=ALU.add), reads=["dt"], writes=["dt"])
    S.op("act", lambda e: e.activation(out=dt[:], in_=dt[:], func=AF.Ln), reads=["dt"], writes=["dt"])
    for n in range(NT):
        S.op("dve", lambda e: e.scalar_tensor_tensor(out=adt[:, n, :], in0=dt[:, n, :], scalar=-1.0, in1=aexp[:], op0=ALU.mult, op1=ALU.mult),
             reads=["dt", "aexp"], writes=["adt"])
    H = sb(es, nc, "Hst", [128, 512], F32)
    Hb = sb(es, nc, "Hb", [128, 512], BF16)
    sumtot = sb(es, nc, "sumtot", [128, 8], F32)
    S.op("pool", lambda e: e.memset(H[:], 0.0), writes=["Hst"])
    S.op("pool", lambda e: e.memset(sumtot[:], 0.0), writes=["sumtot"])
    if Hin is not None:
        Sp, Dp = Hin
        sp_sb = sb(es, nc, "sprev", [128, 512], F32)
        dp_sb = sb(es, nc, "dprev", [128, 56], F32)
        S.dma("sp", dp_sb[:], Dp[:, :], writes=["dprev"])
        for s_ in range(7):
            S.dma("sp", sp_sb[:], Sp[:, s_ * 512:(s_ + 1) * 512], writes=["sprev"])
            for j in range(8):
                S.op("dve", lambda e: e.tensor_scalar(out=H[:, j * 64:(j + 1) * 64], in0=H[:, j * 64:(j + 1) * 64], scalar1=dp_sb[:, s_ * 8 + j:s_ * 8 + j + 1],
                                                      scalar2=None, op0=ALU.mult), reads=["Hst", "dprev"], writes=["Hst"])
            S.op("dve", lambda e: e.tensor_tensor(out=H[:], in0=H[:], in1=sp_sb[:], op=ALU.add), reads=["Hst", "sprev"], writes=["Hst"])
    S.op("dve", lambda e: e.tensor_copy(out=Hb[:], in_=H[:]), reads=["Hst"], writes=["Hb"])
    xs = sb(es, nc, "xs_tm", [128, 512], F32)
    xdt = sb(es, nc, "xdt", [128, 512], F32)
    xdtb = sb(es, nc, "xdtb", [128, 512], BF16)
    xdtd = sb(es, nc, "xdtd", [128, 512], BF16)
    Btm = sb(es, nc, "Btm", [128, 256], BF16)
    acs = sb(es, nc, "acs", [128, 16], F32)
    eacs = sb(es, nc, "eacs", [128, 8], F32)
    dend = sb(es, nc, "dend", [128, 8], F32)
    dch = sb(es, nc, "dch", [128, 8], F32)
    if full:
        cbm = sb(es, nc, "cbm", [128, 2, 128], F32)
        adtb = sb(es, nc, "adtb", [128, 128], F32)
        dsb = sb(es, nc, "dsb", [128, 128], F32)
        Mt = sb(es, nc, "Mt", [128, 128], BF16)
        ysb = sb(es, nc, "ysb", [128, 512], F32)
        y2 = sb(es, nc, "y2", [128, 512], F32)
        zt = sb(es, nc, "zt", [128, 512], F32)
        dsk = sb(es, nc, "dsk", [128, 512], F32)
        bnw = sb(es, nc, "bnw", [128, 512], F32)
        S.dma("sp", dsk[:], g.b_dskip[:, :], writes=["dsk"])
        S.dma("sp", bnw[:], g.b_norm[:, :], writes=["bnw"])
        ssq = sb(es, nc, "ssq", [128, 2], F32)
        mst = sb(es, nc, "mst", [128, 512], BF16)
    for n in range(NT):
        cs = slice(n * 128, (n + 1) * 128)
        for ti in range(4):
            S.op("pe", lambda e: e.matmul(g.ps[0][:, ti * 128:(ti + 1) * 128], lhsT=xsT[:, ti, cs], rhs=C["identf"][:, :], start=True, stop=True),
                 reads=["xsT", "identf"], writes=["ps0"])
        S.op("act", lambda e: e.copy(out=xs[:], in_=g.ps[0][:, :]), reads=["ps0"], writes=["xs_tm"])
        for gp in range(2):
            S.op("pe", lambda e: e.transpose(out=g.ps_bf[:, gp * 128:(gp + 1) * 128], in_=BT[:, gp, cs], identity=C["identb"][:]),
                 reads=["BT", "identb"], writes=["ps_bf"])
        S.op("act", lambda e: e.copy(out=Btm[:], in_=g.ps_bf[:, 0:256]), reads=["ps_bf"], writes=["Btm"])
        S.op("pe", lambda e: e.matmul(g.ps[1][:, 0:8], lhsT=triu[:, :], rhs=adt[:, n, :], start=True, stop=True), reads=["triu", "adt"], writes=["ps1"])
        S.op("pe", lambda e: e.matmul(g.ps[1][:, 8:16], lhsT=ones[:, :], rhs=adt[:, n, :], start=True, stop=True), reads=["onesf", "adt"], writes=["ps1"])
        S.op("dve", lambda e: e.tensor_copy(out=acs[:], in_=g.ps[1][:, 0:16]), reads=["ps1"], writes=["acs"])
        S.op("act", lambda e: e.activation(out=eacs[:], in_=acs[:, 0:8], func=AF.Exp), reads=["acs"], writes=["eacs"])
        S.op("act", lambda e: e.activation(out=dch[:], in_=acs[:, 8:16], func=AF.Exp), reads=["acs"], writes=["dch"])
        S.op("dve", lambda e: e.tensor_tensor(out=dend[:], in0=acs[:, 8:16], in1=acs[:, 0:8], op=ALU.subtract), reads=["acs"], writes=["dend"])
        S.op("act", lambda e: e.activation(out=dend[:], in_=dend[:], func=AF.Exp), reads=["dend"], writes=["dend"])
        S.op("dve", lambda e: e.tensor_tensor(out=sumtot[:], in0=sumtot[:], in1=acs[:, 8:16], op=ALU.add), reads=["sumtot", "acs"], writes=["sumtot"])
        for j in range(8):
            js = slice(j * 64, (j + 1) * 64)
            S.op("dve", lambda e: e.tensor_scalar(out=xdt[:, js], in0=xs[:, js], scalar1=dt[:, n, j:j + 1], scalar2=None, op0=ALU.mult),
                 reads=["xs_tm", "dt"], writes=["xdt"])
            S.op("pool", lambda e: e.tensor_scalar(out=xdtd[:, js], in0=xdt[:, js], scalar1=dend[:, j:j + 1], scalar2=None, op0=ALU.mult),
                 reads=["xdt", "dend"], writes=["xdtd"])
        S.op("act", lambda e: e.copy(out=xdtb[:], in_=xdt[:]), reads=["xdt"], writes=["xdtb"])
        if full:
            for gp in range(2):
                S.op("pe", lambda e: e.matmul(g.ps[2][:, gp * 128:(gp + 1) * 128], lhsT=BT[:, gp, cs], rhs=CT[:, gp, cs], start=True, stop=True),
                     reads=["BT", "CT"], writes=["ps2"])
            S.op("dve", lambda e: e.tensor_tensor(out=cbm[:], in0=g.ps[2][:, 0:256].rearrange("p (a b) -> p a b", a=2),
                                                  in1=triu[:, :].unsqueeze(1).to_broadcast([128, 2, 128]), op=ALU.mult), reads=["ps2", "triu"], writes=["cbm"])
            for j in range(8):
                gp = j // 4
                js = slice(j * 64, (j + 1) * 64)
                S.op("pool", lambda e: e.tensor_scalar(out=adtb[:], in0=ones[:], scalar1=adt[:, n, j:j + 1], scalar2=None, op0=ALU.mult),
                     reads=["onesf", "adt"], writes=["adtb"])
                S.op("pe", lambda e: e.matmul(g.ps[3][:, 0:128], lhsT=adtb[:, :], rhs=triu[:, :], start=True, stop=True), reads=["adtb", "triu"], writes=["ps3"])
                S.op("dve", lambda e: e.tensor_scalar(out=dsb[:], in0=g.ps[3][:, 0:128], scalar1=acs[:, j:j + 1], scalar2=0.0, op0=ALU.subtract, op1=ALU.min),
                     reads=["ps3", "acs"], writes=["dsb"])
                S.op("act", lambda e: e.activation(out=dsb[:], in_=dsb[:], func=AF.Exp), reads=["dsb"], writes=["dsb"])
                S.op("dve", lambda e: e.tensor_tensor(out=Mt[:], in0=dsb[:], in1=cbm[:, gp, :], op=ALU.mult), reads=["dsb", "cbm"], writes=["Mt"])
                S.op("pe", lambda e: e.matmul(g.ps[4][:, js], lhsT=Mt[:, :], rhs=xdtb[:, js], start=True, stop=True), reads=["Mt", "xdtb"], writes=["ps4"])
            for gp in range(2):
                S.op("pe", lambda e: e.matmul(g.ps[5][:, gp * 256:(gp + 1) * 256], lhsT=CT[:, gp, cs], rhs=Hb[:, gp * 256:(gp + 1) * 256], start=True, stop=True),
                     reads=["CT", "Hb"], writes=["ps5"])
            for j in range(8):
                js = slice(j * 64, (j + 1) * 64)
                S.op("dve", lambda e: e.tensor_scalar(out=ysb[:, js], in0=g.ps[5][:, js], scalar1=eacs[:, j:j + 1], scalar2=None, op0=ALU.mult),
                     reads=["ps5", "eacs"], writes=["ysb"])
            S.op("dve", lambda e: e.tensor_tensor(out=ysb[:], in0=g.ps[4][:, :], in1=ysb[:], op=ALU.add), reads=["ps4", "ysb"], writes=["ysb"])
            S.op("pool", lambda e: e.tensor_tensor(out=y2[:], in0=xs[:], in1=dsk[:], op=ALU.mult), reads=["xs_tm", "dsk"], writes=["y2"])
            S.op("dve", lambda e: e.tensor_tensor(out=ysb[:], in0=ysb[:], in1=y2[:], op=ALU.add), reads=["ysb", "y2"], writes=["ysb"])
            S.dma("sp", zt[:], g.zs[n * 128:(n + 1) * 128, 512:1024], reads=["zs_scr"], writes=["zt"])
            S.op("dve", lambda e: e.tensor_tensor(out=ysb[:], in0=ysb[:], in1=zt[:], op=ALU.mult), reads=["ysb", "zt"], writes=["ysb"])
            S.op("pool", lambda e: e.memset(ssq[:], 0.0), writes=["ssq"])
            for gp in range(2):
                S.op("act", lambda e: e.activation(out=y2[:, gp * 256:(gp + 1) * 256], in_=ysb[:, gp * 256:(gp + 1) * 256], func=AF.Square,
                                                   accum_out=ssq[:, gp:gp + 1]), reads=["ysb", "ssq"], writes=["y2", "ssq"])
            S.op("act", lambda e: e.activation(out=ssq[:], in_=ssq[:], func=AF.Sqrt, bias=C["epsc"][:, 0:1], scale=1.0 / 256), reads=["ssq"], writes=["ssq"])
            S.op("dve", lambda e: e.reciprocal(out=ssq[:], in_=ssq[:]), reads=["ssq"], writes=["ssq"])
            for gp in range(2):
                gs = slice(gp * 256, (gp + 1) * 256)
                S.op("dve", lambda e: e.scalar_tensor_tensor(out=mst[:, gs], in0=ysb[:, gs], scalar=ssq[:, gp:gp + 1], in1=bnw[:, gs], op0=ALU.mult, op1=ALU.mult),
                     reads=["ysb", "ssq", "bnw"], writes=["mst"])
            S.dma("sp", g.mix[n * 128:(n + 1) * 128, 512:1024], mst[:], reads=["mst"], writes=["mix_scr"])
        for gp in range(2):
            S.op("pe", lambda e: e.matmul(g.ps[6][:, gp * 256:(gp + 1) * 256], lhsT=Btm[:, gp * 128:(gp + 1) * 128], rhs=xdtd[:, gp * 256:(gp + 1) * 256],
                                          start=True, stop=True), reads=["Btm", "xdtd"], writes=["ps6"])
        for j in range(8):
            js = slice(j * 64, (j + 1) * 64)
            S.op("dve", lambda e: e.tensor_scalar(out=H[:, js], in0=H[:, js], scalar1=dch[:, j:j + 1], scalar2=None, op0=ALU.mult),
                 reads=["Hst", "dch"], writes=["Hst"])
        S.op("dve", lambda e: e.tensor_tensor(out=H[:], in0=g.ps[6][:, :], in1=H[:], op=ALU.add), reads=["ps6", "Hst"], writes=["Hst"])
        S.op("act", lambda e: e.copy(out=Hb[:], in_=H[:]), reads=["Hst"], writes=["Hb"])
    if out_S is not None:
        S.dma("sp", out_S[:, :], H[:], reads=["Hst"], writes=["o_S"])
        S.op("act", lambda e: e.activation(out=sumtot[:], in_=sumtot[:], func=AF.Exp), reads=["sumtot"], writes=["sumtot"])
        S.dma("sp", out_D[:, :], sumtot[:], reads=["sumtot"], writes=["o_D"])


def phase_B(g):
    with Scope(g.S, g.nc) as es:
        emit_ssd(g, es, g.WB, FMT["b_x"], g.dtraw, True, Hin=(g.b_Sprev, g.b_Dprev))


def host_ssd_params(layer, P):
    bc = lambda v: np.ascontiguousarray(np.broadcast_to(v[None, :], (128, v.shape[0]))).astype(np.float32)
    d = {}
    d["b_cw"] = np.ascontiguousarray(P["b_conv_w"][layer].reshape(4, 8, 128).transpose(2, 1, 0))
    d["b_cb"] = np.ascontiguousarray(P["b_conv_b"][layer].reshape(8, 128).T)
    par = np.zeros((128, 3, 8), np.float32)
    par[:, 0, :] = P["b_dt_bias"][layer][None, :]
    par[:, 1, :] = P["b_a_log"][layer][None, :]
    d["b_par"] = par
    d["b_dskip"] = bc(np.repeat(P["b_d"][layer], 64))
    d["b_norm"] = bc(P["b_norm"][layer])
    return d


def host_B_ssd_chain(c, G):
    Sp = np.zeros((128, 7 * 512), np.float32)
    Dp = np.ones((128, 56), np.float32)
    for s_ in range(7):
        src = c - 7 + s_
        if src >= 0:
            Sp[:, s_ * 512:(s_ + 1) * 512] = G["S"][src]
            Dp[:, s_ * 8:(s_ + 1) * 8] = G["D"][src]
    return {"b_Sprev": Sp, "b_Dprev": Dp}


def get_nc(name):
    if name not in _NC_CACHE:
        _NC_CACHE[name] = {"A": build_A, "B": build_B}[name]()
    return _NC_CACHE[name]


def kernel(**inputs):
    P = {k: np.asarray(v) for k, v in inputs.items()}
    _HOST_CACHE.clear()
    x = np.ascontiguousarray(P["x"][0], dtype=np.float32)
    cores = list(range(NCORE))
    for layer in range(4):
        resA = run_bass_kernel_spmd(get_nc("A"), [host_A_inputs(c, x, layer, P) for c in cores], core_ids=cores)
        G = gather_A(resA.results)
        resB = run_bass_kernel_spmd(get_nc("B"), [host_B_inputs(c, x, layer, P, G) for c in cores], core_ids=cores)
        x = np.concatenate([np.asarray(resB.results[c]["x_out"], dtype=np.float32) for c in cores], axis=0)
    return x[None].astype(np.float32)
```

```python
import math
import os
from contextlib import ExitStack
import numpy as np
import ml_dtypes
import concourse.bass as bass
import concourse.mybir as mybir
from concourse.bass_utils import run_bass_kernel_spmd

F32 = mybir.dt.float32
BF16 = mybir.dt.bfloat16
I32 = mybir.dt.int32
AF = mybir.ActivationFunctionType
ALU = mybir.AluOpType
AX = mybir.AxisListType
NPBF = ml_dtypes.bfloat16

NCORE = 8
TPC = 2048
NT = 16
NTT = NT + 1
EXT = NTT * 128
DM = 1024
EPS = 1e-6
NEG = -30000.0
SEM_LIMIT = 30000


class Sync:
    def __init__(self, nc, n_dma_sems=12):
        self.nc = nc
        self.engs = {"pe": nc.tensor, "dve": nc.vector, "act": nc.scalar, "pool": nc.gpsimd, "sp": nc.sync}
        self.nsem = 0
        self.sem = {}
        self.semkey = {}
        self.cnt = {}
        for k in self.engs:
            self._newsem(k)
        self.waited = {k: {} for k in self.engs}
        self.dma_sems = [nc.alloc_semaphore(f"sdma{i}") for i in range(n_dma_sems)]
        self.dma_uses = [0] * n_dma_sems
        self.dma_i = 0
        self.bufs = {}
        self.nops = 0

    def _newsem(self, k):
        self.nsem += 1
        self.sem[k] = self.nc.alloc_semaphore(f"s{k}{self.nsem}")
        self.semkey[k] = f"{k}{self.nsem}"
        self.cnt[k] = 0

    def _wait(self, eng, tok):
        if tok is None:
            return
        semkey, sem, val, src = tok
        if eng == "pe" and src == "pe":
            return
        if self.waited[eng].get(semkey, 0) >= val:
            return
        self.engs[eng].wait_ge(sem, val)
        self.waited[eng][semkey] = val

    def _deps(self, eng, reads, writes):
        for k in reads:
            b = self.bufs.get(k)
            if b:
                self._wait(eng, b["w"])
                if k.startswith("ps"):
                    for t in b["r"]:
                        if t[3] != eng:
                            self._wait(eng, t)
        for k in writes:
            b = self.bufs.get(k)
            if b:
                self._wait(eng, b["w"])
                for t in b["r"]:
                    self._wait(eng, t)

    def _commit(self, tok, reads, writes):
        for k in reads:
            b = self.bufs.setdefault(k, {"w": None, "r": []})
            b["r"].append(tok)
            if len(b["r"]) > 24:
                last = {}
                for t in b["r"]:
                    if t[0] not in last or last[t[0]][2] < t[2]:
                        last[t[0]] = t
                b["r"] = list(last.values())
        for k in writes:
            self.bufs[k] = {"w": tok, "r": []}

    def op(self, eng, fn, reads=(), writes=()):
        self._deps(eng, reads, writes)
        if self.cnt[eng] >= SEM_LIMIT:
            self._newsem(eng)
        ins = fn(self.engs[eng])
        self.cnt[eng] += 1
        ins.then_inc(self.sem[eng], 1)
        tok = (self.semkey[eng], self.sem[eng], self.cnt[eng], eng)
        self._commit(tok, reads, writes)
        self.nops += 1
        return tok

    def dma(self, eng, out, in_, reads=(), writes=()):
        self._deps(eng, reads, writes)
        i = self.dma_i % len(self.dma_sems)
        self.dma_i += 1
        sem = self.dma_sems[i]
        if self.dma_uses[i] > 0:
            self._wait(eng, (f"dma{i}", sem, 16 * self.dma_uses[i], "dma"))
        if self.dma_uses[i] * 16 >= SEM_LIMIT:
            sem = self.dma_sems[i] = self.nc.alloc_semaphore(f"sdma{i}_{self.dma_i}")
            self.dma_uses[i] = 0
            self.waited_reset(f"dma{i}")
        self.dma_uses[i] += 1
        self.engs[eng].dma_start(out=out, in_=in_).then_inc(sem, 16)
        tok = (f"dma{i}", sem, 16 * self.dma_uses[i], "dma")
        self._commit(tok, reads, writes)
        self.nops += 1
        return tok

    def waited_reset(self, semkey):
        for e in self.waited:
            self.waited[e].pop(semkey, None)
        for b in self.bufs.values():
            if b["w"] is not None and b["w"][0] == semkey:
                b["w"] = None
            b["r"] = [t for t in b["r"] if t[0] != semkey]

    def release(self, keys):
        for eng in self.engs:
            for k in keys:
                b = self.bufs.get(k)
                if b:
                    self._wait(eng, b["w"])
                    for t in b["r"]:
                        self._wait(eng, t)
        for k in keys:
            self.bufs.pop(k, None)

    def finish(self, eng="sp"):
        for k, b in self.bufs.items():
            self._wait(eng, b["w"])
            for t in b["r"]:
                self._wait(eng, t)


OFF = {}
_o = 0
for _n, _w in [("a_q", 512), ("a_kc", 128), ("a_vc", 128), ("a_ks", 128), ("a_vs", 128), ("a_kw", 128), ("a_vw", 128),
               ("a_gate", 24), ("a_z", 512), ("b_x", 512), ("b_B", 256), ("b_C", 256), ("b_dt", 8), ("b_z", 512),
               ("c_q", 512), ("c_k", 128), ("c_v", 128), ("c_z", 512), ("d_q", 1536), ("d_k", 512), ("d_v", 512),
               ("d_z", 512), ("m_q", 256), ("m_z", 256)]:
    OFF[_n] = (_o, _w)
    _o += _w
assert _o == 8224


def cols(*names):
    out = []
    for n in names:
        o, w = OFF[n]
        out.extend(range(o, o + w))
    return out


A_FM = ["a_kc", "a_vc", "a_ks", "a_kw", "c_k", "d_k", "b_x", "b_B", "b_C"]
A_TM = ["a_vs", "a_vw", "c_v", "d_v", "b_dt"]
A_COLS = cols(*A_FM) + cols(*A_TM)
A_NFM = 17
A_TM0 = A_NFM * 128


class Scope:
    def __init__(self, S, nc):
        self.S, self.nc, self.es, self.names = S, nc, ExitStack(), []

    def __enter__(self):
        self.es.__enter__()
        return self

    def __exit__(self, *a):
        self.S.release(self.names)
        return self.es.__exit__(*a)

    def enter_context(self, cm):
        return self.es.enter_context(cm)


_SB_COUNT = [0]


def sb(es, nc, name, shape, dt):
    if isinstance(es, Scope):
        es.names.append(name)
    _SB_COUNT[0] += 1
    return es.enter_context(nc.sbuf_tensor(f"{name}__{_SB_COUNT[0]}", shape, dt))


def emit_consts(S, nc, es):
    C = {}
    C["identb"] = sb(es, nc, "identb", [128, 128], BF16)
    C["identf"] = sb(es, nc, "identf", [128, 128], F32)
    for nm in ("identb", "identf"):
        t = C[nm]
        S.op("pool", lambda e: e.memset(t[:], 1.0), writes=[nm])
        S.op("pool", lambda e: e.affine_select(out=t[:], in_=t[:], pattern=[[-1, 128]], compare_op=ALU.is_equal,
                                                fill=0.0, base=0, channel_multiplier=1), reads=[nm], writes=[nm])
    return C


def emit_hT(S, nc, C, x_ext, prenorm_b, hT, ps_bf, ntt=NTT):
    with Scope(S, nc) as es:
        pn = sb(es, nc, "pn", [128, DM], F32)
        S.dma("sp", pn[:], prenorm_b[:, :], writes=["pn"])
        xt = [sb(es, nc, f"xt{i}", [128, DM], F32) for i in range(2)]
        hb = [sb(es, nc, f"hb{i}", [128, DM], BF16) for i in range(2)]
        junk = sb(es, nc, "junk", [128, DM], F32)
        ss = [sb(es, nc, f"ss{i}", [128, 1], F32) for i in range(2)]
        for t in range(ntt):
            b = t % 2
            S.dma("sp", xt[b][:], x_ext[t * 128:(t + 1) * 128, :], writes=[f"xt{b}"])
            S.op("pool", lambda e: e.memset(ss[b][:], 0.0), writes=[f"ss{b}"])
            S.op("act", lambda e: e.activation(out=junk[:], in_=xt[b][:], func=AF.Square, accum_out=ss[b][:]),
                 reads=[f"xt{b}", f"ss{b}"], writes=["junk", f"ss{b}"])
            S.op("act", lambda e: e.activation(out=ss[b][:], in_=ss[b][:], func=AF.Sqrt, bias=C["epsc"][:, 0:1], scale=1.0 / DM),
                 reads=[f"ss{b}"], writes=[f"ss{b}"])
            S.op("dve", lambda e: e.reciprocal(out=ss[b][:], in_=ss[b][:]), reads=[f"ss{b}"], writes=[f"ss{b}"])
            S.op("dve", lambda e: e.scalar_tensor_tensor(out=hb[b][:], in0=xt[b][:], scalar=ss[b][:, 0:1], in1=pn[:],
                                                         op0=ALU.mult, op1=ALU.mult),
                 reads=[f"xt{b}", f"ss{b}", "pn"], writes=[f"hb{b}"])
            for k in range(8):
                S.op("pe", lambda e: e.transpose(out=ps_bf[:, k * 128:(k + 1) * 128], in_=hb[b][:, k * 128:(k + 1) * 128],
                                                 identity=C["identb"][:]),
                     reads=[f"hb{b}", "identb"], writes=["ps_bf"])
            S.op("act", lambda e: e.copy(out=hT[:, :, t * 128:(t + 1) * 128],
                                         in_=ps_bf[:, :].rearrange("p (k t) -> p k t", k=8)),
                 reads=["ps_bf"], writes=["hT"])


class WLoader:
    def __init__(self, S, nc, es, W, width, name="w"):
        self.S, self.nc, self.W, self.width, self.name = S, nc, W, width, name
        self.wf = [sb(es, nc, f"{name}f{i}", [128, 8, width], F32) for i in range(2)]
        self.wb = [sb(es, nc, f"{name}b{i}", [128, 8, width], BF16) for i in range(2)]
        self.i = 0

    def load(self, c0, w=None):
        w = w or self.width
        b = self.i % 2
        self.i += 1
        S = self.S
        S.dma("sp", self.wf[b][:, :, :w], self.W[:, c0:c0 + w].rearrange("(k p) c -> p k c", p=128),
              writes=[f"{self.name}f{b}"])
        S.op("pool", lambda e: e.tensor_copy(out=self.wb[b][:, :, :w], in_=self.wf[b][:, :, :w]),
             reads=[f"{self.name}f{b}"], writes=[f"{self.name}b{b}"])
        return self.wb[b], f"{self.name}b{b}"


def tok_groups(t0, t1, g=512):
    out = []
    while t0 < t1:
        w = min(g, t1 - t0)
        out.append((t0, w))
        t0 += w
    return out


def mm_fm(S, ps, pskey, wb, wkey, hT, t0, w, m=128, mo=0):
    for k in range(8):
        S.op("pe", lambda e: e.matmul(ps[:m, :w], lhsT=wb[:, k, mo:mo + m], rhs=hT[:, k, t0:t0 + w], start=(k == 0), stop=(k == 7)),
             reads=[wkey, "hT"], writes=[pskey])


def mm_tm(S, ps, pskey, wb, wkey, hT, t, c0, w):
    for k in range(8):
        S.op("pe", lambda e: e.matmul(ps[:, :w], lhsT=hT[:, k, t * 128:(t + 1) * 128], rhs=wb[:, k, c0:c0 + w], start=(k == 0), stop=(k == 7)),
             reads=[wkey, "hT"], writes=[pskey])


def build_A():
    nc = bass.Bass("TRN2", target_bir_lowering=False)
    D = lambda name, shape, dt, kind: nc.dram_tensor(name, shape, dt, kind=kind).ap()
    x_ext = D("x_ext", [EXT, DM], F32, "ExternalInput")
    prenorm_b = D("prenorm_b", [128, DM], F32, "ExternalInput")
    WA = D("WA", [DM, len(A_COLS)], F32, "ExternalInput")
    pos_b = D("pos_b", [128, TPC], I32, "ExternalInput")
    ropec = D("ropec", [128, 2], F32, "ExternalInput")
    ropeR = D("ropeR", [128, 128], F32, "ExternalInput")
    w1d = D("w1d", [2, 128, 32 * 128], F32, "ExternalInput")
    w2 = D("w2", [2, 128, 64], F32, "ExternalInput")
    peT = D("peT", [2, 128, 32], F32, "ExternalInput")
    o_fm = D("o_fm", [7, 128, TPC], BF16, "ExternalOutput")
    o_tm = D("o_tm", [TPC, 896], BF16, "ExternalOutput")
    o_kcT = D("o_kcT", [128, 128], BF16, "ExternalOutput")
    o_vc = D("o_vc", [128, 128], BF16, "ExternalOutput")
    o_S = D("o_S", [128, 512], F32, "ExternalOutput")
    o_D = D("o_D", [128, 8], F32, "ExternalOutput")
    g = PB()
    g.nc = nc
    g.b_cw = D("b_cw", [128, 8, 4], F32, "ExternalInput")
    g.b_cb = D("b_cb", [128, 8], F32, "ExternalInput")
    g.b_par = D("b_par", [128, 3, 8], F32, "ExternalInput")

    with ExitStack() as es:
        S = Sync(nc)
        C = emit_consts(S, nc, es)
        C["epsc"] = sb(es, nc, "epsc", [128, 1], F32)
        S.op("pool", lambda e: e.memset(C["epsc"][:], EPS), writes=["epsc"])
        hT = sb(es, nc, "hT", [128, 8, EXT], BF16)
        ps = [es.enter_context(nc.psum_tensor(f"ps{i}", [128, 512], F32)) for i in range(7)]
        ps_bf = es.enter_context(nc.psum_tensor("ps_bf", [128, 1024], BF16))
        emit_hT(S, nc, C, x_ext, prenorm_b, hT, ps_bf)
        g.S, g.C, g.hT, g.ps, g.ps_bf = S, C, hT, ps, ps_bf

        dtraw = sb(es, nc, "dtraw", [128, NTT, 8], F32)
        with Scope(S, nc) as e1:
            cosT = sb(e1, nc, "cosT", [128, TPC], F32)
            sinT = sb(e1, nc, "sinT", [128, TPC], F32)
            rc = sb(e1, nc, "rc", [128, 2], F32)
            rR = sb(e1, nc, "rR", [128, 128], F32)
            S.dma("sp", rc[:], ropec[:, :], writes=["rc"])
            S.dma("sp", rR[:], ropeR[:, :], writes=["rR"])
            emit_rope_tables(S, nc, pos_b, rc, cosT, sinT)

            wl = WLoader(S, nc, e1, WA, 128)
            kcT = sb(e1, nc, "kcT_raw", [128, 2, EXT], BF16)
            stage = [sb(e1, nc, f"stage{i}", [128, TPC], BF16) for i in range(2)]
            xf = sb(e1, nc, "xf", [128, 512], F32)
            pi = 0
            for ti in range(9):
                wb, wkey = wl.load(ti * 128)
                if ti < 2:
                    for (t0, w) in tok_groups(0, EXT):
                        p, pk = ps[pi % 4], f"ps{pi % 4}"
                        pi += 1
                        mm_fm(S, p, pk, wb, wkey, hT, t0, w)
                        S.op("act", lambda e: e.copy(out=kcT[:, ti, t0:t0 + w], in_=p[:, :w]), reads=[pk], writes=["kcT_raw"])
                else:
                    sbuf, skey = stage[ti % 2], f"stage{ti % 2}"
                    for (t0, w) in tok_groups(128, EXT):
                        p, pk = ps[pi % 4], f"ps{pi % 4}"
                        pi += 1
                        mm_fm(S, p, pk, wb, wkey, hT, t0, w)
                        l0 = t0 - 128
                        if ti == 4:
                            emit_rope_apply(S, nc, p, pk, w, xf, rR, cosT, sinT, l0, ps[4], "ps4", sbuf[:, l0:l0 + w], skey, scale=1.0)
                        else:
                            S.op("act", lambda e: e.copy(out=sbuf[:, l0:l0 + w], in_=p[:, :w]), reads=[pk], writes=[skey])
                    S.dma("sp", o_fm[ti - 2, :, :], sbuf[:, :], reads=[skey], writes=[f"o_fm{ti}"])
            wlt = WLoader(S, nc, e1, WA, 512, name="wt")
            vst = [sb(e1, nc, f"vst{i}", [128, 896], BF16) for i in range(2)]
            wbs = []
            wA, kA = wlt.load(A_TM0, 512)
            wB, kB = wlt.load(A_TM0 + 512, 384)
            for t in range(1, NTT):
                b = t % 2
                for (wb, wk, c0, w) in ((wA, kA, 0, 512), (wB, kB, 512, 384)):
                    p, pk = ps[pi % 4], f"ps{pi % 4}"
                    pi += 1
                    mm_tm(S, p, pk, wb, wk, hT, t, 0, w)
                    S.op("act", lambda e: e.copy(out=vst[b][:, c0:c0 + w], in_=p[:, :w]), reads=[pk], writes=[f"vst{b}"])
                S.dma("sp", o_tm[(t - 1) * 128:t * 128, :], vst[b][:, :], reads=[f"vst{b}"], writes=[f"o_tm{t}"])

            emit_compress(S, nc, e1, C, kcT, w1d, w2, peT, ps, o_kcT, o_vc)
            wd, kd_ = wlt.load(A_TM0 + 896, 8)
            for t in range(NTT):
                p, pk = ps[pi % 4], f"ps{pi % 4}"
                pi += 1
                mm_tm(S, p, pk, wd, kd_, hT, t, 0, 8)
                S.op("act", lambda e: e.copy(out=dtraw[:, t, :], in_=p[:, 0:8]), reads=[pk], writes=["dtraw"])
        with Scope(S, nc) as e2:
            emit_ssd(g, e2, WA, 9, dtraw, False, out_S=o_S, out_D=o_D)
        S.finish("sp")
    return nc


def emit_rope_tables(S, nc, pos_b, rc, cosT, sinT):
    with Scope(S, nc) as es:
        pi_ = sb(es, nc, "pos_i", [128, TPC], I32)
        ang = sb(es, nc, "ang", [128, TPC], F32)
        tmp = sb(es, nc, "rtmp", [128, TPC], F32)
        kf = sb(es, nc, "rkf", [128, TPC], F32)
        S.dma("sp", pi_[:], pos_b[:, :], writes=["pos_i"])
        S.op("dve", lambda e: e.tensor_copy(out=ang[:], in_=pi_[:]), reads=["pos_i"], writes=["ang"])
        S.op("dve", lambda e: e.tensor_scalar(out=ang[:], in0=ang[:], scalar1=rc[:, 0:1], scalar2=None, op0=ALU.mult),
             reads=["ang", "rc"], writes=["ang"])
        for (dst, dk, shift) in ((sinT, "sinT", 0.0), (cosT, "cosT", 0.25)):
            S.op("dve", lambda e: e.tensor_scalar(out=tmp[:], in0=ang[:], scalar1=shift, scalar2=None, op0=ALU.add),
                 reads=["ang"], writes=["rtmp"])
            S.op("dve", lambda e: e.tensor_copy(out=pi_[:], in_=tmp[:]), reads=["rtmp"], writes=["pos_i"])
            S.op("dve", lambda e: e.tensor_copy(out=kf[:], in_=pi_[:]), reads=["pos_i"], writes=["rkf"])
            S.op("dve", lambda e: e.tensor_tensor(out=tmp[:], in0=tmp[:], in1=kf[:], op=ALU.subtract),
                 reads=["rtmp", "rkf"], writes=["rtmp"])
            S.op("act", lambda e: e.activation(out=dst[:], in_=tmp[:], func=AF.Sin, scale=2.0 * math.pi),
                 reads=["rtmp"], writes=[dk])


def emit_rope_apply(S, nc, p, pk, w, xf, rR, cosT, sinT, l0, p2, p2k, dst, dkey, scale=1.0):
    S.op("act", lambda e: e.copy(out=xf[:, :w], in_=p[:, :w]), reads=[pk], writes=["xf"])
    S.op("pe", lambda e: e.matmul(p2[:, :w], lhsT=rR[:, :], rhs=xf[:, :w], start=True, stop=True), reads=["rR", "xf"], writes=[p2k])
    S.op("dve", lambda e: e.tensor_tensor(out=xf[:, :w], in0=xf[:, :w], in1=cosT[:, l0:l0 + w], op=ALU.mult),
         reads=["xf", "cosT"], writes=["xf"])
    S.op("dve", lambda e: e.tensor_tensor(out=p[:, :w], in0=p2[:, :w], in1=sinT[:, l0:l0 + w], op=ALU.mult),
         reads=[p2k, "sinT"], writes=[pk])
    if scale == 1.0:
        S.op("dve", lambda e: e.tensor_tensor(out=dst, in0=p[:, :w], in1=xf[:, :w], op=ALU.add), reads=[pk, "xf"], writes=[dkey])
    else:
        S.op("dve", lambda e: e.scalar_tensor_tensor(out=dst, in0=p[:, :w], scalar=scale, in1=xf[:, :w], op0=ALU.mult, op1=ALU.add),
             reads=[pk, "xf"], writes=[dkey])


def emit_compress(S, nc, es0, C, kcT, w1d, w2, peT, ps, o_kcT, o_vc):
    with Scope(S, nc) as es:
        w1f = sb(es, nc, "w1f", [128, 32 * 128], F32)
        w1b = [sb(es, nc, f"w1b{i}", [128, 32, 128], BF16) for i in range(2)]
        w2f = sb(es, nc, "w2f", [128, 2, 64], F32)
        w2b = sb(es, nc, "w2b", [128, 2, 64], BF16)
        pef = sb(es, nc, "pef", [128, 2, 32], F32)
        peb = sb(es, nc, "peb", [128, 2, 32], BF16)
        biasc = sb(es, nc, "biasc", [128, 2], F32)
        g1 = sb(es, nc, "g1", [128, 128], F32)
        g2 = sb(es, nc, "g2", [128, 128], F32)
        hid = sb(es, nc, "hid", [128, 128], BF16)
        okc = sb(es, nc, "okc", [128, 128], BF16)
        ovc = sb(es, nc, "ovc", [128, 128], BF16)
        for kv in range(2):
            S.dma("sp", w1f[:], w1d[kv, :, :], writes=["w1f"])
            S.op("pool", lambda e: e.tensor_copy(out=w1b[kv][:, :, :], in_=w1f[:, :].rearrange("p (l m) -> p l m", l=32)),
                 reads=["w1f"], writes=[f"w1b{kv}"])
            S.dma("sp", w2f[:, kv, :], w2[kv, :, :], writes=["w2f"])
            S.dma("sp", pef[:, kv, :], peT[kv, :, :], writes=["pef"])
        S.op("dve", lambda e: e.tensor_copy(out=w2b[:], in_=w2f[:]), reads=["w2f"], writes=["w2b"])
        S.op("dve", lambda e: e.tensor_copy(out=peb[:], in_=pef[:]), reads=["pef"], writes=["peb"])
        for kv in range(2):
            for l in range(32):
                S.op("pe", lambda e: e.matmul(ps[5][:, 0:1], lhsT=w1b[kv][0:64, l, :], rhs=peb[0:64, kv, l:l + 1], start=(l == 0), stop=(l == 31)),
                     reads=[f"w1b{kv}", "peb"], writes=["ps5"])
            S.op("act", lambda e: e.copy(out=biasc[:, kv:kv + 1], in_=ps[5][:, 0:1]), reads=["ps5"], writes=["biasc"])
            for hh in range(2):
                p, pk = ps[hh], f"ps{hh}"
                r0 = 64 * hh
                for l in range(32):
                    S.op("pe", lambda e: e.matmul(p[:, 0:128], lhsT=w1b[kv][r0:r0 + 64, l, :],
                                                  rhs=kcT[r0:r0 + 64, kv, 112 + l:112 + l + 16 * 127 + 1:16], start=(l == 0), stop=(l == 31)),
                         reads=[f"w1b{kv}", "kcT_raw"], writes=[pk])
                S.op("act", lambda e: e.activation(out=g1[:], in_=p[:, 0:128], func=AF.Identity, bias=biasc[:, kv:kv + 1], scale=1.0),
                     reads=[pk, "biasc"], writes=["g1"])
                S.op("dve", lambda e: e.tensor_tensor(out=g2[:], in0=g1[:], in1=g1[:], op=ALU.mult), reads=["g1"], writes=["g2"])
                S.op("dve", lambda e: e.tensor_scalar(out=g2[:], in0=g2[:], scalar1=0.044715, scalar2=1.0, op0=ALU.mult, op1=ALU.add),
                     reads=["g2"], writes=["g2"])
                S.op("dve", lambda e: e.tensor_tensor(out=g2[:], in0=g2[:], in1=g1[:], op=ALU.mult), reads=["g1", "g2"], writes=["g2"])
                S.op("act", lambda e: e.activation(out=g2[:], in_=g2[:], func=AF.Tanh, scale=math.sqrt(2.0 / math.pi)),
                     reads=["g2"], writes=["g2"])
                S.op("dve", lambda e: e.scalar_tensor_tensor(out=g2[:], in0=g2[:], scalar=1.0, in1=g1[:], op0=ALU.add, op1=ALU.mult),
                     reads=["g1", "g2"], writes=["g2"])
                S.op("act", lambda e: e.activation(out=hid[:], in_=g2[:], func=AF.Copy, scale=0.5), reads=["g2"], writes=["hid"])
                if kv == 0:
                    S.op("pe", lambda e: e.matmul(ps[2][0:64, 0:128], lhsT=w2b[:, 0, :], rhs=hid[:], start=True, stop=True),
                         reads=["w2b", "hid"], writes=["ps2"])
                    S.op("act", lambda e: e.copy(out=okc[r0:r0 + 64, :], in_=ps[2][0:64, 0:128]), reads=["ps2"], writes=["okc"])
                else:
                    S.op("pe", lambda e: e.matmul(ps[2][:, 0:64], lhsT=hid[:], rhs=w2b[:, 1, :], start=True, stop=True),
                         reads=["w2b", "hid"], writes=["ps2"])
                    S.op("act", lambda e: e.copy(out=ovc[:, r0:r0 + 64], in_=ps[2][:, 0:64]), reads=["ps2"], writes=["ovc"])
        S.dma("sp", o_kcT[:, :], okc[:], reads=["okc"], writes=["o_kcT"])
        S.dma("sp", o_vc[:, :], ovc[:], reads=["ovc"], writes=["o_vc"])


def host_A_inputs(c, x_cur, layer, P):
    lo = c * TPC
    x_ext = np.zeros((EXT, DM), np.float32)
    if c > 0:
        x_ext[:128] = x_cur[lo - 128:lo]
    x_ext[128:] = x_cur[lo:lo + TPC]
    inv = (150000.0 ** (-np.arange(32, dtype=np.float32) / 32)).astype(np.float32)
    ropec = np.zeros((128, 2), np.float32)
    ropec[:, 0] = inv[np.arange(128) % 32] / np.float32(2 * math.pi)
    R = np.zeros((128, 128), np.float32)
    for m in range(128):
        if m % 64 < 32:
            R[m + 32, m] = -1.0
        else:
            R[m - 32, m] = 1.0
    w1 = P["a_cmp_w1"][layer]
    w1d = np.ascontiguousarray(w1.reshape(2, 32, 64, 128).transpose(0, 2, 1, 3))
    w1d = np.concatenate([w1d, w1d], axis=1).reshape(2, 128, 32 * 128)
    peT = np.ascontiguousarray(P["a_cmp_pos"][layer].transpose(0, 2, 1))
    peT = np.concatenate([peT, peT], axis=1)
    return {
        "x_ext": x_ext,
        "prenorm_b": np.ascontiguousarray(np.broadcast_to(P["pre_norm"][layer][None, :], (128, DM))),
        "WA": np.ascontiguousarray(P["w_in"][layer][:, A_COLS]),
        "pos_b": np.ascontiguousarray(np.broadcast_to(P["positions"][0, lo:lo + TPC][None, :], (128, TPC))).astype(np.int32),
        "ropec": ropec, "ropeR": R,
        "w1d": np.ascontiguousarray(w1d), "w2": np.ascontiguousarray(P["a_cmp_w2"][layer]), "peT": np.ascontiguousarray(peT),
        **{k: v for k, v in host_ssd_params(layer, P).items() if k in ("b_cw", "b_cb", "b_par")},
    }


_NC_CACHE = {}


B_FM = ["a_q", "c_q", "d_q", "m_q", "b_x", "b_B", "b_C"]
B_TM = ["a_z", "b_z", "c_z", "d_z", "m_z", "a_gate", "b_dt"]
B_COLS = cols(*B_FM) + cols(*B_TM)
B_NFM = 30
B_TM0 = B_NFM * 128
FMT = {"a_q": 0, "c_q": 4, "d_q": 8, "m_q": 20, "b_x": 22, "b_B": 26, "b_C": 28}
MIXW = 2304
VW = 128


class PB:
    pass


def attn_unit(S, spe, spek, spo, spok, regions, PT, ptk, bias_ap, bias_reads=()):
    R = len(regions)
    half = (R + 1) // 2
    for r, mms in enumerate(regions):
        n = len(mms)
        sp, spk = (spe, spek) if r % 2 == 0 else (spo, spok)
        c0 = (r // 2) * 128
        for j, (lhsT, rhs, rd) in enumerate(mms):
            S.op("pe", lambda e: e.matmul(sp[:, c0:c0 + 128], lhsT=lhsT, rhs=rhs, start=(j == 0), stop=(j == n - 1)),
                 reads=rd, writes=[spk])
    S.op("act", lambda e: e.activation(out=PT[:, 0:half * 128], in_=spe[:, 0:half * 128], func=AF.Exp, bias=bias_ap, scale=1.0),
         reads=[spek] + list(bias_reads), writes=[ptk])
    if R > 1:
        S.op("act", lambda e: e.activation(out=PT[:, half * 128:R * 128], in_=spo[:, 0:(R - half) * 128], func=AF.Exp, bias=bias_ap, scale=1.0),
             reads=[spok] + list(bias_reads), writes=[ptk])


def ptcol(h, R=4):
    return (h % 2) * ((R + 1) // 2) + h // 2


def finalize_T(S, nc, C, po, pok, R, osb, osbk, ptr, ptrk, otm, otmk):
    if po is not None:
        S.op("act", lambda e: e.copy(out=osb[:65, :R * 128], in_=po[:65, :R * 128]), reads=[pok], writes=[osbk])
    for r in range(R):
        S.op("pe", lambda e: e.matmul(ptr[:, r * 128:r * 128 + 65], lhsT=osb[:65, r * 128:(r + 1) * 128], rhs=C["identf"][:65, :65], start=True, stop=True),
             reads=[osbk, "identf"], writes=[ptrk])
    S.op("dve", lambda e: e.tensor_copy(out=otm[:, :R, :], in_=ptr[:, :R * 128].rearrange("p (r d) -> p r d", r=R)[:, :, 0:65]),
         reads=[ptrk], writes=[otmk])


def build_B(debug=False):
    nc = bass.Bass("TRN2", target_bir_lowering=False)
    D = lambda name, shape, dt, kind="ExternalInput": nc.dram_tensor(name, shape, dt, kind=kind).ap()
    g = PB()
    g.nc = nc
    g.x_ext = D("x_ext", [EXT, DM], F32)
    g.prenorm_b = D("prenorm_b", [128, DM], F32)
    g.postnorm_b = D("postnorm_b", [128, DM], F32)
    g.WB = D("WB", [DM, len(B_COLS)], F32)
    g.Wout = D("Wout", [MIXW, DM], F32)
    g.mem = D("mem", [256, DM], F32)
    g.mnorm_b = D("mnorm_b", [128, DM], F32)
    g.Wkv = D("Wkv", [DM, 512], F32)
    g.D = D
    g.pos_b = D("pos_b", [128, TPC], I32)
    g.ropec = D("ropec", [128, 2], F32)
    g.ropeR = D("ropeR", [128, 128], F32)
    g.kval_d = D("kval", [128, 128], F32)
    g.C_kT = D("C_kT", [2, 128, 17 * 128], BF16)
    g.C_v = D("C_v", [2, 128, 17 * VW], BF16)
    g.C_mask = D("C_mask", [2, 128, 128], F32)
    g.sinks_b = D("sinks_b", [128, 8], F32)
    g.A_ksT = D("A_ksT", [2, 128, 128 * 128], BF16)
    g.A_vs = D("A_vs", [2, 128, 128 * VW], BF16)
    g.A_kwT = D("A_kwT", [2, 128, 20 * 128], BF16)
    g.A_vw = D("A_vw", [2, 128, 20 * VW], BF16)
    g.A_kcT = D("A_kcT", [2, 128, 1024], BF16)
    g.A_vc = D("A_vc", [2, 128, 8 * VW], BF16)
    g.A_kvalc = D("A_kvalc", [128, 8], F32)
    g.A_selB = D("A_selB", [104, 128, 128], F32)
    g.A_winB = D("A_winB", [40, 128, 128], F32)
    g.A_cmpB = D("A_cmpB", [16, 16, 128, 128], F32)
    g.A_farc = D("A_farc", [128, 8], F32)
    g.A_farrow = D("A_farrow", [1, 8, 128], F32)
    g.A_wimp = D("A_wimp", [16, 128, 128], F32)
    g.A_f0 = D("A_f0", [128, 256], F32)
    g.A_mkak = D("A_mkak", [128, 4], F32)
    g.A_ewide = D("A_ewide", [128, 8256], BF16)
    g.D_kT = D("D_kT", [4, 128, 4096], BF16)
    g.D_v = D("D_v", [4, 128, 69 * 2 * VW], BF16)
    g.D_bias = D("D_bias", [48, 128, 128], F32)
    g.b_cw = D("b_cw", [128, 8, 4], F32)
    g.b_cb = D("b_cb", [128, 8], F32)
    g.b_par = D("b_par", [128, 3, 8], F32)
    g.b_dskip = D("b_dskip", [128, 512], F32)
    g.b_norm = D("b_norm", [128, 512], F32)
    g.b_Sprev = D("b_Sprev", [128, 7 * 512], F32)
    g.b_Dprev = D("b_Dprev", [128, 56], F32)
    g.x_out = D("x_out", [TPC, DM], F32, "ExternalOutput")
    g.zs = D("zs_scr", [TPC, MIXW], F32, "ExternalOutput")
    g.mix = D("mix_scr", [TPC, MIXW], BF16, "ExternalOutput")

    with ExitStack() as es:
        S = Sync(nc)
        g.S = S
        C = emit_consts(S, nc, es)
        g.C = C
        C["epsc"] = sb(es, nc, "epsc", [128, 1], F32)
        S.op("pool", lambda e: e.memset(C["epsc"][:], EPS), writes=["epsc"])
        C["zeroc"] = sb(es, nc, "zeroc", [128, 1], F32)
        S.op("pool", lambda e: e.memset(C["zeroc"][:], 0.0), writes=["zeroc"])
        g.ps = [es.enter_context(nc.psum_tensor(f"ps{i}", [128, 512], F32)) for i in range(7)]
        g.ps_bf = es.enter_context(nc.psum_tensor("ps_bf", [128, 1024], BF16))
        g.gates = sb(es, nc, "gates", [128, NT, 24], F32)
        g.dtraw = sb(es, nc, "dtraw", [128, NTT, 8], F32)
        g.kval = sb(es, nc, "kval_sb", [128, 128], F32)
        S.dma("sp", g.kval[:], g.kval_d[:, :], writes=["kval"])
        C["onesb"] = sb(es, nc, "onesb", [1, 128], BF16)
        S.op("pool", lambda e: e.memset(C["onesb"][:], 1.0), writes=["onesb"])
        farf = sb(es, nc, "farf", [1, 8, 128], F32)
        g.farrow = sb(es, nc, "farrow", [1, 8, 128], BF16)
        S.dma("sp", farf[:], g.A_farrow[:, :, :], writes=["farf"])
        S.op("dve", lambda e: e.tensor_copy(out=g.farrow[:], in_=farf[:]), reads=["farf"], writes=["farrow"])
        import os
        ph = os.environ.get("PHASES", "ZCMDBAO")
        qa = sb(es, nc, "qa", [128, 4, TPC], BF16) if "A" in ph else None
        with Scope(S, nc) as eh:
            g.hT = sb(eh, nc, "hT", [128, 8, EXT], BF16)
            emit_hT(S, nc, C, g.x_ext, g.prenorm_b, g.hT, g.ps_bf)
            if "Z" in ph:
                phase_Z(g)
            if "C" in ph:
                phase_C(g)
            if "M" in ph:
                phase_M(g)
            if "D" in ph:
                phase_D(g)
            if "B" in ph:
                phase_B(g)
            if "A" in ph:
                with Scope(S, nc) as eq:
                    proj_q(g, eq, "qa", FMT["a_q"], 4, qT=qa)
        if "A" in ph:
            phase_A(g, qa)
        if "O" in ph:
            phase_out(g)
        S.finish("sp")
    return nc


def phase_Z(g):
    S, nc = g.S, g.nc
    with Scope(S, nc) as es:
        wl = WLoader(S, nc, es, g.WB, 512, name="wz")
        zst = [sb(es, nc, f"zst{i}", [128, 512], F32) for i in range(2)]
        pi = 0
        import os
        for cb in [int(c) for c in os.environ.get('ZCB', '01234')]:
            c0 = B_TM0 + cb * 512
            w = 512 if cb < 4 else 256 + 32
            wb, wk = wl.load(c0, w)
            for t in range(NTT):
                if cb < 4 and t == 0:
                    continue
                p, pk = g.ps[pi % 2], f"ps{pi % 2}"
                b = pi % 2
                pi += 1
                mm_tm(S, p, pk, wb, wk, g.hT, t, 0, w)
                if cb == 4:
                    S.op("dve", lambda e: e.tensor_copy(out=g.dtraw[:, t, :], in_=p[:, 280:288]), reads=[pk], writes=["dtraw"])
                    if t == 0:
                        continue
                    S.op("act", lambda e: e.activation(out=g.gates[:, t - 1, :], in_=p[:, 256:280], func=AF.Tanh, scale=0.5), reads=[pk], writes=["gates"])
                    S.op("dve", lambda e: e.tensor_scalar(out=g.gates[:, t - 1, :], in0=g.gates[:, t - 1, :], scalar1=0.5, scalar2=0.5, op0=ALU.mult, op1=ALU.add),
                         reads=["gates"], writes=["gates"])
                wz = 512 if cb < 4 else 256
                S.op("act", lambda e: e.activation(out=zst[b][:, :wz], in_=p[:, :wz], func=AF.Tanh, scale=0.5), reads=[pk], writes=[f"zst{b}"])
                S.op("dve", lambda e: e.tensor_scalar(out=zst[b][:, :wz], in0=zst[b][:, :wz], scalar1=0.5, scalar2=0.5, op0=ALU.mult, op1=ALU.add),
                     reads=[f"zst{b}"], writes=[f"zst{b}"])
                S.op("dve", lambda e: e.tensor_tensor(out=zst[b][:, :wz], in0=p[:, :wz], in1=zst[b][:, :wz], op=ALU.mult),
                     reads=[f"zst{b}", pk], writes=[f"zst{b}"])
                S.dma("sp", g.zs[(t - 1) * 128:t * 128, cb * 512:cb * 512 + wz], zst[b][:, :wz], reads=[f"zst{b}"], writes=["zs_scr"])


def proj_q(g, es, name, tile0, ntiles, scale=0.125, pbase=0, qT=None):
    S, nc = g.S, g.nc
    if qT is None:
        qT = sb(es, nc, name, [128, ntiles, TPC], BF16)
    wl = WLoader(S, nc, es, g.WB, 128, name=name + "w")
    pi = 0
    for ti in range(ntiles):
        wb, wk = wl.load((tile0 + ti) * 128)
        for (t0, w) in tok_groups(128, EXT):
            p, pk = g.ps[pbase + pi % 2], f"ps{pbase + pi % 2}"
            pi += 1
            mm_fm(S, p, pk, wb, wk, g.hT, t0, w)
            S.op("act", lambda e: e.activation(out=qT[:, ti, t0 - 128:t0 - 128 + w], in_=p[:, :w], func=AF.Copy, scale=scale),
                 reads=[pk], writes=[name])
    return qT


def phase_M(g):
    S, nc, C = g.S, g.nc, g.C
    with Scope(S, nc) as es:
        memT = sb(es, nc, "memT", [128, 8, 256], BF16)
        emit_hT_named(S, nc, C, g.mem, g.mnorm_b, memT, "memT", g.ps_bf, ntt=2)
        wl = WLoader(S, nc, es, g.Wkv, 512, name="wkv")
        wb, wk = wl.load(0, 512)
        kmT = sb(es, nc, "kmT", [128, 2, 256], BF16)
        vm = sb(es, nc, "vm", [128, 2, 4, VW], BF16)
        S.op("pool", lambda e: e.memset(vm[:], 1.0), writes=["vm"])
        for ti in range(2):
            p, pk = g.ps[0], "ps0"
            for k in range(8):
                S.op("pe", lambda e: e.matmul(p[:, :256], lhsT=wb[:, k, ti * 128:(ti + 1) * 128], rhs=memT[:, k, :], start=(k == 0), stop=(k == 7)),
                     reads=[wk, "memT"], writes=[pk])
            S.op("act", lambda e: e.copy(out=kmT[:, ti, :], in_=p[:, :256]), reads=[pk], writes=["kmT"])
        for mt in range(2):
            p, pk = g.ps[1], "ps1"
            for k in range(8):
                S.op("pe", lambda e: e.matmul(p[:, :256], lhsT=memT[:, k, mt * 128:(mt + 1) * 128], rhs=wb[:, k, 256:512], start=(k == 0), stop=(k == 7)),
                     reads=[wk, "memT"], writes=[pk])
            S.op("act", lambda e: e.copy(out=vm[:, mt, :, 0:64], in_=p[:, :256].rearrange("p (h d) -> p h d", h=4)), reads=[pk], writes=["vm"])
        import os
        MSTOP = int(os.environ.get("MSTOP", "9"))
        if MSTOP < 1:
            return
        qT = proj_q(g, es, "qm", FMT["m_q"], 2)
        if MSTOP < 2:
            return
        PT = [sb(es, nc, f"PT{i}", [128, 512], BF16) for i in range(2)]
        osb = sb(es, nc, "osb", [128, 512], F32)
        otm = sb(es, nc, "otm", [128, 4, 65], F32)
        rden = sb(es, nc, "rden", [128, 4], F32)
        zt = sb(es, nc, "zt", [128, 256], F32)
        mst = sb(es, nc, "mst", [128, 256], BF16)
        ptr, ptrk = g.ps[0], "ps0"
        ui = 0
        for i in range(int(os.environ.get("MI", NT))):
            for mt in range(2):
                pt, ptk = PT[ui % 2], f"PT{ui % 2}"
                ui += 1
                regions = []
                for h in range(4):
                    r0 = 64 * (h % 2)
                    regions.append([(kmT[r0:r0 + 64, h // 2, mt * 128:(mt + 1) * 128], qT[r0:r0 + 64, h // 2, i * 128:(i + 1) * 128], ["kmT", "qm"])])
                attn_unit(S, g.ps[1], "ps1", g.ps[2], "ps2", regions, pt, ptk, C["zeroc"][:, 0:1], ["zeroc"])
                for h in range(4):
                    pc = ptcol(h)
                    S.op("pe", lambda e: e.matmul(g.ps[3 + h][:, 0:128], lhsT=vm[:, mt, h, :], rhs=pt[:, pc * 128:(pc + 1) * 128],
                                                  start=(mt == 0), stop=(mt == 1)), reads=["vm", ptk], writes=[f"ps{3 + h}"])
            if MSTOP < 3:
                continue
            for h in range(4):
                S.op("act", lambda e: e.copy(out=osb[:65, h * 128:(h + 1) * 128], in_=g.ps[3 + h][:65, 0:128]), reads=[f"ps{3 + h}"], writes=["osb"])
            finalize_T(S, nc, C, None, None, 4, osb, "osb", ptr, ptrk, otm, "otm")
            if MSTOP < 4:
                continue
            S.op("dve", lambda e: e.reciprocal(out=rden[:], in_=otm[:, :, 64]), reads=["otm"], writes=["rden"])
            S.dma("sp", zt[:], g.zs[i * 128:(i + 1) * 128, 2048:2304], reads=["zs_scr"], writes=["zt"])
            for h in range(4):
                S.op("dve", lambda e: e.scalar_tensor_tensor(out=mst[:, h * 64:(h + 1) * 64], in0=otm[:, h, 0:64], scalar=rden[:, h:h + 1],
                                                             in1=zt[:, h * 64:(h + 1) * 64], op0=ALU.mult, op1=ALU.mult),
                     reads=["otm", "rden", "zt"], writes=["mst"])
            S.dma("sp", g.mix[i * 128:(i + 1) * 128, 2048:2304], mst[:], reads=["mst"], writes=["mix_scr"])


def emit_hT_named(S, nc, C, x_ext, prenorm_b, hT, hkey, ps_bf, ntt):
    with Scope(S, nc) as es:
        pn = sb(es, nc, "pn2", [128, DM], F32)
        S.dma("sp", pn[:], prenorm_b[:, :], writes=["pn2"])
        xt = sb(es, nc, "xt2", [128, DM], F32)
        hb = sb(es, nc, "hb2", [128, DM], BF16)
        junk = sb(es, nc, "junk2", [128, DM], F32)
        ss = sb(es, nc, "ss2", [128, 1], F32)
        for t in range(ntt):
            S.dma("sp", xt[:], x_ext[t * 128:(t + 1) * 128, :], writes=["xt2"])
            S.op("pool", lambda e: e.memset(ss[:], 0.0), writes=["ss2"])
            S.op("act", lambda e: e.activation(out=junk[:], in_=xt[:], func=AF.Square, accum_out=ss[:]), reads=["xt2", "ss2"], writes=["junk2", "ss2"])
            S.op("act", lambda e: e.activation(out=ss[:], in_=ss[:], func=AF.Sqrt, bias=C["epsc"][:, 0:1], scale=1.0 / DM), reads=["ss2"], writes=["ss2"])
            S.op("dve", lambda e: e.reciprocal(out=ss[:], in_=ss[:]), reads=["ss2"], writes=["ss2"])
            S.op("dve", lambda e: e.scalar_tensor_tensor(out=hb[:], in0=xt[:], scalar=ss[:, 0:1], in1=pn[:], op0=ALU.mult, op1=ALU.mult),
                 reads=["xt2", "ss2", "pn2"], writes=["hb2"])
            for k in range(8):
                S.op("pe", lambda e: e.transpose(out=ps_bf[:, k * 128:(k + 1) * 128], in_=hb[:, k * 128:(k + 1) * 128], identity=C["identb"][:]),
                     reads=["hb2", "identb"], writes=["ps_bf"])
            S.op("act", lambda e: e.copy(out=hT[:, :, t * 128:(t + 1) * 128], in_=ps_bf[:, :].rearrange("p (k t) -> p k t", k=8)),
                 reads=["ps_bf"], writes=[hkey])


def phase_out(g):
    S, nc, C = g.S, g.nc, g.C
    with Scope(S, nc) as es:
        wo = sb(es, nc, "wo", [128, 18, DM], BF16)
        wof = [sb(es, nc, f"wof{i}", [128, DM], F32) for i in range(2)]
        for k in range(18):
            b = k % 2
            S.dma("sp", wof[b][:], g.Wout[k * 128:(k + 1) * 128, :], writes=[f"wof{b}"])
            S.op("pool", lambda e: e.tensor_copy(out=wo[:, k, :], in_=wof[b][:]), reads=[f"wof{b}"], writes=["wo"])
        pnb = sb(es, nc, "pnb", [128, DM], F32)
        S.dma("sp", pnb[:], g.postnorm_b[:, :], writes=["pnb"])
        mt_ = [sb(es, nc, f"mixt{i}", [128, MIXW], BF16) for i in range(2)]
        mT = [sb(es, nc, f"mixT{i}", [128, 18, 128], BF16) for i in range(2)]
        xt = [sb(es, nc, f"xo{i}", [128, DM], F32) for i in range(2)]
        y = [sb(es, nc, f"yo{i}", [128, DM], F32) for i in range(2)]
        junk = sb(es, nc, "junko", [128, DM], F32)
        ss = [sb(es, nc, f"sso{i}", [128, 1], F32) for i in range(2)]
        for t in range(NT):
            b = t % 2
            S.dma("sp", mt_[b][:], g.mix[t * 128:(t + 1) * 128, :], reads=["mix_scr"], writes=[f"mixt{b}"])
            S.dma("sp", xt[b][:], g.x_ext[(t + 1) * 128:(t + 2) * 128, :], writes=[f"xo{b}"])
            for half in range(3):
                k0 = half * 8
                nk = min(8, 18 - k0)
                for k in range(nk):
                    S.op("pe", lambda e: e.transpose(out=g.ps_bf[:, k * 128:(k + 1) * 128], in_=mt_[b][:, (k0 + k) * 128:(k0 + k + 1) * 128],
                                                     identity=C["identb"][:]), reads=[f"mixt{b}", "identb"], writes=["ps_bf"])
                S.op("act", lambda e: e.copy(out=mT[b][:, k0:k0 + nk, :], in_=g.ps_bf[:, :nk * 128].rearrange("p (k t) -> p k t", k=nk)),
                     reads=["ps_bf"], writes=[f"mixT{b}"])
            for nh in range(2):
                p, pk = g.ps[nh], f"ps{nh}"
                for k in range(18):
                    S.op("pe", lambda e: e.matmul(p[:, :512], lhsT=mT[b][:, k, :], rhs=wo[:, k, nh * 512:(nh + 1) * 512], start=(k == 0), stop=(k == 17)),
                         reads=[f"mixT{b}", "wo"], writes=[pk])
                S.op("act", lambda e: e.copy(out=y[b][:, nh * 512:(nh + 1) * 512], in_=p[:, :512]), reads=[pk], writes=[f"yo{b}"])
            S.op("pool", lambda e: e.memset(ss[b][:], 0.0), writes=[f"sso{b}"])
            S.op("act", lambda e: e.activation(out=junk[:], in_=y[b][:], func=AF.Square, accum_out=ss[b][:]), reads=[f"yo{b}", f"sso{b}"], writes=["junko", f"sso{b}"])
            S.op("act", lambda e: e.activation(out=ss[b][:], in_=ss[b][:], func=AF.Sqrt, bias=C["epsc"][:, 0:1], scale=1.0 / DM), reads=[f"sso{b}"], writes=[f"sso{b}"])
            S.op("dve", lambda e: e.reciprocal(out=ss[b][:], in_=ss[b][:]), reads=[f"sso{b}"], writes=[f"sso{b}"])
            S.op("dve", lambda e: e.scalar_tensor_tensor(out=y[b][:], in0=y[b][:], scalar=ss[b][:, 0:1], in1=pnb[:], op0=ALU.mult, op1=ALU.mult),
                 reads=[f"yo{b}", f"sso{b}", "pnb"], writes=[f"yo{b}"])
            S.op("pool", lambda e: e.tensor_tensor(out=y[b][:], in0=y[b][:], in1=xt[b][:], op=ALU.add), reads=[f"yo{b}", f"xo{b}"], writes=[f"yo{b}"])
            S.dma("sp", g.x_out[t * 128:(t + 1) * 128, :], y[b][:], reads=[f"yo{b}"], writes=[f"x_out{t}"])


def host_B_inputs(c, x_cur, layer, P, Aout):
    lo = c * TPC
    x_ext = np.zeros((EXT, DM), np.float32)
    if c > 0:
        x_ext[:128] = x_cur[lo - 128:lo]
    x_ext[128:] = x_cur[lo:lo + TPC]
    bc = lambda v: np.ascontiguousarray(np.broadcast_to(v[None, :], (128, v.shape[0])))
    return {
        "x_ext": x_ext,
        "prenorm_b": bc(P["pre_norm"][layer]), "postnorm_b": bc(P["post_norm"][layer]),
        "WB": np.ascontiguousarray(P["w_in"][layer][:, B_COLS]),
        "Wout": np.ascontiguousarray(P["w_out"][layer]),
        "mem": np.ascontiguousarray(P["mem"][0]), "mnorm_b": bc(P["m_norm"][layer]),
        "Wkv": np.ascontiguousarray(P["m_w_kv"][layer]),
        **host_B_attn_inputs(c, layer, P, Aout),
        **host_B_A_inputs(c, layer, P, Aout),
        **host_B_D_inputs(c, layer, P, Aout),
        **host_ssd_params(layer, P),
        **host_B_ssd_chain(c, Aout),
    }


def t5_bucket_np(dist):
    dist = np.maximum(dist, 0)
    rel = np.maximum(dist, 16).astype(np.float32)
    large = 16 + (np.log(rel / np.float32(16)) / np.float32(math.log(2048 / 16)) * np.float32(16)).astype(np.int32)
    return np.where(dist < 16, dist, np.minimum(large, 31)).astype(np.int64)


def toeplitz_tile(table_col, dist, allowed):
    out = np.full(dist.shape, NEG, np.float32)
    if table_col is None:
        out[allowed] = 0.0
    else:
        out[allowed] = table_col[t5_bucket_np(dist)][allowed]
    return out


_KI = np.arange(128)[:, None]
_QI = np.arange(128)[None, :]


def gqa_run(g, units, PT, po, pok, first_bufs=None):
    S = g.S
    n = len(units)

    def stage1(un):
        b = g.ui % 2
        g.ui += 1
        spe, spek, spo, spok = g.ps[2 * b], f"ps{2 * b}", g.ps[2 * b + 1], f"ps{2 * b + 1}"
        attn_unit(S, spe, spek, spo, spok, un["regions"], PT[b], f"PT{b}", un["bias"][0], un["bias"][1])
        return b

    nxt = stage1(units[0])
    for u, un in enumerate(units):
        b = nxt
        if u + 1 < n:
            nxt = stage1(units[u + 1])
        pt, ptk = PT[b], f"PT{b}"
        vap, vreads = un["v"]
        S.op("pe", lambda e: e.matmul(po[:, 0:512], lhsT=vap, rhs=pt[:, 0:512], start=(u == 0), stop=(u == n - 1)),
             reads=list(vreads) + [ptk], writes=[pok])


def load_bias_tiles(g, es, name, dram, n):
    S, nc = g.S, g.nc
    t = sb(es, nc, name, [128, n, 128], BF16)
    with Scope(S, nc) as es2:
        st = [sb(es2, nc, f"{name}_st{i}", [128, 8, 128], F32) for i in range(2)]
        for j0 in range(0, n, 8):
            b = (j0 // 8) % 2
            m = min(8, n - j0)
            S.dma("sp", st[b][:, :m, :], dram[j0:j0 + m, :, :].rearrange("n p q -> p n q"), writes=[f"{name}_st{b}"])
            S.op("pool", lambda e: e.tensor_copy(out=t[:, j0:j0 + m, :], in_=st[b][:, :m, :]), reads=[f"{name}_st{b}"], writes=[name])
    return t


def load_bf(g, es, name, dram, shape):
    t = sb(es, g.nc, name, shape, BF16)
    g.S.dma("sp", t[:], dram, writes=[name])
    return t


def head_regions(kT, kkey, kcols, qT, qkey, gi, i, extra=None):
    regions = []
    for h in range(4):
        r0 = 64 * (h % 2)
        mm = [(kT[r0:r0 + 64, kcols], qT[r0:r0 + 64, 2 * gi + h // 2, i * 128:(i + 1) * 128], [kkey, qkey])]
        if extra is not None:
            mm += extra(h)
        regions.append(mm)
    return regions


def phase_C(g):
    S, nc, C = g.S, g.nc, g.C
    D = g.D
    with Scope(S, nc) as es:
        cosT = sb(es, nc, "cosT", [128, TPC], F32)
        sinT = sb(es, nc, "sinT", [128, TPC], F32)
        rc = sb(es, nc, "rc", [128, 2], F32)
        rR = sb(es, nc, "rR", [128, 128], F32)
        S.dma("sp", rc[:], g.ropec[:, :], writes=["rc"])
        S.dma("sp", rR[:], g.ropeR[:, :], writes=["rR"])
        emit_rope_tables(S, nc, g.pos_b, rc, cosT, sinT)
        qT = sb(es, nc, "qc", [128, 4, TPC], BF16)
        wl = WLoader(S, nc, es, g.WB, 128, name="qcw")
        xf = sb(es, nc, "xf", [128, 512], F32)
        pi = 0
        for ti in range(4):
            wb, wk = wl.load((FMT["c_q"] + ti) * 128)
            for (t0, w) in tok_groups(128, EXT):
                p, pk = g.ps[pi % 2], f"ps{pi % 2}"
                pi += 1
                mm_fm(S, p, pk, wb, wk, g.hT, t0, w)
                l0 = t0 - 128
                emit_rope_apply(S, nc, p, pk, w, xf, rR, cosT, sinT, l0, g.ps[2], "ps2", qT[:, ti, l0:l0 + w], "qc", scale=1.0)
        S.op("pool", lambda e: e.tensor_scalar(out=qT[:], in0=qT[:], scalar1=0.125, scalar2=None, op0=ALU.mult), reads=["qc"], writes=["qc"])
        kT = [load_bf(g, es, f"kcT{gi}", g.C_kT[gi, :, :], [128, 17 * 128]) for gi in range(2)]
        vv = [load_bf(g, es, f"vcc{gi}", g.C_v[gi, :, :], [128, 17 * VW]) for gi in range(2)]
        cm = load_bias_tiles(g, es, "cmask", g.C_mask, 2)
        sinkf = sb(es, nc, "sinkf", [128, 8], F32)
        S.dma("sp", sinkf[:], g.sinks_b[:, :], writes=["sinkf"])
        S.op("act", lambda e: e.activation(out=sinkf[:], in_=sinkf[:], func=AF.Exp), reads=["sinkf"], writes=["sinkf"])
        PT = [sb(es, nc, f"PT{i}", [128, 512], BF16) for i in range(2)]
        osb = sb(es, nc, "osb", [128, 512], F32)
        otm = sb(es, nc, "otm", [128, 4, 65], F32)
        rden = sb(es, nc, "rden", [128, 4], F32)
        zt = sb(es, nc, "zt", [128, 512], F32)
        mst = sb(es, nc, "mst", [128, 512], BF16)
        po, pok, ptr, ptrk = g.ps[4], "ps4", g.ps[5], "ps5"
        g.ui = 0
        for i in range(NT):
            S.dma("sp", zt[:], g.zs[i * 128:(i + 1) * 128, 1024:1536], reads=["zs_scr"], writes=["zt"])
            for gi in range(2):
                units = []
                for w in (1, 0):
                    T = 1 + i - w
                    ext = lambda h, w=w: [(C["identb"][:, :], cm[:, w, :], ["identb", "cmask"])]
                    units.append(dict(regions=head_regions(kT[gi], f"kcT{gi}", slice(T * 128, (T + 1) * 128), qT, "qc", gi, i, ext),
                                      bias=(g.kval[:, 111 + T:112 + T], ["kval"]),
                                      v=(vv[gi][:, T * VW:(T + 1) * VW], [f"vcc{gi}"])))
                gqa_run(g, units, PT, po, pok)
                finalize_T(S, nc, C, po, pok, 4, osb, "osb", ptr, ptrk, otm, "otm")
                for h in range(4):
                    pc = ptcol(h)
                    hh = 4 * gi + h
                    S.op("dve", lambda e: e.tensor_tensor(out=rden[:, h:h + 1], in0=otm[:, pc, 64:65], in1=sinkf[:, hh:hh + 1], op=ALU.add),
                         reads=["otm", "sinkf"], writes=["rden"])
                S.op("dve", lambda e: e.reciprocal(out=rden[:], in_=rden[:]), reads=["rden"], writes=["rden"])
                for h in range(4):
                    pc = ptcol(h)
                    hh = 4 * gi + h
                    S.op("dve", lambda e: e.scalar_tensor_tensor(out=mst[:, hh * 64:(hh + 1) * 64], in0=otm[:, pc, 0:64], scalar=rden[:, h:h + 1],
                                                                 in1=zt[:, hh * 64:(hh + 1) * 64], op0=ALU.mult, op1=ALU.mult),
                         reads=["otm", "rden", "zt"], writes=["mst"])
            S.dma("sp", g.mix[i * 128:(i + 1) * 128, 1024:1536], mst[:], reads=["mst"], writes=["mix_scr"])


def gather_A(res):
    G = {}
    G["fm"] = np.concatenate([np.asarray(r["o_fm"]) for r in res], axis=2)
    G["tm"] = np.concatenate([np.asarray(r["o_tm"]) for r in res], axis=0)
    G["kcT"] = np.concatenate([np.asarray(r["o_kcT"]) for r in res], axis=1)
    G["vc"] = np.concatenate([np.asarray(r["o_vc"]) for r in res], axis=0)
    G["S"] = [np.asarray(r["o_S"]) for r in res]
    G["D"] = [np.asarray(r["o_D"]) for r in res]
    return G


def rope_consts():
    inv = (150000.0 ** (-np.arange(32, dtype=np.float32) / 32)).astype(np.float32)
    ropec = np.zeros((128, 2), np.float32)
    ropec[:, 0] = inv[np.arange(128) % 32] / np.float32(2 * math.pi)
    R = np.zeros((128, 128), np.float32)
    for m in range(128):
        if m % 64 < 32:
            R[m + 32, m] = -1.0
        else:
            R[m - 32, m] = 1.0
    return ropec, R


def kT_window(G, fm_idx, gi, c, first_slot, dup=True):
    n = 128 - first_slot
    out = np.zeros((128, n * 128), NPBF)
    for s in range(n):
        Tg = first_slot + s - 16 * (7 - c)
        if Tg < 0:
            continue
        blk = G["fm"][fm_idx][:, Tg * 128:(Tg + 1) * 128]
        if dup:
            out[0:64, s * 128:(s + 1) * 128] = blk[64 * gi:64 * gi + 64]
            out[64:128, s * 128:(s + 1) * 128] = blk[64 * gi:64 * gi + 64]
        else:
            out[:, s * 128:(s + 1) * 128] = blk
    return out


def v_window(G, col0, c, first_slot):
    n = 128 - first_slot
    out = np.zeros((128, n, VW), NPBF)
    out[:, :, 64] = 1.0
    for s in range(n):
        Tg = first_slot + s - 16 * (7 - c)
        if Tg < 0:
            continue
        out[:, s, 0:64] = G["tm"][Tg * 128:(Tg + 1) * 128, col0:col0 + 64]
    return out


def host_B_attn_inputs(c, layer, P, G):
    lo = c * TPC
    ropec, R = rope_consts()
    d = {}
    d["pos_b"] = np.ascontiguousarray(np.broadcast_to(P["positions"][0, lo:lo + TPC][None, :], (128, TPC))).astype(np.int32)
    d["ropec"], d["ropeR"] = ropec, R
    kval = np.zeros((128, 128), np.float32)
    kval[:, :16 * (7 - c)] = NEG
    d["kval"] = kval
    d["C_kT"] = np.stack([kT_window(G, 2, gi, c, 111) for gi in range(2)])
    d["C_v"] = np.stack([v_window(G, 256 + 64 * gi, c, 111).reshape(128, -1) for gi in range(2)])
    cm = np.stack([toeplitz_tile(None, _QI - _KI, (_QI - _KI) >= 0), toeplitz_tile(None, 128 + _QI - _KI, (128 + _QI - _KI) <= 127)])
    d["C_mask"] = cm
    d["sinks_b"] = np.ascontiguousarray(np.broadcast_to(P["c_sinks"][layer][None, :], (128, 8)))
    return d


def phase_A(g, qa):
    S, nc, C = g.S, g.nc, g.C
    with Scope(S, nc) as es:
        ew = load_bf(g, es, "ewide", g.A_ewide[:, :], [128, 8256])
        wimp = load_bias_tiles(g, es, "wimp", g.A_wimp, 16)
        farc = sb(es, nc, "farc", [128, 8], F32)
        S.dma("sp", farc[:], g.A_farc[:, :], writes=["farc"])
        f0 = sb(es, nc, "f0", [128, 256], F32)
        S.dma("sp", f0[:], g.A_f0[:, :], writes=["f0"])
        mkak = sb(es, nc, "mkak", [128, 4], F32)
        S.dma("sp", mkak[:], g.A_mkak[:, :], writes=["mkak"])
        kvalc = sb(es, nc, "kvalc", [128, 8], F32)
        S.dma("sp", kvalc[:], g.A_kvalc[:, :], writes=["kvalc"])
        PT = [sb(es, nc, f"PT{i}", [128, 512], BF16) for i in range(2)]
        PTc = sb(es, nc, "PTc", [128, 8, 512], BF16)
        osb = sb(es, nc, "osb", [128, 512], F32)
        otm = [sb(es, nc, f"otm{b}", [128, 4, 65], F32) for b in range(3)]
        rden = sb(es, nc, "rden", [128, 3, 4], F32)
        acc = sb(es, nc, "acc_a", [128, 64], F32)
        imp = sb(es, nc, "imp", [128, 256], F32)
        imp2 = sb(es, nc, "imp2", [128, 256], F32)
        mx = sb(es, nc, "mx", [128, 16], F32)
        negm = sb(es, nc, "negm", [128, 256], F32)
        nmn = sb(es, nc, "nmn", [128, 2, 128], BF16)
        nmf = sb(es, nc, "nmf", [128, 2, 4, 128], BF16)
        cst = sb(es, nc, "cst", [128, 8, 128], F32)
        cbt = sb(es, nc, "cbt", [128, 8, 128], BF16)
        zt = sb(es, nc, "zt", [128, 512], F32)
        mst = sb(es, nc, "mst", [128, 512], BF16)
        po, pok, ptr, ptrk, pu, puk = g.ps[4], "ps4", g.ps[5], "ps5", g.ps[6], "ps6"
        g.ui = 0
        for gi in range(2):
            with Scope(S, nc) as eg:
                ksT = load_bf(g, eg, f"ksT{gi}", g.A_ksT[gi, :, :], [128, 128 * 128])
                vs = load_bf(g, eg, f"vs{gi}", g.A_vs[gi, :, :], [128, 128 * VW])
                kwT = load_bf(g, eg, f"kwT{gi}", g.A_kwT[gi, :, :], [128, 20 * 128])
                vw = load_bf(g, eg, f"vw{gi}", g.A_vw[gi, :, :], [128, 20 * VW])
                kcT = load_bf(g, eg, f"kcT{gi}", g.A_kcT[gi, :, :], [128, 1024])
                vc = load_bf(g, eg, f"vc{gi}", g.A_vc[gi, :, :], [128, 8 * VW])
                selB = load_bias_tiles(g, eg, f"selB{gi}", g.A_selB[gi * 52:(gi + 1) * 52, :, :], 52)
                winB = load_bias_tiles(g, eg, f"winB{gi}", g.A_winB[gi * 20:(gi + 1) * 20, :, :], 20)
                for i in range(int(os.environ.get("AI", NT))):
                    S.dma("sp", cst[:], g.A_cmpB[i, gi * 8:(gi + 1) * 8, :, :].rearrange("n p q -> p n q"), writes=["cst"])
                    S.op("pool", lambda e: e.tensor_copy(out=cbt[:], in_=cst[:]), reads=["cst"], writes=["cbt"])
                    for m in range(8):
                        b = g.ui % 2
                        g.ui += 1
                        if m >= 6:
                            ext = lambda h, m=m: [(C["identb"][:, :], cbt[:, h * 2 + (m - 6), :], ["identb", "cbt"])]
                        else:
                            ext = lambda h: [(C["onesb"][0:1, :], g.farrow[0:1, gi * 4 + h, :], ["onesb", "farrow"])]
                        regions = head_regions(kcT, f"kcT{gi}", slice(m * 128, (m + 1) * 128), qa, "qa", gi, i, ext)
                        attn_unit(S, g.ps[2 * b], f"ps{2 * b}", g.ps[2 * b + 1], f"ps{2 * b + 1}", regions, PTc[:, m, :], "PTc",
                                  kvalc[:, m:m + 1], ["kvalc"])
                        S.op("pe", lambda e: e.matmul(po[:, 0:512], lhsT=vc[:, m * VW:(m + 1) * VW], rhs=PTc[:, m, :], start=(m == 0), stop=(m == 7)),
                             reads=[f"vc{gi}", "PTc"], writes=[pok])
                    finalize_T(S, nc, C, po, pok, 4, osb, "osb", ptr, ptrk, otm[0], "otm0")
                    S.op("dve", lambda e: e.tensor_scalar(out=rden[:, 0, :], in0=otm[0][:, :, 64], scalar1=1e-30, scalar2=None, op0=ALU.max), reads=["otm0"], writes=["rden"])
                    S.op("dve", lambda e: e.reciprocal(out=rden[:, 0, :], in_=rden[:, 0, :]), reads=["rden"], writes=["rden"])
                    for pair in range(2):
                        for hp in range(2):
                            pc = pair * 2 + hp
                            for m in range(8):
                                S.op("pe", lambda e: e.matmul(pu[:, hp * 256:(hp + 1) * 256], lhsT=PTc[:, m, pc * 128:(pc + 1) * 128],
                                                              rhs=wimp[:, 2 * m:2 * m + 2, :].rearrange("p a b -> p (a b)"), start=(m == 0), stop=(m == 7)),
                                     reads=["PTc", "wimp"], writes=[puk])
                        for hp in range(2):
                            pc = pair * 2 + hp
                            if pc == 0:
                                S.op("dve", lambda e: e.scalar_tensor_tensor(out=imp[:], in0=pu[:, 0:256], scalar=rden[:, 0, 0:1], in1=f0[:],
                                                                             op0=ALU.mult, op1=ALU.add), reads=[puk, "rden", "f0"], writes=["imp"])
                            else:
                                S.op("dve", lambda e: e.scalar_tensor_tensor(out=imp[:], in0=pu[:, hp * 256:(hp + 1) * 256], scalar=rden[:, 0, pc:pc + 1],
                                                                             in1=imp[:], op0=ALU.mult, op1=ALU.add), reads=[puk, "rden", "imp"], writes=["imp"])
                    c0 = 224 + 2 * i
                    S.op("dve", lambda e: e.tensor_tensor(out=imp[:, c0:c0 + 2], in0=imp[:, c0:c0 + 2], in1=mkak[:, 0:2], op=ALU.mult),
                         reads=["imp", "mkak"], writes=["imp"])
                    S.op("dve", lambda e: e.tensor_tensor(out=imp[:, c0:c0 + 2], in0=imp[:, c0:c0 + 2], in1=mkak[:, 2:4], op=ALU.add),
                         reads=["imp", "mkak"], writes=["imp"])
                    if c0 + 2 < 256:
                        S.op("dve", lambda e: e.memset(imp[:, c0 + 2:256], -1.0), reads=["imp"], writes=["imp"])
                    S.op("dve", lambda e: e.max(out=mx[:, 0:8], in_=imp[:]), reads=["imp"], writes=["mx"])
                    S.op("dve", lambda e: e.match_replace(out=imp2[:], in_to_replace=mx[:, 0:8], in_values=imp[:], imm_value=-1e9),
                         reads=["imp", "mx"], writes=["imp2"])
                    S.op("dve", lambda e: e.max(out=mx[:, 8:16], in_=imp2[:]), reads=["imp2"], writes=["mx"])
                    S.op("dve", lambda e: e.tensor_scalar(out=negm[:], in0=imp[:], scalar1=mx[:, 15:16], scalar2=1.0, op0=ALU.is_ge, op1=ALU.subtract),
                         reads=["imp", "mx"], writes=["negm"])
                    for half in range(2):
                        S.op("pe", lambda e: e.matmul(pu[:, half * 128:(half + 1) * 128], lhsT=negm[:, half * 128:(half + 1) * 128], rhs=C["identf"][:, :],
                                                      start=True, stop=True), reads=["negm", "identf"], writes=[puk])
                    S.op("act", lambda e: e.activation(out=nmn[:].rearrange("p a b -> p (a b)"), in_=pu[:, 0:256], func=AF.Copy, scale=-NEG),
                         reads=[puk], writes=["nmn"])
                    for h in range(4):
                        pc = ptcol(h)
                        S.op("act", lambda e: e.activation(out=nmf[:, :, h, :], in_=pu[:, 0:256].rearrange("p (a b) -> p a b", a=2), func=AF.Identity,
                                                           bias=farc[:, gi * 4 + h:gi * 4 + h + 1], scale=-NEG), reads=[puk, "farc"], writes=["nmf"])
                    units = []
                    nkt = 113 + i
                    for kt in range(int(os.environ.get("KT0", 0)), nkt):
                        delta = 112 + i - kt
                        half, r = (2 * kt) // 128, (2 * kt) % 128
                        if delta <= 12:
                            ext = lambda h, delta=delta, half=half, r=r: [(ew[:, 64 * r:64 * r + 128], nmn[:, half, :], ["ewide", "nmn"]),
                                                                           (C["identb"][:, :], selB[:, h * 13 + delta, :], ["identb", f"selB{gi}"])]
                        else:
                            ext = lambda h, half=half, r=r: [(ew[:, 64 * r:64 * r + 128], nmf[:, half, h, :], ["ewide", "nmf"])]
                        units.append(dict(regions=head_regions(ksT, f"ksT{gi}", slice(kt * 128, (kt + 1) * 128), qa, "qa", gi, i, ext),
                                          bias=(g.kval[:, kt:kt + 1], ["kval"]), v=(vs[:, kt * VW:(kt + 1) * VW], [f"vs{gi}"])))
                    gqa_run(g, units, PT, po, pok)
                    finalize_T(S, nc, C, po, pok, 4, osb, "osb", ptr, ptrk, otm[1], "otm1")
                    S.op("dve", lambda e: e.tensor_scalar(out=rden[:, 1, :], in0=otm[1][:, :, 64], scalar1=1e-30, scalar2=None, op0=ALU.max), reads=["otm1"], writes=["rden"])
                    S.op("dve", lambda e: e.reciprocal(out=rden[:, 1, :], in_=rden[:, 1, :]), reads=["rden"], writes=["rden"])
                    units = []
                    for w in (4, 3, 2, 1, 0):
                        T = 4 + i - w
                        ext = lambda h, w=w: [(C["identb"][:, :], winB[:, h * 5 + w, :], ["identb", f"winB{gi}"])]
                        units.append(dict(regions=head_regions(kwT, f"kwT{gi}", slice(T * 128, (T + 1) * 128), qa, "qa", gi, i, ext),
                                          bias=(g.kval[:, 108 + T:109 + T], ["kval"]), v=(vw[:, T * VW:(T + 1) * VW], [f"vw{gi}"])))
                    gqa_run(g, units, PT, po, pok)
                    finalize_T(S, nc, C, po, pok, 4, osb, "osb", ptr, ptrk, otm[2], "otm2")
                    S.op("dve", lambda e: e.tensor_scalar(out=rden[:, 2, :], in0=otm[2][:, :, 64], scalar1=1e-30, scalar2=None, op0=ALU.max), reads=["otm2"], writes=["rden"])
                    S.op("dve", lambda e: e.reciprocal(out=rden[:, 2, :], in_=rden[:, 2, :]), reads=["rden"], writes=["rden"])
                    S.dma("sp", zt[:, 0:256], g.zs[i * 128:(i + 1) * 128, gi * 256:(gi + 1) * 256], reads=["zs_scr"], writes=["zt"])
                    for h in range(4):
                        pc = ptcol(h)
                        hh = 4 * gi + h
                        for br in range(3):
                            S.op("dve", lambda e: e.tensor_tensor(out=rden[:, br, pc:pc + 1], in0=rden[:, br, pc:pc + 1],
                                                                  in1=g.gates[:, i, hh * 3 + br:hh * 3 + br + 1], op=ALU.mult),
                                 reads=["rden", "gates"], writes=["rden"])
                        S.op("dve", lambda e: e.tensor_scalar(out=acc[:], in0=otm[0][:, pc, 0:64], scalar1=rden[:, 0, pc:pc + 1], scalar2=None, op0=ALU.mult),
                             reads=["otm0", "rden"], writes=["acc_a"])
                        for br in (1, 2):
                            S.op("dve", lambda e: e.scalar_tensor_tensor(out=acc[:], in0=otm[br][:, pc, 0:64], scalar=rden[:, br, pc:pc + 1], in1=acc[:],
                                                                         op0=ALU.mult, op1=ALU.add), reads=[f"otm{br}", "rden", "acc_a"], writes=["acc_a"])
                        S.op("dve", lambda e: e.tensor_tensor(out=mst[:, h * 64:(h + 1) * 64], in0=acc[:], in1=zt[:, h * 64:(h + 1) * 64], op=ALU.mult),
                             reads=["acc_a", "zt"], writes=["mst"])
                    S.dma("sp", g.mix[i * 128:(i + 1) * 128, gi * 256:(gi + 1) * 256], mst[:, 0:256], reads=["mst"], writes=["mix_scr"])


_HOST_CACHE = {}


def _host_A_static(rel_a):
    d = {}
    selB = np.zeros((8, 13, 128, 128), np.float32)
    winB = np.zeros((8, 5, 128, 128), np.float32)
    for h in range(8):
        for dl in range(13):
            dist = 128 * dl + _QI - _KI
            selB[h, dl] = toeplitz_tile(rel_a[:, h], dist, dist >= 0)
        for w in range(5):
            dist = 128 * w + _QI - _KI
            winB[h, w] = toeplitz_tile(rel_a[:, h], dist, (dist >= 0) & (dist < 512))
    d["A_selB"] = selB.reshape(104, 128, 128)
    d["A_winB"] = winB.reshape(40, 128, 128)
    cmpB = np.zeros((16, 8, 2, 128, 128), np.float32)
    for i in range(16):
        for mi, m in enumerate((6, 7)):
            dist = 14336 + 128 * i + _QI - 2048 * m - 16 * _KI - 15
            for h in range(8):
                cmpB[i, h, mi] = toeplitz_tile(rel_a[:, h], dist, dist >= 0)
    d["A_cmpB"] = cmpB.reshape(16, 16, 128, 128)
    d["A_farc"] = np.ascontiguousarray(np.broadcast_to(rel_a[31][None, :], (128, 8)))
    d["A_farrow"] = np.ascontiguousarray(np.broadcast_to(rel_a[31][None, :, None], (1, 8, 128))).astype(np.float32)
    wimp = np.zeros((1024, 256), np.float32)
    wt = {-1: 1.0, 0: 2.0, 1: 2.0, 2: 2.0, 3: 1.0}
    for p_ in range(1024):
        j = p_ - 1
        for b in range(256):
            k = j - 4 * b
            if k in wt:
                wimp[p_, b] = wt[k]
    d["A_wimp"] = np.ascontiguousarray(wimp.reshape(8, 128, 2, 128).transpose(0, 2, 1, 3).reshape(16, 128, 128))
    return d


def host_B_A_inputs(c, layer, P, G):
    rel_a = P["rel_bias"][:, :8]
    d = {}
    if "A_static" not in _HOST_CACHE:
        _HOST_CACHE["A_static"] = _host_A_static(rel_a)
    d.update(_HOST_CACHE["A_static"])
    d["A_ksT"] = np.stack([kT_window(G, 0, gi, c, 0) for gi in range(2)])
    d["A_vs"] = np.stack([v_window(G, 0 + 64 * gi, c, 0).reshape(128, -1) for gi in range(2)])
    d["A_kwT"] = np.stack([kT_window(G, 1, gi, c, 108) for gi in range(2)])
    d["A_vw"] = np.stack([v_window(G, 128 + 64 * gi, c, 108).reshape(128, -1) for gi in range(2)])
    sh = 128 * (7 - c)
    kc = np.zeros((2, 128, 1024), NPBF)
    vc = np.zeros((2, 128, 8, VW), NPBF)
    vc[:, :, :, 64] = 1.0
    n = 1024 - sh
    for gi in range(2):
        kc[gi, 0:64, sh:] = G["kcT"][64 * gi:64 * gi + 64, :n]
        kc[gi, 64:128, sh:] = G["kcT"][64 * gi:64 * gi + 64, :n]
        vfull = np.zeros((1024, 64), NPBF)
        vfull[sh:] = G["vc"][:n, 64 * gi:64 * gi + 64]
        vc[gi, :, :, 0:64] = vfull.reshape(8, 128, 64).transpose(1, 0, 2)
    d["A_kcT"] = kc
    d["A_vc"] = vc.reshape(2, 128, -1)
    kvalc = np.zeros((128, 8), np.float32)
    pos = (np.arange(8)[None, :] * 128 + np.arange(128)[:, None])
    kvalc[pos <= sh] = NEG
    d["A_kvalc"] = kvalc
    f0 = np.zeros((128, 256), np.float32)
    f0[:, 32 * (7 - c)] = 1e4
    d["A_f0"] = f0
    mkak = np.zeros((128, 4), np.float32)
    mkak[64:, 0] = 1.0
    mkak[:64, 2] = 1e4
    mkak[:64, 3] = -1.0
    mkak[64:, 3] = 1e4
    d["A_mkak"] = mkak
    ewide = np.zeros((128, 8256), NPBF)
    xx = np.arange(8256)
    for b in range(128):
        ewide[b, (xx // 64) == b] = 1.0
    d["A_ewide"] = ewide
    return d


def phase_D(g):
    S, nc, C = g.S, g.nc, g.C
    with Scope(S, nc) as es:
        DB = load_bias_tiles(g, es, "DB", g.D_bias, 48)
        PT = [sb(es, nc, f"PT{i}", [128, 256], BF16) for i in range(2)]
        OT = sb(es, nc, "OT", [128, 2, TPC], F32)
        osb = sb(es, nc, "osb", [128, 128], F32)
        otm = sb(es, nc, "otm", [128, 65], F32)
        rden = sb(es, nc, "rden", [128, 1], F32)
        zt = sb(es, nc, "zt", [128, 128], F32)
        mst = sb(es, nc, "mst", [128, 128], BF16)
        ptr, ptrk = g.ps[6], "ps6"
        g.ui = 0
        for t in range(4):
            with Scope(S, nc) as et:
                qd = sb(et, nc, f"qd{t}", [128, 3, TPC], BF16)
                wl = WLoader(S, nc, et, g.WB, 128, name=f"qdw{t}")
                pi = 0
                for p_ in range(3):
                    wb, wk = wl.load((FMT["d_q"] + p_ * 4 + t) * 128)
                    for (t0, w) in tok_groups(128, EXT):
                        pp, pk = g.ps[pi % 2], f"ps{pi % 2}"
                        pi += 1
                        mm_fm(S, pp, pk, wb, wk, g.hT, t0, w)
                        S.op("act", lambda e: e.activation(out=qd[:, p_, t0 - 128:t0 - 128 + w], in_=pp[:, :w], func=AF.Copy, scale=0.125),
                             reads=[pk], writes=[f"qd{t}"])
                kd = load_bf(g, et, f"kd{t}", g.D_kT[t, :, :], [128, 4096])
                vd = load_bf(g, et, f"vd{t}", g.D_v[t, :, :], [128, 69 * 2 * VW])

                def run_set(p_, qsl, keys, first):
                    n = len(keys)
                    for u, (ksl, vt, w, kvc) in enumerate(keys):
                        b = g.ui % 2
                        g.ui += 1
                        regions = []
                        for h in range(2):
                            r0 = 64 * h
                            regions.append([(kd[r0:r0 + 64, ksl], qd[r0:r0 + 64, p_, qsl], [f"kd{t}", f"qd{t}"]),
                                            (C["identb"][:, :], DB[:, (p_ * 8 + 2 * t + h) * 2 + w, :], ["identb", "DB"])])
                        attn_unit(S, g.ps[2 * b], f"ps{2 * b}", g.ps[2 * b + 1], f"ps{2 * b + 1}", regions, PT[b], f"PT{b}",
                                  g.kval[:, kvc:kvc + 1], ["kval"])
                        for h in range(2):
                            v0 = (vt * 2 + h) * VW
                            S.op("pe", lambda e: e.matmul(g.ps[4 + h][:, 0:128], lhsT=vd[:, v0:v0 + VW], rhs=PT[b][:, h * 128:(h + 1) * 128],
                                                          start=(u == 0), stop=(u == n - 1)), reads=[f"vd{t}", f"PT{b}"], writes=[f"ps{4 + h}"])
                    for h in range(2):
                        if first:
                            S.op("act", lambda e: e.copy(out=OT[:, h, qsl], in_=g.ps[4 + h][:, 0:128]), reads=[f"ps{4 + h}"], writes=["OT"])
                        else:
                            S.op("dve", lambda e: e.tensor_tensor(out=OT[:, h, qsl], in0=g.ps[4 + h][:, 0:128], in1=OT[:, h, qsl], op=ALU.add),
                                 reads=[f"ps{4 + h}", "OT"], writes=["OT"])

                for i in range(NT):
                    keys = [(slice((16 + i - w) * 128, (17 + i - w) * 128), 1 + i - w, w, 112 + i - w) for w in (1, 0)]
                    run_set(0, slice(i * 128, (i + 1) * 128), keys, True)
                for U in range(4):
                    for rho in range(4):
                        q0 = 512 * U + rho
                        keys = []
                        for w in (1, 0):
                            k0 = 2048 + 512 * (U - w) + rho
                            keys.append((slice(k0, k0 + 4 * 127 + 1, 4), 17 + (U - w + 1) * 4 + rho, w, 112 if U - w >= 0 else 111))
                        run_set(1, slice(q0, q0 + 4 * 127 + 1, 4), keys, False)
                for r in range(16):
                    keys = []
                    for w in (1, 0):
                        k0 = 2048 * (1 - w) + r
                        keys.append((slice(k0, k0 + 16 * 127 + 1, 16), 37 + (1 - w) * 16 + r, w, 112 if w == 0 else 111))
                    run_set(2, slice(r, r + 16 * 127 + 1, 16), keys, False)
                for i in range(NT):
                    S.dma("sp", zt[:], g.zs[i * 128:(i + 1) * 128, 1536 + t * 128:1536 + (t + 1) * 128], reads=["zs_scr"], writes=["zt"])
                    for h in range(2):
                        S.op("pe", lambda e: e.matmul(ptr[:, 0:65], lhsT=OT[:65, h, i * 128:(i + 1) * 128], rhs=C["identf"][:65, :65], start=True, stop=True),
                             reads=["OT", "identf"], writes=[ptrk])
                        S.op("dve", lambda e: e.tensor_copy(out=otm[:], in_=ptr[:, 0:65]), reads=[ptrk], writes=["otm"])
                        S.op("dve", lambda e: e.reciprocal(out=rden[:], in_=otm[:, 64:65]), reads=["otm"], writes=["rden"])
                        S.op("dve", lambda e: e.scalar_tensor_tensor(out=mst[:, h * 64:(h + 1) * 64], in0=otm[:, 0:64], scalar=rden[:, 0:1],
                                                                     in1=zt[:, h * 64:(h + 1) * 64], op0=ALU.mult, op1=ALU.mult),
                             reads=["otm", "rden", "zt"], writes=["mst"])
                    S.dma("sp", g.mix[i * 128:(i + 1) * 128, 1536 + t * 128:1536 + (t + 1) * 128], mst[:], reads=["mst"], writes=["mix_scr"])


def host_B_D_inputs(c, layer, P, G):
    rel_d = P["rel_bias"][:, 8:]
    d = {}
    if "D_bias" in _HOST_CACHE:
        d["D_bias"] = _HOST_CACHE["D_bias"]
    d["D_kT"] = np.stack([kT_window(G, 3 + t, None, c, 96, dup=False) for t in range(4)])
    base = c * TPC
    dv = G["tm"][:, 384:896]

    def vtile(tok):
        out = np.zeros((128, 8, VW), NPBF)
        out[:, :, 64] = 1.0
        ok = tok >= 0
        if ok.any():
            out[ok, :, 0:64] = dv[tok[ok]].reshape(-1, 8, 64)
        return out

    tiles = []
    for T in range(111, 128):
        tiles.append(vtile(base + (T - 112) * 128 + np.arange(128)))
    for U in range(-1, 4):
        for rho in range(4):
            tiles.append(vtile(base + 512 * U + rho + 4 * np.arange(128)))
    for cs in range(2):
        for r in range(16):
            tiles.append(vtile(base + (cs - 1) * 2048 + r + 16 * np.arange(128)))
    V = np.stack(tiles, axis=1)
    d["D_v"] = np.stack([np.ascontiguousarray(V[:, :, 2 * t:2 * t + 2, :]).reshape(128, -1) for t in range(4)])
    if "D_bias" in d:
        return d
    DB = np.zeros((3, 8, 2, 128, 128), np.float32)
    for p_, dil in enumerate((1, 4, 16)):
        for s_ in range(8):
            for w in range(2):
                dist = 128 * w + _QI - _KI
                DB[p_, s_, w] = toeplitz_tile(rel_d[:, p_ * 8 + s_], dist * dil, (dist >= 0) & (dist <= 128))
    d["D_bias"] = _HOST_CACHE["D_bias"] = DB.reshape(48, 128, 128)
    return d


def emit_ssd(g, es, W, fm_tile0, dtraw, full, Hin=None, out_S=None, out_D=None):
    S, nc, C = g.S, g.nc, g.C
    triu = sb(es, nc, "triu", [128, 128], F32)
    ones = sb(es, nc, "onesf", [128, 128], F32)
    S.op("pool", lambda e: e.memset(ones[:], 1.0), writes=["onesf"])
    S.op("pool", lambda e: e.memset(triu[:], 1.0), writes=["triu"])
    S.op("pool", lambda e: e.affine_select(out=triu[:], in_=triu[:], pattern=[[1, 128]], compare_op=ALU.is_ge, fill=0.0, base=0,
                                            channel_multiplier=-1), reads=["triu"], writes=["triu"])
    cw = sb(es, nc, "cw", [128, 8, 4], F32)
    cbias = sb(es, nc, "cbias", [128, 8], F32)
    S.dma("sp", cw[:], g.b_cw[:, :, :], writes=["cw"])
    S.dma("sp", cbias[:], g.b_cb[:, :], writes=["cbias"])
    par = sb(es, nc, "bpar", [128, 3, 8], F32)
    S.dma("sp", par[:], g.b_par[:, :, :], writes=["bpar"])
    xsT = sb(es, nc, "xsT", [128, 4, TPC], F32)
    BT = sb(es, nc, "BT", [128, 2, TPC], BF16)
    CT = sb(es, nc, "CT", [128, 2, TPC], BF16)
    with Scope(S, nc) as e1:
        wl = WLoader(S, nc, e1, W, 128, name="wssd")
        raw = sb(e1, nc, "craw", [128, EXT], F32)
        acc = sb(e1, nc, "cacc", [128, TPC], F32)
        tmp = sb(e1, nc, "ctmp", [128, TPC], F32)
        pi = 0
        for ti in range(8):
            wb, wk = wl.load((fm_tile0 + ti) * 128)
            for (t0, w) in tok_groups(0, EXT):
                p, pk = g.ps[pi % 2], f"ps{pi % 2}"
                pi += 1
                mm_fm(S, p, pk, wb, wk, g.hT, t0, w)
                S.op("act", lambda e: e.copy(out=raw[:, t0:t0 + w], in_=p[:, :w]), reads=[pk], writes=["craw"])
            S.op("dve", lambda e: e.tensor_scalar(out=acc[:], in0=raw[:, 125:125 + TPC], scalar1=cw[:, ti, 0:1], scalar2=cbias[:, ti:ti + 1],
                                                  op0=ALU.mult, op1=ALU.add), reads=["craw", "cw", "cbias"], writes=["cacc"])
            for k in range(1, 4):
                S.op("dve", lambda e: e.scalar_tensor_tensor(out=acc[:], in0=raw[:, 125 + k:125 + k + TPC], scalar=cw[:, ti, k:k + 1], in1=acc[:],
                                                             op0=ALU.mult, op1=ALU.add), reads=["craw", "cw", "cacc"], writes=["cacc"])
            S.op("act", lambda e: e.activation(out=tmp[:], in_=acc[:], func=AF.Tanh, scale=0.5), reads=["cacc"], writes=["ctmp"])
            S.op("dve", lambda e: e.tensor_scalar(out=tmp[:], in0=tmp[:], scalar1=0.5, scalar2=0.5, op0=ALU.mult, op1=ALU.add), reads=["ctmp"], writes=["ctmp"])
            if ti < 4:
                S.op("dve", lambda e: e.tensor_tensor(out=xsT[:, ti, :], in0=tmp[:], in1=acc[:], op=ALU.mult), reads=["ctmp", "cacc"], writes=["xsT"])
            elif ti < 6:
                S.op("dve", lambda e: e.tensor_tensor(out=BT[:, ti - 4, :], in0=tmp[:], in1=acc[:], op=ALU.mult), reads=["ctmp", "cacc"], writes=["BT"])
            else:
                S.op("dve", lambda e: e.tensor_tensor(out=CT[:, ti - 6, :], in0=tmp[:], in1=acc[:], op=ALU.mult), reads=["ctmp", "cacc"], writes=["CT"])
    dt = sb(es, nc, "dt", [128, NT, 8], F32)
    adt = sb(es, nc, "adt", [128, NT, 8], F32)
    aexp = sb(es, nc, "aexp", [128, 8], F32)
    S.op("act", lambda e: e.activation(out=aexp[:], in_=par[:, 1, :], func=AF.Exp), reads=["bpar"], writes=["aexp"])
    for n in range(NT):
        S.op("dve", lambda e: e.tensor_tensor(out=dt[:, n, :], in0=dtraw[:, n + 1, :], in1=par[:, 0, :], op=ALU.add), reads=["dtraw", "bpar"], writes=["dt"])
    S.op("act", lambda e: e.activation(out=dt[:], in_=dt[:], func=AF.Exp), reads=["dt"], writes=["dt"])
    S.op("dve", lambda e: e.tensor_scalar(out=dt[:], in0=dt[:], scalar1=1.0, scalar2=None, op0=ALU.add), reads=["dt"], writes=["dt"])
    S.op("act", lambda e: e.activation(out=dt[:], in_=dt[:], func=AF.Ln), reads=["dt"], writes=["dt"])
    for n in range(NT):
        S.op("dve", lambda e: e.scalar_tensor_tensor(out=adt[:, n, :], in0=dt[:, n, :], scalar=-1.0, in1=aexp[:], op0=ALU.mult, op1=ALU.mult),
             reads=["dt", "aexp"], writes=["adt"])
    H = sb(es, nc, "Hst", [128, 512], F32)
    Hb = sb(es, nc, "Hb", [128, 512], BF16)
    sumtot = sb(es, nc, "sumtot", [128, 8], F32)
    S.op("pool", lambda e: e.memset(H[:], 0.0), writes=["Hst"])
    S.op("pool", lambda e: e.memset(sumtot[:], 0.0), writes=["sumtot"])
    if Hin is not None:
        Sp, Dp = Hin
        sp_sb = sb(es, nc, "sprev", [128, 512], F32)
        dp_sb = sb(es, nc, "dprev", [128, 56], F32)
        S.dma("sp", dp_sb[:], Dp[:, :], writes=["dprev"])
        for s_ in range(7):
            S.dma("sp", sp_sb[:], Sp[:, s_ * 512:(s_ + 1) * 512], writes=["sprev"])
            for j in range(8):
                S.op("dve", lambda e: e.tensor_scalar(out=H[:, j * 64:(j + 1) * 64], in0=H[:, j * 64:(j + 1) * 64], scalar1=dp_sb[:, s_ * 8 + j:s_ * 8 + j + 1],
                                                      scalar2=None, op0=ALU.mult), reads=["Hst", "dprev"], writes=["Hst"])
            S.op("dve", lambda e: e.tensor_tensor(out=H[:], in0=H[:], in1=sp_sb[:], op=ALU.add), reads=["Hst", "sprev"], writes=["Hst"])
    S.op("dve", lambda e: e.tensor_copy(out=Hb[:], in_=H[:]), reads=["Hst"], writes=["Hb"])
    xs = sb(es, nc, "xs_tm", [128, 512], F32)
    xdt = sb(es, nc, "xdt", [128, 512], F32)
    xdtb = sb(es, nc, "xdtb", [128, 512], BF16)
    xdtd = sb(es, nc, "xdtd", [128, 512], BF16)
    Btm = sb(es, nc, "Btm", [128, 256], BF16)
    acs = sb(es, nc, "acs", [128, 16], F32)
    eacs = sb(es, nc, "eacs", [128, 8], F32)
    dend = sb(es, nc, "dend", [128, 8], F32)
    dch = sb(es, nc, "dch", [128, 8], F32)
    if full:
        cbm = sb(es, nc, "cbm", [128, 2, 128], F32)
        adtb = sb(es, nc, "adtb", [128, 128], F32)
        dsb = sb(es, nc, "dsb", [128, 128], F32)
        Mt = sb(es, nc, "Mt", [128, 128], BF16)
        ysb = sb(es, nc, "ysb", [128, 512], F32)
        y2 = sb(es, nc, "y2", [128, 512], F32)
        zt = sb(es, nc, "zt", [128, 512], F32)
        dsk = sb(es, nc, "dsk", [128, 512], F32)
        bnw = sb(es, nc, "bnw", [128, 512], F32)
        S.dma("sp", dsk[:], g.b_dskip[:, :], writes=["dsk"])
        S.dma("sp", bnw[:], g.b_norm[:, :], writes=["bnw"])
        ssq = sb(es, nc, "ssq", [128, 2], F32)
        mst = sb(es, nc, "mst", [128, 512], BF16)
    for n in range(NT):
        cs = slice(n * 128, (n + 1) * 128)
        for ti in range(4):
            S.op("pe", lambda e: e.matmul(g.ps[0][:, ti * 128:(ti + 1) * 128], lhsT=xsT[:, ti, cs], rhs=C["identf"][:, :], start=True, stop=True),
                 reads=["xsT", "identf"], writes=["ps0"])
        S.op("act", lambda e: e.copy(out=xs[:], in_=g.ps[0][:, :]), reads=["ps0"], writes=["xs_tm"])
        for gp in range(2):
            S.op("pe", lambda e: e.transpose(out=g.ps_bf[:, gp * 128:(gp + 1) * 128], in_=BT[:, gp, cs], identity=C["identb"][:]),
                 reads=["BT", "identb"], writes=["ps_bf"])
        S.op("act", lambda e: e.copy(out=Btm[:], in_=g.ps_bf[:, 0:256]), reads=["ps_bf"], writes=["Btm"])
        S.op("pe", lambda e: e.matmul(g.ps[1][:, 0:8], lhsT=triu[:, :], rhs=adt[:, n, :], start=True, stop=True), reads=["triu", "adt"], writes=["ps1"])
        S.op("pe", lambda e: e.matmul(g.ps[1][:, 8:16], lhsT=ones[:, :], rhs=adt[:, n, :], start=True, stop=True), reads=["onesf", "adt"], writes=["ps1"])
        S.op("dve", lambda e: e.tensor_copy(out=acs[:], in_=g.ps[1][:, 0:16]), reads=["ps1"], writes=["acs"])
        S.op("act", lambda e: e.activation(out=eacs[:], in_=acs[:, 0:8], func=AF.Exp), reads=["acs"], writes=["eacs"])
        S.op("act", lambda e: e.activation(out=dch[:], in_=acs[:, 8:16], func=AF.Exp), reads=["acs"], writes=["dch"])
        S.op("dve", lambda e: e.tensor_tensor(out=dend[:], in0=acs[:, 8:16], in1=acs[:, 0:8], op=ALU.subtract), reads=["acs"], writes=["dend"])
        S.op("act", lambda e: e.activation(out=dend[:], in_=dend[:], func=AF.Exp), reads=["dend"], writes=["dend"])
        S.op("dve", lambda e: e.tensor_tensor(out=sumtot[:], in0=sumtot[:], in1=acs[:, 8:16], op=ALU.add), reads=["sumtot", "acs"], writes=["sumtot"])
        for j in range(8):
            js = slice(j * 64, (j + 1) * 64)
            S.op("dve", lambda e: e.tensor_scalar(out=xdt[:, js], in0=xs[:, js], scalar1=dt[:, n, j:j + 1], scalar2=None, op0=ALU.mult),
                 reads=["xs_tm", "dt"], writes=["xdt"])
            S.op("pool", lambda e: e.tensor_scalar(out=xdtd[:, js], in0=xdt[:, js], scalar1=dend[:, j:j + 1], scalar2=None, op0=ALU.mult),
                 reads=["xdt", "dend"], writes=["xdtd"])
        S.op("act", lambda e: e.copy(out=xdtb[:], in_=xdt[:]), reads=["xdt"], writes=["xdtb"])
        if full:
            for gp in range(2):
                S.op("pe", lambda e: e.matmul(g.ps[2][:, gp * 128:(gp + 1) * 128], lhsT=BT[:, gp, cs], rhs=CT[:, gp, cs], start=True, stop=True),
                     reads=["BT", "CT"], writes=["ps2"])
            S.op("dve", lambda e: e.tensor_tensor(out=cbm[:], in0=g.ps[2][:, 0:256].rearrange("p (a b) -> p a b", a=2),
                                                  in1=triu[:, :].unsqueeze(1).to_broadcast([128, 2, 128]), op=ALU.mult), reads=["ps2", "triu"], writes=["cbm"])
            for j in range(8):
                gp = j // 4
                js = slice(j * 64, (j + 1) * 64)
                S.op("pool", lambda e: e.tensor_scalar(out=adtb[:], in0=ones[:], scalar1=adt[:, n, j:j + 1], scalar2=None, op0=ALU.mult),
                     reads=["onesf", "adt"], writes=["adtb"])
                S.op("pe", lambda e: e.matmul(g.ps[3][:, 0:128], lhsT=adtb[:, :], rhs=triu[:, :], start=True, stop=True), reads=["adtb", "triu"], writes=["ps3"])
                S.op("dve", lambda e: e.tensor_scalar(out=dsb[:], in0=g.ps[3][:, 0:128], scalar1=acs[:, j:j + 1], scalar2=0.0, op0=ALU.subtract, op1=ALU.min),
                     reads=["ps3", "acs"], writes=["dsb"])
                S.op("act", lambda e: e.activation(out=dsb[:], in_=dsb[:], func=AF.Exp), reads=["dsb"], writes=["dsb"])
                S.op("dve", lambda e: e.tensor_tensor(out=Mt[:], in0=dsb[:], in1=cbm[:, gp, :], op=ALU.mult), reads=["dsb", "cbm"], writes=["Mt"])
                S.op("pe", lambda e: e.matmul(g.ps[4][:, js], lhsT=Mt[:, :], rhs=xdtb[:, js], start=True, stop=True), reads=["Mt", "xdtb"], writes=["ps4"])
            for gp in range(2):
                S.op("pe", lambda e: e.matmul(g.ps[5][:, gp * 256:(gp + 1) * 256], lhsT=CT[:, gp, cs], rhs=Hb[:, gp * 256:(gp + 1) * 256], start=True, stop=True),
                     reads=["CT", "Hb"], writes=["ps5"])
            for j in range(8):
                js = slice(j * 64, (j + 1) * 64)
                S.op("dve", lambda e: e.tensor_scalar(out=ysb[:, js], in0=g.ps[5][:, js], scalar1=eacs[:, j:j + 1], scalar2=None, op0=ALU.mult),
                     reads=["ps5", "eacs"], writes=["ysb"])
            S.op("dve", lambda e: e.tensor_tensor(out=ysb[:], in0=g.ps[4][:, :], in1=ysb[:], op=ALU.add), reads=["ps4", "ysb"], writes=["ysb"])
            S.op("pool", lambda e: e.tensor_tensor(out=y2[:], in0=xs[:], in1=dsk[:], op=ALU.mult), reads=["xs_tm", "dsk"], writes=["y2"])
            S.op("dve", lambda e: e.tensor_tensor(out=ysb[:], in0=ysb[:], in1=y2[:], op=ALU.add), reads=["ysb", "y2"], writes=["ysb"])
            S.dma("sp", zt[:], g.zs[n * 128:(n + 1) * 128, 512:1024], reads=["zs_scr"], writes=["zt"])
            S.op("dve", lambda e: e.tensor_tensor(out=ysb[:], in0=ysb[:], in1=zt[:], op=ALU.mult), reads=["ysb", "zt"], writes=["ysb"])
            S.op("pool", lambda e: e.memset(ssq[:], 0.0), writes=["ssq"])
            for gp in range(2):
                S.op("act", lambda e: e.activation(out=y2[:, gp * 256:(gp + 1) * 256], in_=ysb[:, gp * 256:(gp + 1) * 256], func=AF.Square,
                                                   accum_out=ssq[:, gp:gp + 1]), reads=["ysb", "ssq"], writes=["y2", "ssq"])
            S.op("act", lambda e: e.activation(out=ssq[:], in_=ssq[:], func=AF.Sqrt, bias=C["epsc"][:, 0:1], scale=1.0 / 256), reads=["ssq"], writes=["ssq"])
            S.op("dve", lambda e: e.reciprocal(out=ssq[:], in_=ssq[:]), reads=["ssq"], writes=["ssq"])
            for gp in range(2):
                gs = slice(gp * 256, (gp + 1) * 256)
                S.op("dve", lambda e: e.scalar_tensor_tensor(out=mst[:, gs], in0=ysb[:, gs], scalar=ssq[:, gp:gp + 1], in1=bnw[:, gs], op0=ALU.mult, op1=ALU.mult),
                     reads=["ysb", "ssq", "bnw"], writes=["mst"])
            S.dma("sp", g.mix[n * 128:(n + 1) * 128, 512:1024], mst[:], reads=["mst"], writes=["mix_scr"])
        for gp in range(2):
            S.op("pe", lambda e: e.matmul(g.ps[6][:, gp * 256:(gp + 1) * 256], lhsT=Btm[:, gp * 128:(gp + 1) * 128], rhs=xdtd[:, gp * 256:(gp + 1) * 256],
                                          start=True, stop=True), reads=["Btm", "xdtd"], writes=["ps6"])
        for j in range(8):
            js = slice(j * 64, (j + 1) * 64)
            S.op("dve", lambda e: e.tensor_scalar(out=H[:, js], in0=H[:, js], scalar1=dch[:, j:j + 1], scalar2=None, op0=ALU.mult),
                 reads=["Hst", "dch"], writes=["Hst"])
        S.op("dve", lambda e: e.tensor_tensor(out=H[:], in0=g.ps[6][:, :], in1=H[:], op=ALU.add), reads=["ps6", "Hst"], writes=["Hst"])
        S.op("act", lambda e: e.copy(out=Hb[:], in_=H[:]), reads=["Hst"], writes=["Hb"])
    if out_S is not None:
        S.dma("sp", out_S[:, :], H[:], reads=["Hst"], writes=["o_S"])
        S.op("act", lambda e: e.activation(out=sumtot[:], in_=sumtot[:], func=AF.Exp), reads=["sumtot"], writes=["sumtot"])
        S.dma("sp", out_D[:, :], sumtot[:], reads=["sumtot"], writes=["o_D"])


def phase_B(g):
    with Scope(g.S, g.nc) as es:
        emit_ssd(g, es, g.WB, FMT["b_x"], g.dtraw, True, Hin=(g.b_Sprev, g.b_Dprev))


def host_ssd_params(layer, P):
    bc = lambda v: np.ascontiguousarray(np.broadcast_to(v[None, :], (128, v.shape[0]))).astype(np.float32)
    d = {}
    d["b_cw"] = np.ascontiguousarray(P["b_conv_w"][layer].reshape(4, 8, 128).transpose(2, 1, 0))
    d["b_cb"] = np.ascontiguousarray(P["b_conv_b"][layer].reshape(8, 128).T)
    par = np.zeros((128, 3, 8), np.float32)
    par[:, 0, :] = P["b_dt_bias"][layer][None, :]
    par[:, 1, :] = P["b_a_log"][layer][None, :]
    d["b_par"] = par
    d["b_dskip"] = bc(np.repeat(P["b_d"][layer], 64))
    d["b_norm"] = bc(P["b_norm"][layer])
    return d


def host_B_ssd_chain(c, G):
    Sp = np.zeros((128, 7 * 512), np.float32)
    Dp = np.ones((128, 56), np.float32)
    for s_ in range(7):
        src = c - 7 + s_
        if src >= 0:
            Sp[:, s_ * 512:(s_ + 1) * 512] = G["S"][src]
            Dp[:, s_ * 8:(s_ + 1) * 8] = G["D"][src]
    return {"b_Sprev": Sp, "b_Dprev": Dp}


def get_nc(name):
    if name not in _NC_CACHE:
        _NC_CACHE[name] = {"A": build_A, "B": build_B}[name]()
    return _NC_CACHE[name]


def kernel(**inputs):
    P = {k: np.asarray(v) for k, v in inputs.items()}
    _HOST_CACHE.clear()
    x = np.ascontiguousarray(P["x"][0], dtype=np.float32)
    cores = list(range(NCORE))
    for layer in range(4):
        resA = run_bass_kernel_spmd(get_nc("A"), [host_A_inputs(c, x, layer, P) for c in cores], core_ids=cores)
        G = gather_A(resA.results)
        resB = run_bass_kernel_spmd(get_nc("B"), [host_B_inputs(c, x, layer, P, G) for c in cores], core_ids=cores)
        x = np.concatenate([np.asarray(resB.results[c]["x_out"], dtype=np.float32) for c in cores], axis=0)
    return x[None].astype(np.float32)
```

```python
import math
import os
from contextlib import ExitStack
import numpy as np
import ml_dtypes
import concourse.bass as bass
import concourse.mybir as mybir
from concourse.bass_utils import run_bass_kernel_spmd

F32 = mybir.dt.float32
BF16 = mybir.dt.bfloat16
I32 = mybir.dt.int32
AF = mybir.ActivationFunctionType
ALU = mybir.AluOpType
AX = mybir.AxisListType
NPBF = ml_dtypes.bfloat16

NCORE = 8
TPC = 2048
NT = 16
NTT = NT + 1
EXT = NTT * 128
DM = 1024
EPS = 1e-6
NEG = -30000.0
SEM_LIMIT = 30000


class Sync:
    def __init__(self, nc, n_dma_sems=12):
        self.nc = nc
        self.engs = {"pe": nc.tensor, "dve": nc.vector, "act": nc.scalar, "pool": nc.gpsimd, "sp": nc.sync}
        self.nsem = 0
        self.sem = {}
        self.semkey = {}
        self.cnt = {}
        for k in self.engs:
            self._newsem(k)
        self.waited = {k: {} for k in self.engs}
        self.dma_sems = [nc.alloc_semaphore(f"sdma{i}") for i in range(n_dma_sems)]
        self.dma_uses = [0] * n_dma_sems
        self.dma_i = 0
        self.bufs = {}
        self.nops = 0

    def _newsem(self, k):
        self.nsem += 1
        self.sem[k] = self.nc.alloc_semaphore(f"s{k}{self.nsem}")
        self.semkey[k] = f"{k}{self.nsem}"
        self.cnt[k] = 0

    def _wait(self, eng, tok):
        if tok is None:
            return
        semkey, sem, val, src = tok
        if eng == "pe" and src == "pe":
            return
        if self.waited[eng].get(semkey, 0) >= val:
            return
        self.engs[eng].wait_ge(sem, val)
        self.waited[eng][semkey] = val

    def _deps(self, eng, reads, writes):
        for k in reads:
            b = self.bufs.get(k)
            if b:
                self._wait(eng, b["w"])
                if k.startswith("ps"):
                    for t in b["r"]:
                        if t[3] != eng:
                            self._wait(eng, t)
        for k in writes:
            b = self.bufs.get(k)
            if b:
                self._wait(eng, b["w"])
                for t in b["r"]:
                    self._wait(eng, t)

    def _commit(self, tok, reads, writes):
        for k in reads:
            b = self.bufs.setdefault(k, {"w": None, "r": []})
            b["r"].append(tok)
            if len(b["r"]) > 24:
                last = {}
                for t in b["r"]:
                    if t[0] not in last or last[t[0]][2] < t[2]:
                        last[t[0]] = t
                b["r"] = list(last.values())
        for k in writes:
            self.bufs[k] = {"w": tok, "r": []}

    def op(self, eng, fn, reads=(), writes=()):
        self._deps(eng, reads, writes)
        if self.cnt[eng] >= SEM_LIMIT:
            self._newsem(eng)
        ins = fn(self.engs[eng])
        self.cnt[eng] += 1
        ins.then_inc(self.sem[eng], 1)
        tok = (self.semkey[eng], self.sem[eng], self.cnt[eng], eng)
        self._commit(tok, reads, writes)
        self.nops += 1
        return tok

    def dma(self, eng, out, in_, reads=(), writes=()):
        self._deps(eng, reads, writes)
        i = self.dma_i % len(self.dma_sems)
        self.dma_i += 1
        sem = self.dma_sems[i]
        if self.dma_uses[i] > 0:
            self._wait(eng, (f"dma{i}", sem, 16 * self.dma_uses[i], "dma"))
        if self.dma_uses[i] * 16 >= SEM_LIMIT:
            sem = self.dma_sems[i] = self.nc.alloc_semaphore(f"sdma{i}_{self.dma_i}")
            self.dma_uses[i] = 0
            self.waited_reset(f"dma{i}")
        self.dma_uses[i] += 1
        self.engs[eng].dma_start(out=out, in_=in_).then_inc(sem, 16)
        tok = (f"dma{i}", sem, 16 * self.dma_uses[i], "dma")
        self._commit(tok, reads, writes)
        self.nops += 1
        return tok

    def waited_reset(self, semkey):
        for e in self.waited:
            self.waited[e].pop(semkey, None)
        for b in self.bufs.values():
            if b["w"] is not None and b["w"][0] == semkey:
                b["w"] = None
            b["r"] = [t for t in b["r"] if t[0] != semkey]

    def release(self, keys):
        for eng in self.engs:
            for k in keys:
                b = self.bufs.get(k)
                if b:
                    self._wait(eng, b["w"])
                    for t in b["r"]:
                        self._wait(eng, t)
        for k in keys:
            self.bufs.pop(k, None)

    def finish(self, eng="sp"):
        for k, b in self.bufs.items():
            self._wait(eng, b["w"])
            for t in b["r"]:
                self._wait(eng, t)


OFF = {}
_o = 0
for _n, _w in [("a_q", 512), ("a_kc", 128), ("a_vc", 128), ("a_ks", 128), ("a_vs", 128), ("a_kw", 128), ("a_vw", 128),
               ("a_gate", 24), ("a_z", 512), ("b_x", 512), ("b_B", 256), ("b_C", 256), ("b_dt", 8), ("b_z", 512),
               ("c_q", 512), ("c_k", 128), ("c_v", 128), ("c_z", 512), ("d_q", 1536), ("d_k", 512), ("d_v", 512),
               ("d_z", 512), ("m_q", 256), ("m_z", 256)]:
    OFF[_n] = (_o, _w)
    _o += _w
assert _o == 8224


def cols(*names):
    out = []
    for n in names:
        o, w = OFF[n]
        out.extend(range(o, o + w))
    return out


A_FM = ["a_kc", "a_vc", "a_ks", "a_kw", "c_k", "d_k", "b_x", "b_B", "b_C"]
A_TM = ["a_vs", "a_vw", "c_v", "d_v", "b_dt"]
A_COLS = cols(*A_FM) + cols(*A_TM)
A_NFM = 17
A_TM0 = A_NFM * 128


class Scope:
    def __init__(self, S, nc):
        self.S, self.nc, self.es, self.names = S, nc, ExitStack(), []

    def __enter__(self):
        self.es.__enter__()
        return self

    def __exit__(self, *a):
        self.S.release(self.names)
        return self.es.__exit__(*a)

    def enter_context(self, cm):
        return self.es.enter_context(cm)


_SB_COUNT = [0]


def sb(es, nc, name, shape, dt):
    if isinstance(es, Scope):
        es.names.append(name)
    _SB_COUNT[0] += 1
    return es.enter_context(nc.sbuf_tensor(f"{name}__{_SB_COUNT[0]}", shape, dt))


def emit_consts(S, nc, es):
    C = {}
    C["identb"] = sb(es, nc, "identb", [128, 128], BF16)
    C["identf"] = sb(es, nc, "identf", [128, 128], F32)
    for nm in ("identb", "identf"):
        t = C[nm]
        S.op("pool", lambda e: e.memset(t[:], 1.0), writes=[nm])
        S.op("pool", lambda e: e.affine_select(out=t[:], in_=t[:], pattern=[[-1, 128]], compare_op=ALU.is_equal,
                                                fill=0.0, base=0, channel_multiplier=1), reads=[nm], writes=[nm])
    return C


def emit_hT(S, nc, C, x_ext, prenorm_b, hT, ps_bf, ntt=NTT):
    with Scope(S, nc) as es:
        pn = sb(es, nc, "pn", [128, DM], F32)
        S.dma("sp", pn[:], prenorm_b[:, :], writes=["pn"])
        xt = [sb(es, nc, f"xt{i}", [128, DM], F32) for i in range(2)]
        hb = [sb(es, nc, f"hb{i}", [128, DM], BF16) for i in range(2)]
        junk = sb(es, nc, "junk", [128, DM], F32)
        ss = [sb(es, nc, f"ss{i}", [128, 1], F32) for i in range(2)]
        for t in range(ntt):
            b = t % 2
            S.dma("sp", xt[b][:], x_ext[t * 128:(t + 1) * 128, :], writes=[f"xt{b}"])
            S.op("pool", lambda e: e.memset(ss[b][:], 0.0), writes=[f"ss{b}"])
            S.op("act", lambda e: e.activation(out=junk[:], in_=xt[b][:], func=AF.Square, accum_out=ss[b][:]),
                 reads=[f"xt{b}", f"ss{b}"], writes=["junk", f"ss{b}"])
            S.op("act", lambda e: e.activation(out=ss[b][:], in_=ss[b][:], func=AF.Sqrt, bias=C["epsc"][:, 0:1], scale=1.0 / DM),
                 reads=[f"ss{b}"], writes=[f"ss{b}"])
            S.op("dve", lambda e: e.reciprocal(out=ss[b][:], in_=ss[b][:]), reads=[f"ss{b}"], writes=[f"ss{b}"])
            S.op("dve", lambda e: e.scalar_tensor_tensor(out=hb[b][:], in0=xt[b][:], scalar=ss[b][:, 0:1], in1=pn[:],
                                                         op0=ALU.mult, op1=ALU.mult),
                 reads=[f"xt{b}", f"ss{b}", "pn"], writes=[f"hb{b}"])
            for k in range(8):
                S.op("pe", lambda e: e.transpose(out=ps_bf[:, k * 128:(k + 1) * 128], in_=hb[b][:, k * 128:(k + 1) * 128],
                                                 identity=C["identb"][:]),
                     reads=[f"hb{b}", "identb"], writes=["ps_bf"])
            S.op("act", lambda e: e.copy(out=hT[:, :, t * 128:(t + 1) * 128],
                                         in_=ps_bf[:, :].rearrange("p (k t) -> p k t", k=8)),
                 reads=["ps_bf"], writes=["hT"])


class WLoader:
    def __init__(self, S, nc, es, W, width, name="w"):
        self.S, self.nc, self.W, self.width, self.name = S, nc, W, width, name
        self.wf = [sb(es, nc, f"{name}f{i}", [128, 8, width], F32) for i in range(2)]
        self.wb = [sb(es, nc, f"{name}b{i}", [128, 8, width], BF16) for i in range(2)]
        self.i = 0

    def load(self, c0, w=None):
        w = w or self.width
        b = self.i % 2
        self.i += 1
        S = self.S
        S.dma("sp", self.wf[b][:, :, :w], self.W[:, c0:c0 + w].rearrange("(k p) c -> p k c", p=128),
              writes=[f"{self.name}f{b}"])
        S.op("pool", lambda e: e.tensor_copy(out=self.wb[b][:, :, :w], in_=self.wf[b][:, :, :w]),
             reads=[f"{self.name}f{b}"], writes=[f"{self.name}b{b}"])
        return self.wb[b], f"{self.name}b{b}"


def tok_groups(t0, t1, g=512):
    out = []
    while t0 < t1:
        w = min(g, t1 - t0)
        out.append((t0, w))
        t0 += w
    return out


def mm_fm(S, ps, pskey, wb, wkey, hT, t0, w, m=128, mo=0):
    for k in range(8):
        S.op("pe", lambda e: e.matmul(ps[:m, :w], lhsT=wb[:, k, mo:mo + m], rhs=hT[:, k, t0:t0 + w], start=(k == 0), stop=(k == 7)),
             reads=[wkey, "hT"], writes=[pskey])


def mm_tm(S, ps, pskey, wb, wkey, hT, t, c0, w):
    for k in range(8):
        S.op("pe", lambda e: e.matmul(ps[:, :w], lhsT=hT[:, k, t * 128:(t + 1) * 128], rhs=wb[:, k, c0:c0 + w], start=(k == 0), stop=(k == 7)),
             reads=[wkey, "hT"], writes=[pskey])


def build_A():
    nc = bass.Bass("TRN2", target_bir_lowering=False)
    D = lambda name, shape, dt, kind: nc.dram_tensor(name, shape, dt, kind=kind).ap()
    x_ext = D("x_ext", [EXT, DM], F32, "ExternalInput")
    prenorm_b = D("prenorm_b", [128, DM], F32, "ExternalInput")
    WA = D("WA", [DM, len(A_COLS)], F32, "ExternalInput")
    pos_b = D("pos_b", [128, TPC], I32, "ExternalInput")
    ropec = D("ropec", [128, 2], F32, "ExternalInput")
    ropeR = D("ropeR", [128, 128], F32, "ExternalInput")
    w1d = D("w1d", [2, 128, 32 * 128], F32, "ExternalInput")
    w2 = D("w2", [2, 128, 64], F32, "ExternalInput")
    peT = D("peT", [2, 128, 32], F32, "ExternalInput")
    o_fm = D("o_fm", [7, 128, TPC], BF16, "ExternalOutput")
    o_tm = D("o_tm", [TPC, 896], BF16, "ExternalOutput")
    o_kcT = D("o_kcT", [128, 128], BF16, "ExternalOutput")
    o_vc = D("o_vc", [128, 128], BF16, "ExternalOutput")
    o_S = D("o_S", [128, 512], F32, "ExternalOutput")
    o_D = D("o_D", [128, 8], F32, "ExternalOutput")
    g = PB()
    g.nc = nc
    g.b_cw = D("b_cw", [128, 8, 4], F32, "ExternalInput")
    g.b_cb = D("b_cb", [128, 8], F32, "ExternalInput")
    g.b_par = D("b_par", [128, 3, 8], F32, "ExternalInput")

    with ExitStack() as es:
        S = Sync(nc)
        C = emit_consts(S, nc, es)
        C["epsc"] = sb(es, nc, "epsc", [128, 1], F32)
        S.op("pool", lambda e: e.memset(C["epsc"][:], EPS), writes=["epsc"])
        hT = sb(es, nc, "hT", [128, 8, EXT], BF16)
        ps = [es.enter_context(nc.psum_tensor(f"ps{i}", [128, 512], F32)) for i in range(7)]
        ps_bf = es.enter_context(nc.psum_tensor("ps_bf", [128, 1024], BF16))
        emit_hT(S, nc, C, x_ext, prenorm_b, hT, ps_bf)
        g.S, g.C, g.hT, g.ps, g.ps_bf = S, C, hT, ps, ps_bf

        dtraw = sb(es, nc, "dtraw", [128, NTT, 8], F32)
        with Scope(S, nc) as e1:
            cosT = sb(e1, nc, "cosT", [128, TPC], F32)
            sinT = sb(e1, nc, "sinT", [128, TPC], F32)
            rc = sb(e1, nc, "rc", [128, 2], F32)
            rR = sb(e1, nc, "rR", [128, 128], F32)
            S.dma("sp", rc[:], ropec[:, :], writes=["rc"])
            S.dma("sp", rR[:], ropeR[:, :], writes=["rR"])
            emit_rope_tables(S, nc, pos_b, rc, cosT, sinT)

            wl = WLoader(S, nc, e1, WA, 128)
            kcT = sb(e1, nc, "kcT_raw", [128, 2, EXT], BF16)
            stage = [sb(e1, nc, f"stage{i}", [128, TPC], BF16) for i in range(2)]
            xf = sb(e1, nc, "xf", [128, 512], F32)
            pi = 0
            for ti in range(9):
                wb, wkey = wl.load(ti * 128)
                if ti < 2:
                    for (t0, w) in tok_groups(0, EXT):
                        p, pk = ps[pi % 4], f"ps{pi % 4}"
                        pi += 1
                        mm_fm(S, p, pk, wb, wkey, hT, t0, w)
                        S.op("act", lambda e: e.copy(out=kcT[:, ti, t0:t0 + w], in_=p[:, :w]), reads=[pk], writes=["kcT_raw"])
                else:
                    sbuf, skey = stage[ti % 2], f"stage{ti % 2}"
                    for (t0, w) in tok_groups(128, EXT):
                        p, pk = ps[pi % 4], f"ps{pi % 4}"
                        pi += 1
                        mm_fm(S, p, pk, wb, wkey, hT, t0, w)
                        l0 = t0 - 128
                        if ti == 4:
                            emit_rope_apply(S, nc, p, pk, w, xf, rR, cosT, sinT, l0, ps[4], "ps4", sbuf[:, l0:l0 + w], skey, scale=1.0)
                        else:
                            S.op("act", lambda e: e.copy(out=sbuf[:, l0:l0 + w], in_=p[:, :w]), reads=[pk], writes=[skey])
                    S.dma("sp", o_fm[ti - 2, :, :], sbuf[:, :], reads=[skey], writes=[f"o_fm{ti}"])
            wlt = WLoader(S, nc, e1, WA, 512, name="wt")
            vst = [sb(e1, nc, f"vst{i}", [128, 896], BF16) for i in range(2)]
            wbs = []
            wA, kA = wlt.load(A_TM0, 512)
            wB, kB = wlt.load(A_TM0 + 512, 384)
            for t in range(1, NTT):
                b = t % 2
                for (wb, wk, c0, w) in ((wA, kA, 0, 512), (wB, kB, 512, 384)):
                    p, pk = ps[pi % 4], f"ps{pi % 4}"
                    pi += 1
                    mm_tm(S, p, pk, wb, wk, hT, t, 0, w)
                    S.op("act", lambda e: e.copy(out=vst[b][:, c0:c0 + w], in_=p[:, :w]), reads=[pk], writes=[f"vst{b}"])
                S.dma("sp", o_tm[(t - 1) * 128:t * 128, :], vst[b][:, :], reads=[f"vst{b}"], writes=[f"o_tm{t}"])

            emit_compress(S, nc, e1, C, kcT, w1d, w2, peT, ps, o_kcT, o_vc)
            wd, kd_ = wlt.load(A_TM0 + 896, 8)
            for t in range(NTT):
                p, pk = ps[pi % 4], f"ps{pi % 4}"
                pi += 1
                mm_tm(S, p, pk, wd, kd_, hT, t, 0, 8)
                S.op("act", lambda e: e.copy(out=dtraw[:, t, :], in_=p[:, 0:8]), reads=[pk], writes=["dtraw"])
        with Scope(S, nc) as e2:
            emit_ssd(g, e2, WA, 9, dtraw, False, out_S=o_S, out_D=o_D)
        S.finish("sp")
    return nc


def emit_rope_tables(S, nc, pos_b, rc, cosT, sinT):
    with Scope(S, nc) as es:
        pi_ = sb(es, nc, "pos_i", [128, TPC], I32)
        ang = sb(es, nc, "ang", [128, TPC], F32)
        tmp = sb(es, nc, "rtmp", [128, TPC], F32)
        kf = sb(es, nc, "rkf", [128, TPC], F32)
        S.dma("sp", pi_[:], pos_b[:, :], writes=["pos_i"])
        S.op("dve", lambda e: e.tensor_copy(out=ang[:], in_=pi_[:]), reads=["pos_i"], writes=["ang"])
        S.op("dve", lambda e: e.tensor_scalar(out=ang[:], in0=ang[:], scalar1=rc[:, 0:1], scalar2=None, op0=ALU.mult),
             reads=["ang", "rc"], writes=["ang"])
        for (dst, dk, shift) in ((sinT, "sinT", 0.0), (cosT, "cosT", 0.25)):
            S.op("dve", lambda e: e.tensor_scalar(out=tmp[:], in0=ang[:], scalar1=shift, scalar2=None, op0=ALU.add),
                 reads=["ang"], writes=["rtmp"])
            S.op("dve", lambda e: e.tensor_copy(out=pi_[:], in_=tmp[:]), reads=["rtmp"], writes=["pos_i"])
            S.op("dve", lambda e: e.tensor_copy(out=kf[:], in_=pi_[:]), reads=["pos_i"], writes=["rkf"])
            S.op("dve", lambda e: e.tensor_tensor(out=tmp[:], in0=tmp[:], in1=kf[:], op=ALU.subtract),
                 reads=["rtmp", "rkf"], writes=["rtmp"])
            S.op("act", lambda e: e.activation(out=dst[:], in_=tmp[:], func=AF.Sin, scale=2.0 * math.pi),
                 reads=["rtmp"], writes=[dk])


def emit_rope_apply(S, nc, p, pk, w, xf, rR, cosT, sinT, l0, p2, p2k, dst, dkey, scale=1.0):
    S.op("act", lambda e: e.copy(out=xf[:, :w], in_=p[:, :w]), reads=[pk], writes=["xf"])
    S.op("pe", lambda e: e.matmul(p2[:, :w], lhsT=rR[:, :], rhs=xf[:, :w], start=True, stop=True), reads=["rR", "xf"], writes=[p2k])
    S.op("dve", lambda e: e.tensor_tensor(out=xf[:, :w], in0=xf[:, :w], in1=cosT[:, l0:l0 + w], op=ALU.mult),
         reads=["xf", "cosT"], writes=["xf"])
    S.op("dve", lambda e: e.tensor_tensor(out=p[:, :w], in0=p2[:, :w], in1=sinT[:, l0:l0 + w], op=ALU.mult),
         reads=[p2k, "sinT"], writes=[pk])
    if scale == 1.0:
        S.op("dve", lambda e: e.tensor_tensor(out=dst, in0=p[:, :w], in1=xf[:, :w], op=ALU.add), reads=[pk, "xf"], writes=[dkey])
    else:
        S.op("dve", lambda e: e.scalar_tensor_tensor(out=dst, in0=p[:, :w], scalar=scale, in1=xf[:, :w], op0=ALU.mult, op1=ALU.add),
             reads=[pk, "xf"], writes=[dkey])


def emit_compress(S, nc, es0, C, kcT, w1d, w2, peT, ps, o_kcT, o_vc):
    with Scope(S, nc) as es:
        w1f = sb(es, nc, "w1f", [128, 32 * 128], F32)
        w1b = [sb(es, nc, f"w1b{i}", [128, 32, 128], BF16) for i in range(2)]
        w2f = sb(es, nc, "w2f", [128, 2, 64], F32)
        w2b = sb(es, nc, "w2b", [128, 2, 64], BF16)
        pef = sb(es, nc, "pef", [128, 2, 32], F32)
        peb = sb(es, nc, "peb", [128, 2, 32], BF16)
        biasc = sb(es, nc, "biasc", [128, 2], F32)
        g1 = sb(es, nc, "g1", [128, 128], F32)
        g2 = sb(es, nc, "g2", [128, 128], F32)
        hid = sb(es, nc, "hid", [128, 128], BF16)
        okc = sb(es, nc, "okc", [128, 128], BF16)
        ovc = sb(es, nc, "ovc", [128, 128], BF16)
        for kv in range(2):
            S.dma("sp", w1f[:], w1d[kv, :, :], writes=["w1f"])
            S.op("pool", lambda e: e.tensor_copy(out=w1b[kv][:, :, :], in_=w1f[:, :].rearrange("p (l m) -> p l m", l=32)),
                 reads=["w1f"], writes=[f"w1b{kv}"])
            S.dma("sp", w2f[:, kv, :], w2[kv, :, :], writes=["w2f"])
            S.dma("sp", pef[:, kv, :], peT[kv, :, :], writes=["pef"])
        S.op("dve", lambda e: e.tensor_copy(out=w2b[:], in_=w2f[:]), reads=["w2f"], writes=["w2b"])
        S.op("dve", lambda e: e.tensor_copy(out=peb[:], in_=pef[:]), reads=["pef"], writes=["peb"])
        for kv in range(2):
            for l in range(32):
                S.op("pe", lambda e: e.matmul(ps[5][:, 0:1], lhsT=w1b[kv][0:64, l, :], rhs=peb[0:64, kv, l:l + 1], start=(l == 0), stop=(l == 31)),
                     reads=[f"w1b{kv}", "peb"], writes=["ps5"])
            S.op("act", lambda e: e.copy(out=biasc[:, kv:kv + 1], in_=ps[5][:, 0:1]), reads=["ps5"], writes=["biasc"])
            for hh in range(2):
                p, pk = ps[hh], f"ps{hh}"
                r0 = 64 * hh
                for l in range(32):
                    S.op("pe", lambda e: e.matmul(p[:, 0:128], lhsT=w1b[kv][r0:r0 + 64, l, :],
                                                  rhs=kcT[r0:r0 + 64, kv, 112 + l:112 + l + 16 * 127 + 1:16], start=(l == 0), stop=(l == 31)),
                         reads=[f"w1b{kv}", "kcT_raw"], writes=[pk])
                S.op("act", lambda e: e.activation(out=g1[:], in_=p[:, 0:128], func=AF.Identity, bias=biasc[:, kv:kv + 1], scale=1.0),
                     reads=[pk, "biasc"], writes=["g1"])
                S.op("dve", lambda e: e.tensor_tensor(out=g2[:], in0=g1[:], in1=g1[:], op=ALU.mult), reads=["g1"], writes=["g2"])
                S.op("dve", lambda e: e.tensor_scalar(out=g2[:], in0=g2[:], scalar1=0.044715, scalar2=1.0, op0=ALU.mult, op1=ALU.add),
                     reads=["g2"], writes=["g2"])
                S.op("dve", lambda e: e.tensor_tensor(out=g2[:], in0=g2[:], in1=g1[:], op=ALU.mult), reads=["g1", "g2"], writes=["g2"])
                S.op("act", lambda e: e.activation(out=g2[:], in_=g2[:], func=AF.Tanh, scale=math.sqrt(2.0 / math.pi)),
                     reads=["g2"], writes=["g2"])
                S.op("dve", lambda e: e.scalar_tensor_tensor(out=g2[:], in0=g2[:], scalar=1.0, in1=g1[:], op0=ALU.add, op1=ALU.mult),
                     reads=["g1", "g2"], writes=["g2"])
                S.op("act", lambda e: e.activation(out=hid[:], in_=g2[:], func=AF.Copy, scale=0.5), reads=["g2"], writes=["hid"])
                if kv == 0:
                    S.op("pe", lambda e: e.matmul(ps[2][0:64, 0:128], lhsT=w2b[:, 0, :], rhs=hid[:], start=True, stop=True),
                         reads=["w2b", "hid"], writes=["ps2"])
                    S.op("act", lambda e: e.copy(out=okc[r0:r0 + 64, :], in_=ps[2][0:64, 0:128]), reads=["ps2"], writes=["okc"])
                else:
                    S.op("pe", lambda e: e.matmul(ps[2][:, 0:64], lhsT=hid[:], rhs=w2b[:, 1, :], start=True, stop=True),
                         reads=["w2b", "hid"], writes=["ps2"])
                    S.op("act", lambda e: e.copy(out=ovc[:, r0:r0 + 64], in_=ps[2][:, 0:64]), reads=["ps2"], writes=["ovc"])
        S.dma("sp", o_kcT[:, :], okc[:], reads=["okc"], writes=["o_kcT"])
        S.dma("sp", o_vc[:, :], ovc[:], reads=["ovc"], writes=["o_vc"])


def host_A_inputs(c, x_cur, layer, P):
    lo = c * TPC
    x_ext = np.zeros((EXT, DM), np.float32)
    if c > 0:
        x_ext[:128] = x_cur[lo - 128:lo]
    x_ext[128:] = x_cur[lo:lo + TPC]
    inv = (150000.0 ** (-np.arange(32, dtype=np.float32) / 32)).astype(np.float32)
    ropec = np.zeros((128, 2), np.float32)
    ropec[:, 0] = inv[np.arange(128) % 32] / np.float32(2 * math.pi)
    R = np.zeros((128, 128), np.float32)
    for m in range(128):
        if m % 64 < 32:
            R[m + 32, m] = -1.0
        else:
            R[m - 32, m] = 1.0
    w1 = P["a_cmp_w1"][layer]
    w1d = np.ascontiguousarray(w1.reshape(2, 32, 64, 128).transpose(0, 2, 1, 3))
    w1d = np.concatenate([w1d, w1d], axis=1).reshape(2, 128, 32 * 128)
    peT = np.ascontiguousarray(P["a_cmp_pos"][layer].transpose(0, 2, 1))
    peT = np.concatenate([peT, peT], axis=1)
    return {
        "x_ext": x_ext,
        "prenorm_b": np.ascontiguousarray(np.broadcast_to(P["pre_norm"][layer][None, :], (128, DM))),
        "WA": np.ascontiguousarray(P["w_in"][layer][:, A_COLS]),
        "pos_b": np.ascontiguousarray(np.broadcast_to(P["positions"][0, lo:lo + TPC][None, :], (128, TPC))).astype(np.int32),
        "ropec": ropec, "ropeR": R,
        "w1d": np.ascontiguousarray(w1d), "w2": np.ascontiguousarray(P["a_cmp_w2"][layer]), "peT": np.ascontiguousarray(peT),
        **{k: v for k, v in host_ssd_params(layer, P).items() if k in ("b_cw", "b_cb", "b_par")},
    }


_NC_CACHE = {}


B_FM = ["a_q", "c_q", "d_q", "m_q", "b_x", "b_B", "b_C"]
B_TM = ["a_z", "b_z", "c_z", "d_z", "m_z", "a_gate", "b_dt"]
B_COLS = cols(*B_FM) + cols(*B_TM)
B_NFM = 30
B_TM0 = B_NFM * 128
FMT = {"a_q": 0, "c_q": 4, "d_q": 8, "m_q": 20, "b_x": 22, "b_B": 26, "b_C": 28}
MIXW = 2304
VW = 128


class PB:
    pass


def attn_unit(S, spe, spek, spo, spok, regions, PT, ptk, bias_ap, bias_reads=()):
    R = len(regions)
    half = (R + 1) // 2
    for r, mms in enumerate(regions):
        n = len(mms)
        sp, spk = (spe, spek) if r % 2 == 0 else (spo, spok)
        c0 = (r // 2) * 128
        for j, (lhsT, rhs, rd) in enumerate(mms):
            S.op("pe", lambda e: e.matmul(sp[:, c0:c0 + 128], lhsT=lhsT, rhs=rhs, start=(j == 0), stop=(j == n - 1)),
                 reads=rd, writes=[spk])
    S.op("act", lambda e: e.activation(out=PT[:, 0:half * 128], in_=spe[:, 0:half * 128], func=AF.Exp, bias=bias_ap, scale=1.0),
         reads=[spek] + list(bias_reads), writes=[ptk])
    if R > 1:
        S.op("act", lambda e: e.activation(out=PT[:, half * 128:R * 128], in_=spo[:, 0:(R - half) * 128], func=AF.Exp, bias=bias_ap, scale=1.0),
             reads=[spok] + list(bias_reads), writes=[ptk])


def attn_unit2(S, spe, spek, spo, spok, bank_mms, PT, ptk, bias_ap, bias_reads=()):
    for (sp, spk), mms in zip(((spe, spek), (spo, spok)), bank_mms):
        n = len(mms)
        for j, (lhsT, rhs, rd, three_d) in enumerate(mms):
            out = sp[:, 0:256].rearrange("p (a b) -> p a b", a=2) if three_d else sp[:, 0:256]
            S.op("pe", lambda e: e.matmul(out, lhsT=lhsT, rhs=rhs, start=(j == 0), stop=(j == n - 1)), reads=rd, writes=[spk])
    S.op("act", lambda e: e.activation(out=PT[:, 0:256], in_=spe[:, 0:256], func=AF.Exp, bias=bias_ap, scale=1.0),
         reads=[spek] + list(bias_reads), writes=[ptk])
    S.op("act", lambda e: e.activation(out=PT[:, 256:512], in_=spo[:, 0:256], func=AF.Exp, bias=bias_ap, scale=1.0),
         reads=[spok] + list(bias_reads), writes=[ptk])


def ptcol(h, R=4):
    return (h % 2) * ((R + 1) // 2) + h // 2


def finalize_T(S, nc, C, po, pok, R, osb, osbk, ptr, ptrk, otm, otmk):
    if po is not None:
        S.op("act", lambda e: e.copy(out=osb[:65, :R * 128], in_=po[:65, :R * 128]), reads=[pok], writes=[osbk])
    for r in range(R):
        S.op("pe", lambda e: e.matmul(ptr[:, r * 128:r * 128 + 65], lhsT=osb[:65, r * 128:(r + 1) * 128], rhs=C["identf"][:65, :65], start=True, stop=True),
             reads=[osbk, "identf"], writes=[ptrk])
    S.op("dve", lambda e: e.tensor_copy(out=otm[:, :R, :], in_=ptr[:, :R * 128].rearrange("p (r d) -> p r d", r=R)[:, :, 0:65]),
         reads=[ptrk], writes=[otmk])


def build_B(debug=False):
    nc = bass.Bass("TRN2", target_bir_lowering=False)
    D = lambda name, shape, dt, kind="ExternalInput": nc.dram_tensor(name, shape, dt, kind=kind).ap()
    g = PB()
    g.nc = nc
    g.x_ext = D("x_ext", [EXT, DM], F32)
    g.prenorm_b = D("prenorm_b", [128, DM], F32)
    g.postnorm_b = D("postnorm_b", [128, DM], F32)
    g.WB = D("WB", [DM, len(B_COLS)], F32)
    g.Wout = D("Wout", [MIXW, DM], F32)
    g.mem = D("mem", [256, DM], F32)
    g.mnorm_b = D("mnorm_b", [128, DM], F32)
    g.Wkv = D("Wkv", [DM, 512], F32)
    g.D = D
    g.pos_b = D("pos_b", [128, TPC], I32)
    g.ropec = D("ropec", [128, 2], F32)
    g.ropeR = D("ropeR", [128, 128], F32)
    g.kval_d = D("kval", [128, 128], F32)
    g.C_kT = D("C_kT", [2, 128, 17 * 128], BF16)
    g.C_v = D("C_v", [2, 128, 17 * VW], BF16)
    g.C_mask = D("C_mask", [2, 128, 128], F32)
    g.sinks_b = D("sinks_b", [128, 8], F32)
    g.A_ksT = D("A_ksT", [2, 128, 128 * 128], BF16)
    g.A_vs = D("A_vs", [2, 128, 128 * VW], BF16)
    g.A_kwT = D("A_kwT", [2, 128, 20 * 128], BF16)
    g.A_vw = D("A_vw", [2, 128, 20 * VW], BF16)
    g.A_kcT = D("A_kcT", [2, 128, 1024], BF16)
    g.A_vc = D("A_vc", [2, 128, 8 * VW], BF16)
    g.A_kvalc = D("A_kvalc", [128, 8], F32)
    g.A_selB = D("A_selB", [104, 128, 128], F32)
    g.A_winB = D("A_winB", [40, 128, 128], F32)
    g.A_cmpB = D("A_cmpB", [16, 16, 128, 128], F32)
    g.A_farc = D("A_farc", [128, 8], F32)
    g.A_farrow = D("A_farrow", [1, 8, 128], F32)
    g.A_wimp = D("A_wimp", [16, 128, 128], F32)
    g.A_f0 = D("A_f0", [128, 256], F32)
    g.A_mkak = D("A_mkak", [128, 4], F32)
    g.A_ewide = D("A_ewide", [128, 8256], BF16)
    g.D_kT = D("D_kT", [4, 128, 4096], BF16)
    g.D_v = D("D_v", [4, 128, 69 * 2 * VW], BF16)
    g.D_bias = D("D_bias", [48, 128, 128], F32)
    g.b_cw = D("b_cw", [128, 8, 4], F32)
    g.b_cb = D("b_cb", [128, 8], F32)
    g.b_par = D("b_par", [128, 3, 8], F32)
    g.b_dskip = D("b_dskip", [128, 512], F32)
    g.b_norm = D("b_norm", [128, 512], F32)
    g.b_Sprev = D("b_Sprev", [128, 7 * 512], F32)
    g.b_Dprev = D("b_Dprev", [128, 56], F32)
    g.x_out = D("x_out", [TPC, DM], F32, "ExternalOutput")
    g.zs = D("zs_scr", [TPC, MIXW], F32, "ExternalOutput")
    g.mix = D("mix_scr", [TPC, MIXW], BF16, "ExternalOutput")

    with ExitStack() as es:
        S = Sync(nc)
        g.S = S
        C = emit_consts(S, nc, es)
        g.C = C
        C["epsc"] = sb(es, nc, "epsc", [128, 1], F32)
        S.op("pool", lambda e: e.memset(C["epsc"][:], EPS), writes=["epsc"])
        C["zeroc"] = sb(es, nc, "zeroc", [128, 1], F32)
        S.op("pool", lambda e: e.memset(C["zeroc"][:], 0.0), writes=["zeroc"])
        g.ps = [es.enter_context(nc.psum_tensor(f"ps{i}", [128, 512], F32)) for i in range(7)]
        g.ps_bf = es.enter_context(nc.psum_tensor("ps_bf", [128, 1024], BF16))
        g.gates = sb(es, nc, "gates", [128, NT, 24], F32)
        g.dtraw = sb(es, nc, "dtraw", [128, NTT, 8], F32)
        g.kval = sb(es, nc, "kval_sb", [128, 128], F32)
        S.dma("sp", g.kval[:], g.kval_d[:, :], writes=["kval"])
        C["onesb"] = sb(es, nc, "onesb", [1, 128], BF16)
        S.op("pool", lambda e: e.memset(C["onesb"][:], 1.0), writes=["onesb"])
        farf = sb(es, nc, "farf", [1, 8, 128], F32)
        g.farrow = sb(es, nc, "farrow", [1, 8, 128], BF16)
        S.dma("sp", farf[:], g.A_farrow[:, :, :], writes=["farf"])
        S.op("dve", lambda e: e.tensor_copy(out=g.farrow[:], in_=farf[:]), reads=["farf"], writes=["farrow"])
        import os
        ph = os.environ.get("PHASES", "ZCMDBAO")
        qa = sb(es, nc, "qa", [128, 4, TPC], BF16) if "A" in ph else None
        with Scope(S, nc) as eh:
            g.hT = sb(eh, nc, "hT", [128, 8, EXT], BF16)
            emit_hT(S, nc, C, g.x_ext, g.prenorm_b, g.hT, g.ps_bf)
            if "Z" in ph:
                phase_Z(g)
            if "C" in ph:
                phase_C(g)
            if "M" in ph:
                phase_M(g)
            if "D" in ph:
                phase_D(g)
            if "B" in ph:
                phase_B(g)
            if "A" in ph:
                with Scope(S, nc) as eq:
                    proj_q(g, eq, "qa", FMT["a_q"], 4, qT=qa)
        if "A" in ph:
            phase_A(g, qa)
        if "O" in ph:
            phase_out(g)
        S.finish("sp")
    return nc


def phase_Z(g):
    S, nc = g.S, g.nc
    with Scope(S, nc) as es:
        wl = WLoader(S, nc, es, g.WB, 512, name="wz")
        zst = [sb(es, nc, f"zst{i}", [128, 512], F32) for i in range(2)]
        pi = 0
        import os
        for cb in [int(c) for c in os.environ.get('ZCB', '01234')]:
            c0 = B_TM0 + cb * 512
            w = 512 if cb < 4 else 256 + 32
            wb, wk = wl.load(c0, w)
            for t in range(NTT):
                if cb < 4 and t == 0:
                    continue
                p, pk = g.ps[pi % 2], f"ps{pi % 2}"
                b = pi % 2
                pi += 1
                mm_tm(S, p, pk, wb, wk, g.hT, t, 0, w)
                if cb == 4:
                    S.op("dve", lambda e: e.tensor_copy(out=g.dtraw[:, t, :], in_=p[:, 280:288]), reads=[pk], writes=["dtraw"])
                    if t == 0:
                        continue
                    S.op("act", lambda e: e.activation(out=g.gates[:, t - 1, :], in_=p[:, 256:280], func=AF.Tanh, scale=0.5), reads=[pk], writes=["gates"])
                    S.op("dve", lambda e: e.tensor_scalar(out=g.gates[:, t - 1, :], in0=g.gates[:, t - 1, :], scalar1=0.5, scalar2=0.5, op0=ALU.mult, op1=ALU.add),
                         reads=["gates"], writes=["gates"])
                wz = 512 if cb < 4 else 256
                S.op("act", lambda e: e.activation(out=zst[b][:, :wz], in_=p[:, :wz], func=AF.Tanh, scale=0.5), reads=[pk], writes=[f"zst{b}"])
                S.op("dve", lambda e: e.tensor_scalar(out=zst[b][:, :wz], in0=zst[b][:, :wz], scalar1=0.5, scalar2=0.5, op0=ALU.mult, op1=ALU.add),
                     reads=[f"zst{b}"], writes=[f"zst{b}"])
                S.op("dve", lambda e: e.tensor_tensor(out=zst[b][:, :wz], in0=p[:, :wz], in1=zst[b][:, :wz], op=ALU.mult),
                     reads=[f"zst{b}", pk], writes=[f"zst{b}"])
                S.dma("sp", g.zs[(t - 1) * 128:t * 128, cb * 512:cb * 512 + wz], zst[b][:, :wz], reads=[f"zst{b}"], writes=["zs_scr"])


def proj_q(g, es, name, tile0, ntiles, scale=0.125, pbase=0, qT=None):
    S, nc = g.S, g.nc
    if qT is None:
        qT = sb(es, nc, name, [128, ntiles, TPC], BF16)
    wl = WLoader(S, nc, es, g.WB, 128, name=name + "w")
    pi = 0
    for ti in range(ntiles):
        wb, wk = wl.load((tile0 + ti) * 128)
        for (t0, w) in tok_groups(128, EXT):
            p, pk = g.ps[pbase + pi % 2], f"ps{pbase + pi % 2}"
            pi += 1
            mm_fm(S, p, pk, wb, wk, g.hT, t0, w)
            S.op("act", lambda e: e.activation(out=qT[:, ti, t0 - 128:t0 - 128 + w], in_=p[:, :w], func=AF.Copy, scale=scale),
                 reads=[pk], writes=[name])
    return qT


def phase_M(g):
    S, nc, C = g.S, g.nc, g.C
    with Scope(S, nc) as es:
        memT = sb(es, nc, "memT", [128, 8, 256], BF16)
        emit_hT_named(S, nc, C, g.mem, g.mnorm_b, memT, "memT", g.ps_bf, ntt=2)
        wl = WLoader(S, nc, es, g.Wkv, 512, name="wkv")
        wb, wk = wl.load(0, 512)
        kmT = sb(es, nc, "kmT", [128, 2, 256], BF16)
        vm = sb(es, nc, "vm", [128, 2, 4, VW], BF16)
        S.op("pool", lambda e: e.memset(vm[:], 1.0), writes=["vm"])
        for ti in range(2):
            p, pk = g.ps[0], "ps0"
            for k in range(8):
                S.op("pe", lambda e: e.matmul(p[:, :256], lhsT=wb[:, k, ti * 128:(ti + 1) * 128], rhs=memT[:, k, :], start=(k == 0), stop=(k == 7)),
                     reads=[wk, "memT"], writes=[pk])
            S.op("act", lambda e: e.copy(out=kmT[:, ti, :], in_=p[:, :256]), reads=[pk], writes=["kmT"])
        for mt in range(2):
            p, pk = g.ps[1], "ps1"
            for k in range(8):
                S.op("pe", lambda e: e.matmul(p[:, :256], lhsT=memT[:, k, mt * 128:(mt + 1) * 128], rhs=wb[:, k, 256:512], start=(k == 0), stop=(k == 7)),
                     reads=[wk, "memT"], writes=[pk])
            S.op("act", lambda e: e.copy(out=vm[:, mt, :, 0:64], in_=p[:, :256].rearrange("p (h d) -> p h d", h=4)), reads=[pk], writes=["vm"])
        import os
        MSTOP = int(os.environ.get("MSTOP", "9"))
        if MSTOP < 1:
            return
        qT = proj_q(g, es, "qm", FMT["m_q"], 2)
        if MSTOP < 2:
            return
        PT = [sb(es, nc, f"PT{i}", [128, 512], BF16) for i in range(2)]
        osb = sb(es, nc, "osb", [128, 512], F32)
        otm = sb(es, nc, "otm", [128, 4, 65], F32)
        rden = sb(es, nc, "rden", [128, 4], F32)
        zt = sb(es, nc, "zt", [128, 256], F32)
        mst = sb(es, nc, "mst", [128, 256], BF16)
        ptr, ptrk = g.ps[0], "ps0"
        ui = 0
        for i in range(int(os.environ.get("MI", NT))):
            for mt in range(2):
                pt, ptk = PT[ui % 2], f"PT{ui % 2}"
                ui += 1
                regions = []
                for h in range(4):
                    r0 = 64 * (h % 2)
                    regions.append([(kmT[r0:r0 + 64, h // 2, mt * 128:(mt + 1) * 128], qT[r0:r0 + 64, h // 2, i * 128:(i + 1) * 128], ["kmT", "qm"])])
                attn_unit(S, g.ps[1], "ps1", g.ps[2], "ps2", regions, pt, ptk, C["zeroc"][:, 0:1], ["zeroc"])
                for h in range(4):
                    pc = ptcol(h)
                    S.op("pe", lambda e: e.matmul(g.ps[3 + h][:, 0:128], lhsT=vm[:, mt, h, :], rhs=pt[:, pc * 128:(pc + 1) * 128],
                                                  start=(mt == 0), stop=(mt == 1)), reads=["vm", ptk], writes=[f"ps{3 + h}"])
            if MSTOP < 3:
                continue
            for h in range(4):
                S.op("act", lambda e: e.copy(out=osb[:65, h * 128:(h + 1) * 128], in_=g.ps[3 + h][:65, 0:128]), reads=[f"ps{3 + h}"], writes=["osb"])
            finalize_T(S, nc, C, None, None, 4, osb, "osb", ptr, ptrk, otm, "otm")
            if MSTOP < 4:
                continue
            S.op("dve", lambda e: e.reciprocal(out=rden[:], in_=otm[:, :, 64]), reads=["otm"], writes=["rden"])
            S.dma("sp", zt[:], g.zs[i * 128:(i + 1) * 128, 2048:2304], reads=["zs_scr"], writes=["zt"])
            for h in range(4):
                S.op("dve", lambda e: e.scalar_tensor_tensor(out=mst[:, h * 64:(h + 1) * 64], in0=otm[:, h, 0:64], scalar=rden[:, h:h + 1],
                                                             in1=zt[:, h * 64:(h + 1) * 64], op0=ALU.mult, op1=ALU.mult),
                     reads=["otm", "rden", "zt"], writes=["mst"])
            S.dma("sp", g.mix[i * 128:(i + 1) * 128, 2048:2304], mst[:], reads=["mst"], writes=["mix_scr"])


def emit_hT_named(S, nc, C, x_ext, prenorm_b, hT, hkey, ps_bf, ntt):
    with Scope(S, nc) as es:
        pn = sb(es, nc, "pn2", [128, DM], F32)
        S.dma("sp", pn[:], prenorm_b[:, :], writes=["pn2"])
        xt = sb(es, nc, "xt2", [128, DM], F32)
        hb = sb(es, nc, "hb2", [128, DM], BF16)
        junk = sb(es, nc, "junk2", [128, DM], F32)
        ss = sb(es, nc, "ss2", [128, 1], F32)
        for t in range(ntt):
            S.dma("sp", xt[:], x_ext[t * 128:(t + 1) * 128, :], writes=["xt2"])
            S.op("pool", lambda e: e.memset(ss[:], 0.0), writes=["ss2"])
            S.op("act", lambda e: e.activation(out=junk[:], in_=xt[:], func=AF.Square, accum_out=ss[:]), reads=["xt2", "ss2"], writes=["junk2", "ss2"])
            S.op("act", lambda e: e.activation(out=ss[:], in_=ss[:], func=AF.Sqrt, bias=C["epsc"][:, 0:1], scale=1.0 / DM), reads=["ss2"], writes=["ss2"])
            S.op("dve", lambda e: e.reciprocal(out=ss[:], in_=ss[:]), reads=["ss2"], writes=["ss2"])
            S.op("dve", lambda e: e.scalar_tensor_tensor(out=hb[:], in0=xt[:], scalar=ss[:, 0:1], in1=pn[:], op0=ALU.mult, op1=ALU.mult),
                 reads=["xt2", "ss2", "pn2"], writes=["hb2"])
            for k in range(8):
                S.op("pe", lambda e: e.transpose(out=ps_bf[:, k * 128:(k + 1) * 128], in_=hb[:, k * 128:(k + 1) * 128], identity=C["identb"][:]),
                     reads=["hb2", "identb"], writes=["ps_bf"])
            S.op("act", lambda e: e.copy(out=hT[:, :, t * 128:(t + 1) * 128], in_=ps_bf[:, :].rearrange("p (k t) -> p k t", k=8)),
                 reads=["ps_bf"], writes=[hkey])


def phase_out(g):
    S, nc, C = g.S, g.nc, g.C
    with Scope(S, nc) as es:
        wo = sb(es, nc, "wo", [128, 18, DM], BF16)
        wof = [sb(es, nc, f"wof{i}", [128, DM], F32) for i in range(2)]
        for k in range(18):
            b = k % 2
            S.dma("sp", wof[b][:], g.Wout[k * 128:(k + 1) * 128, :], writes=[f"wof{b}"])
            S.op("pool", lambda e: e.tensor_copy(out=wo[:, k, :], in_=wof[b][:]), reads=[f"wof{b}"], writes=["wo"])
        pnb = sb(es, nc, "pnb", [128, DM], F32)
        S.dma("sp", pnb[:], g.postnorm_b[:, :], writes=["pnb"])
        mt_ = [sb(es, nc, f"mixt{i}", [128, MIXW], BF16) for i in range(2)]
        mT = [sb(es, nc, f"mixT{i}", [128, 18, 128], BF16) for i in range(2)]
        xt = [sb(es, nc, f"xo{i}", [128, DM], F32) for i in range(2)]
        y = [sb(es, nc, f"yo{i}", [128, DM], F32) for i in range(2)]
        junk = sb(es, nc, "junko", [128, DM], F32)
        ss = [sb(es, nc, f"sso{i}", [128, 1], F32) for i in range(2)]
        for t in range(NT):
            b = t % 2
            S.dma("sp", mt_[b][:], g.mix[t * 128:(t + 1) * 128, :], reads=["mix_scr"], writes=[f"mixt{b}"])
            S.dma("sp", xt[b][:], g.x_ext[(t + 1) * 128:(t + 2) * 128, :], writes=[f"xo{b}"])
            for half in range(3):
                k0 = half * 8
                nk = min(8, 18 - k0)
                for k in range(nk):
                    S.op("pe", lambda e: e.transpose(out=g.ps_bf[:, k * 128:(k + 1) * 128], in_=mt_[b][:, (k0 + k) * 128:(k0 + k + 1) * 128],
                                                     identity=C["identb"][:]), reads=[f"mixt{b}", "identb"], writes=["ps_bf"])
                S.op("act", lambda e: e.copy(out=mT[b][:, k0:k0 + nk, :], in_=g.ps_bf[:, :nk * 128].rearrange("p (k t) -> p k t", k=nk)),
                     reads=["ps_bf"], writes=[f"mixT{b}"])
            for nh in range(2):
                p, pk = g.ps[nh], f"ps{nh}"
                for k in range(18):
                    S.op("pe", lambda e: e.matmul(p[:, :512], lhsT=mT[b][:, k, :], rhs=wo[:, k, nh * 512:(nh + 1) * 512], start=(k == 0), stop=(k == 17)),
                         reads=[f"mixT{b}", "wo"], writes=[pk])
                S.op("act", lambda e: e.copy(out=y[b][:, nh * 512:(nh + 1) * 512], in_=p[:, :512]), reads=[pk], writes=[f"yo{b}"])
            S.op("pool", lambda e: e.memset(ss[b][:], 0.0), writes=[f"sso{b}"])
            S.op("act", lambda e: e.activation(out=junk[:], in_=y[b][:], func=AF.Square, accum_out=ss[b][:]), reads=[f"yo{b}", f"sso{b}"], writes=["junko", f"sso{b}"])
            S.op("act", lambda e: e.activation(out=ss[b][:], in_=ss[b][:], func=AF.Sqrt, bias=C["epsc"][:, 0:1], scale=1.0 / DM), reads=[f"sso{b}"], writes=[f"sso{b}"])
            S.op("dve", lambda e: e.reciprocal(out=ss[b][:], in_=ss[b][:]), reads=[f"sso{b}"], writes=[f"sso{b}"])
            S.op("dve", lambda e: e.scalar_tensor_tensor(out=y[b][:], in0=y[b][:], scalar=ss[b][:, 0:1], in1=pnb[:], op0=ALU.mult, op1=ALU.mult),
                 reads=[f"yo{b}", f"sso{b}", "pnb"], writes=[f"yo{b}"])
            S.op("pool", lambda e: e.tensor_tensor(out=y[b][:], in0=y[b][:], in1=xt[b][:], op=ALU.add), reads=[f"yo{b}", f"xo{b}"], writes=[f"yo{b}"])
            S.dma("sp", g.x_out[t * 128:(t + 1) * 128, :], y[b][:], reads=[f"yo{b}"], writes=[f"x_out{t}"])


def host_B_inputs(c, x_cur, layer, P, Aout):
    lo = c * TPC
    x_ext = np.zeros((EXT, DM), np.float32)
    if c > 0:
        x_ext[:128] = x_cur[lo - 128:lo]
    x_ext[128:] = x_cur[lo:lo + TPC]
    bc = lambda v: np.ascontiguousarray(np.broadcast_to(v[None, :], (128, v.shape[0])))
    return {
        "x_ext": x_ext,
        "prenorm_b": bc(P["pre_norm"][layer]), "postnorm_b": bc(P["post_norm"][layer]),
        "WB": np.ascontiguousarray(P["w_in"][layer][:, B_COLS]),
        "Wout": np.ascontiguousarray(P["w_out"][layer]),
        "mem": np.ascontiguousarray(P["mem"][0]), "mnorm_b": bc(P["m_norm"][layer]),
        "Wkv": np.ascontiguousarray(P["m_w_kv"][layer]),
        **host_B_attn_inputs(c, layer, P, Aout),
        **host_B_A_inputs(c, layer, P, Aout),
        **host_B_D_inputs(c, layer, P, Aout),
        **host_ssd_params(layer, P),
        **host_B_ssd_chain(c, Aout),
    }


def t5_bucket_np(dist):
    dist = np.maximum(dist, 0)
    rel = np.maximum(dist, 16).astype(np.float32)
    large = 16 + (np.log(rel / np.float32(16)) / np.float32(math.log(2048 / 16)) * np.float32(16)).astype(np.int32)
    return np.where(dist < 16, dist, np.minimum(large, 31)).astype(np.int64)


def toeplitz_tile(table_col, dist, allowed):
    out = np.full(dist.shape, NEG, np.float32)
    if table_col is None:
        out[allowed] = 0.0
    else:
        out[allowed] = table_col[t5_bucket_np(dist)][allowed]
    return out


_KI = np.arange(128)[:, None]
_QI = np.arange(128)[None, :]


def gqa_run(g, units, PT, po, pok, first_bufs=None):
    S = g.S
    n = len(units)

    def stage1(un):
        b = g.ui % 2
        g.ui += 1
        spe, spek, spo, spok = g.ps[2 * b], f"ps{2 * b}", g.ps[2 * b + 1], f"ps{2 * b + 1}"
        if "banks" in un:
            attn_unit2(S, spe, spek, spo, spok, un["banks"], PT[b], f"PT{b}", un["bias"][0], un["bias"][1])
        else:
            attn_unit(S, spe, spek, spo, spok, un["regions"], PT[b], f"PT{b}", un["bias"][0], un["bias"][1])
        return b

    nxt = stage1(units[0])
    for u, un in enumerate(units):
        b = nxt
        if u + 1 < n:
            nxt = stage1(units[u + 1])
        pt, ptk = PT[b], f"PT{b}"
        vap, vreads = un["v"]
        S.op("pe", lambda e: e.matmul(po[:, 0:512], lhsT=vap, rhs=pt[:, 0:512], start=(u == 0), stop=(u == n - 1)),
             reads=list(vreads) + [ptk], writes=[pok])


def load_bias_tiles(g, es, name, dram, n):
    S, nc = g.S, g.nc
    t = sb(es, nc, name, [128, n, 128], BF16)
    with Scope(S, nc) as es2:
        st = [sb(es2, nc, f"{name}_st{i}", [128, 8, 128], F32) for i in range(2)]
        for j0 in range(0, n, 8):
            b = (j0 // 8) % 2
            m = min(8, n - j0)
            S.dma("sp", st[b][:, :m, :], dram[j0:j0 + m, :, :].rearrange("n p q -> p n q"), writes=[f"{name}_st{b}"])
            S.op("pool", lambda e: e.tensor_copy(out=t[:, j0:j0 + m, :], in_=st[b][:, :m, :]), reads=[f"{name}_st{b}"], writes=[name])
    return t


def load_bf(g, es, name, dram, shape):
    t = sb(es, g.nc, name, shape, BF16)
    g.S.dma("sp", t[:], dram, writes=[name])
    return t


def head_regions(kT, kkey, kcols, qT, qkey, gi, i, extra=None):
    regions = []
    for h in range(4):
        r0 = 64 * (h % 2)
        mm = [(kT[r0:r0 + 64, kcols], qT[r0:r0 + 64, 2 * gi + h // 2, i * 128:(i + 1) * 128], [kkey, qkey])]
        if extra is not None:
            mm += extra(h)
        regions.append(mm)
    return regions


def phase_C(g):
    S, nc, C = g.S, g.nc, g.C
    D = g.D
    with Scope(S, nc) as es:
        cosT = sb(es, nc, "cosT", [128, TPC], F32)
        sinT = sb(es, nc, "sinT", [128, TPC], F32)
        rc = sb(es, nc, "rc", [128, 2], F32)
        rR = sb(es, nc, "rR", [128, 128], F32)
        S.dma("sp", rc[:], g.ropec[:, :], writes=["rc"])
        S.dma("sp", rR[:], g.ropeR[:, :], writes=["rR"])
        emit_rope_tables(S, nc, g.pos_b, rc, cosT, sinT)
        qT = sb(es, nc, "qc", [128, 4, TPC], BF16)
        wl = WLoader(S, nc, es, g.WB, 128, name="qcw")
        xf = sb(es, nc, "xf", [128, 512], F32)
        pi = 0
        for ti in range(4):
            wb, wk = wl.load((FMT["c_q"] + ti) * 128)
            for (t0, w) in tok_groups(128, EXT):
                p, pk = g.ps[pi % 2], f"ps{pi % 2}"
                pi += 1
                mm_fm(S, p, pk, wb, wk, g.hT, t0, w)
                l0 = t0 - 128
                emit_rope_apply(S, nc, p, pk, w, xf, rR, cosT, sinT, l0, g.ps[2], "ps2", qT[:, ti, l0:l0 + w], "qc", scale=1.0)
        S.op("pool", lambda e: e.tensor_scalar(out=qT[:], in0=qT[:], scalar1=0.125, scalar2=None, op0=ALU.mult), reads=["qc"], writes=["qc"])
        kT = [load_bf(g, es, f"kcT{gi}", g.C_kT[gi, :, :], [128, 17 * 128]) for gi in range(2)]
        vv = [load_bf(g, es, f"vcc{gi}", g.C_v[gi, :, :], [128, 17 * VW]) for gi in range(2)]
        cm = load_bias_tiles(g, es, "cmask", g.C_mask, 2)
        sinkf = sb(es, nc, "sinkf", [128, 8], F32)
        S.dma("sp", sinkf[:], g.sinks_b[:, :], writes=["sinkf"])
        S.op("act", lambda e: e.activation(out=sinkf[:], in_=sinkf[:], func=AF.Exp), reads=["sinkf"], writes=["sinkf"])
        PT = [sb(es, nc, f"PT{i}", [128, 512], BF16) for i in range(2)]
        osb = sb(es, nc, "osb", [128, 512], F32)
        otm = sb(es, nc, "otm", [128, 4, 65], F32)
        rden = sb(es, nc, "rden", [128, 4], F32)
        zt = sb(es, nc, "zt", [128, 512], F32)
        mst = sb(es, nc, "mst", [128, 512], BF16)
        po, pok, ptr, ptrk = g.ps[4], "ps4", g.ps[5], "ps5"
        g.ui = 0
        for i in range(NT):
            S.dma("sp", zt[:], g.zs[i * 128:(i + 1) * 128, 1024:1536], reads=["zs_scr"], writes=["zt"])
            for gi in range(2):
                units = []
                for w in (1, 0):
                    T = 1 + i - w
                    ext = lambda h, w=w: [(C["identb"][:, :], cm[:, w, :], ["identb", "cmask"])]
                    units.append(dict(regions=head_regions(kT[gi], f"kcT{gi}", slice(T * 128, (T + 1) * 128), qT, "qc", gi, i, ext),
                                      bias=(g.kval[:, 111 + T:112 + T], ["kval"]),
                                      v=(vv[gi][:, T * VW:(T + 1) * VW], [f"vcc{gi}"])))
                gqa_run(g, units, PT, po, pok)
                finalize_T(S, nc, C, po, pok, 4, osb, "osb", ptr, ptrk, otm, "otm")
                for h in range(4):
                    pc = ptcol(h)
                    hh = 4 * gi + h
                    S.op("dve", lambda e: e.tensor_tensor(out=rden[:, h:h + 1], in0=otm[:, pc, 64:65], in1=sinkf[:, hh:hh + 1], op=ALU.add),
                         reads=["otm", "sinkf"], writes=["rden"])
                S.op("dve", lambda e: e.reciprocal(out=rden[:], in_=rden[:]), reads=["rden"], writes=["rden"])
                for h in range(4):
                    pc = ptcol(h)
                    hh = 4 * gi + h
                    S.op("dve", lambda e: e.scalar_tensor_tensor(out=mst[:, hh * 64:(hh + 1) * 64], in0=otm[:, pc, 0:64], scalar=rden[:, h:h + 1],
                                                                 in1=zt[:, hh * 64:(hh + 1) * 64], op0=ALU.mult, op1=ALU.mult),
                         reads=["otm", "rden", "zt"], writes=["mst"])
            S.dma("sp", g.mix[i * 128:(i + 1) * 128, 1024:1536], mst[:], reads=["mst"], writes=["mix_scr"])


def gather_A(res):
    G = {}
    G["fm"] = np.concatenate([np.asarray(r["o_fm"]) for r in res], axis=2)
    G["tm"] = np.concatenate([np.asarray(r["o_tm"]) for r in res], axis=0)
    G["kcT"] = np.concatenate([np.asarray(r["o_kcT"]) for r in res], axis=1)
    G["vc"] = np.concatenate([np.asarray(r["o_vc"]) for r in res], axis=0)
    G["S"] = [np.asarray(r["o_S"]) for r in res]
    G["D"] = [np.asarray(r["o_D"]) for r in res]
    return G


def rope_consts():
    inv = (150000.0 ** (-np.arange(32, dtype=np.float32) / 32)).astype(np.float32)
    ropec = np.zeros((128, 2), np.float32)
    ropec[:, 0] = inv[np.arange(128) % 32] / np.float32(2 * math.pi)
    R = np.zeros((128, 128), np.float32)
    for m in range(128):
        if m % 64 < 32:
            R[m + 32, m] = -1.0
        else:
            R[m - 32, m] = 1.0
    return ropec, R


def kT_window(G, fm_idx, gi, c, first_slot, dup=True):
    n = 128 - first_slot
    out = np.zeros((128, n * 128), NPBF)
    for s in range(n):
        Tg = first_slot + s - 16 * (7 - c)
        if Tg < 0:
            continue
        blk = G["fm"][fm_idx][:, Tg * 128:(Tg + 1) * 128]
        if dup:
            out[0:64, s * 128:(s + 1) * 128] = blk[64 * gi:64 * gi + 64]
            out[64:128, s * 128:(s + 1) * 128] = blk[64 * gi:64 * gi + 64]
        else:
            out[:, s * 128:(s + 1) * 128] = blk
    return out


def v_window(G, col0, c, first_slot):
    n = 128 - first_slot
    out = np.zeros((128, n, VW), NPBF)
    out[:, :, 64] = 1.0
    for s in range(n):
        Tg = first_slot + s - 16 * (7 - c)
        if Tg < 0:
            continue
        out[:, s, 0:64] = G["tm"][Tg * 128:(Tg + 1) * 128, col0:col0 + 64]
    return out


def host_B_attn_inputs(c, layer, P, G):
    lo = c * TPC
    ropec, R = rope_consts()
    d = {}
    d["pos_b"] = np.ascontiguousarray(np.broadcast_to(P["positions"][0, lo:lo + TPC][None, :], (128, TPC))).astype(np.int32)
    d["ropec"], d["ropeR"] = ropec, R
    kval = np.zeros((128, 128), np.float32)
    kval[:, :16 * (7 - c)] = NEG
    d["kval"] = kval
    d["C_kT"] = np.stack([kT_window(G, 2, gi, c, 111) for gi in range(2)])
    d["C_v"] = np.stack([v_window(G, 256 + 64 * gi, c, 111).reshape(128, -1) for gi in range(2)])
    cm = np.stack([toeplitz_tile(None, _QI - _KI, (_QI - _KI) >= 0), toeplitz_tile(None, 128 + _QI - _KI, (128 + _QI - _KI) <= 127)])
    d["C_mask"] = cm
    d["sinks_b"] = np.ascontiguousarray(np.broadcast_to(P["c_sinks"][layer][None, :], (128, 8)))
    return d


def phase_A(g, qa):
    S, nc, C = g.S, g.nc, g.C
    with Scope(S, nc) as es:
        ew = load_bf(g, es, "ewide", g.A_ewide[:, :], [128, 8256])
        wimp = load_bias_tiles(g, es, "wimp", g.A_wimp, 16)
        farc = sb(es, nc, "farc", [128, 8], F32)
        S.dma("sp", farc[:], g.A_farc[:, :], writes=["farc"])
        f0 = sb(es, nc, "f0", [128, 256], F32)
        S.dma("sp", f0[:], g.A_f0[:, :], writes=["f0"])
        mkak = sb(es, nc, "mkak", [128, 4], F32)
        S.dma("sp", mkak[:], g.A_mkak[:, :], writes=["mkak"])
        kvalc = sb(es, nc, "kvalc", [128, 8], F32)
        S.dma("sp", kvalc[:], g.A_kvalc[:, :], writes=["kvalc"])
        PT = [sb(es, nc, f"PT{i}", [128, 512], BF16) for i in range(2)]
        PTc = sb(es, nc, "PTc", [128, 8, 512], BF16)
        osb = sb(es, nc, "osb", [128, 512], F32)
        otm = [sb(es, nc, f"otm{b}", [128, 4, 65], F32) for b in range(3)]
        rden = sb(es, nc, "rden", [128, 3, 4], F32)
        acc = sb(es, nc, "acc_a", [128, 64], F32)
        imp = sb(es, nc, "imp", [128, 256], F32)
        imp2 = sb(es, nc, "imp2", [128, 256], F32)
        mx = sb(es, nc, "mx", [128, 16], F32)
        negm = sb(es, nc, "negm", [128, 256], F32)
        nmn = sb(es, nc, "nmn", [128, 2, 128], BF16)
        nmf = sb(es, nc, "nmf", [128, 2, 4, 128], BF16)
        cst = sb(es, nc, "cst", [128, 8, 128], F32)
        cbt = sb(es, nc, "cbt", [128, 8, 128], BF16)
        zt = sb(es, nc, "zt", [128, 512], F32)
        mst = sb(es, nc, "mst", [128, 512], BF16)
        po, pok, ptr, ptrk, pu, puk = g.ps[4], "ps4", g.ps[5], "ps5", g.ps[6], "ps6"
        g.ui = 0
        for gi in range(2):
            with Scope(S, nc) as eg:
                ksT = load_bf(g, eg, f"ksT{gi}", g.A_ksT[gi, :, :], [128, 128 * 128])
                vs = load_bf(g, eg, f"vs{gi}", g.A_vs[gi, :, :], [128, 128 * VW])
                kwT = load_bf(g, eg, f"kwT{gi}", g.A_kwT[gi, :, :], [128, 20 * 128])
                vw = load_bf(g, eg, f"vw{gi}", g.A_vw[gi, :, :], [128, 20 * VW])
                kcT = load_bf(g, eg, f"kcT{gi}", g.A_kcT[gi, :, :], [128, 1024])
                vc = load_bf(g, eg, f"vc{gi}", g.A_vc[gi, :, :], [128, 8 * VW])
                selB = load_bias_tiles(g, eg, f"selB{gi}", g.A_selB[gi * 52:(gi + 1) * 52, :, :], 52)
                winB = load_bias_tiles(g, eg, f"winB{gi}", g.A_winB[gi * 20:(gi + 1) * 20, :, :], 20)
                for i in range(int(os.environ.get("AI", NT))):
                    S.dma("sp", cst[:], g.A_cmpB[i, gi * 8:(gi + 1) * 8, :, :].rearrange("n p q -> p n q"), writes=["cst"])
                    S.op("pool", lambda e: e.tensor_copy(out=cbt[:], in_=cst[:]), reads=["cst"], writes=["cbt"])
                    for m in range(8):
                        b = g.ui % 2
                        g.ui += 1
                        if m >= 6:
                            ext = lambda h, m=m: [(C["identb"][:, :], cbt[:, h * 2 + (m - 6), :], ["identb", "cbt"])]
                        else:
                            ext = lambda h: [(C["onesb"][0:1, :], g.farrow[0:1, gi * 4 + h, :], ["onesb", "farrow"])]
                        regions = head_regions(kcT, f"kcT{gi}", slice(m * 128, (m + 1) * 128), qa, "qa", gi, i, ext)
                        attn_unit(S, g.ps[2 * b], f"ps{2 * b}", g.ps[2 * b + 1], f"ps{2 * b + 1}", regions, PTc[:, m, :], "PTc",
                                  kvalc[:, m:m + 1], ["kvalc"])
                        S.op("pe", lambda e: e.matmul(po[:, 0:512], lhsT=vc[:, m * VW:(m + 1) * VW], rhs=PTc[:, m, :], start=(m == 0), stop=(m == 7)),
                             reads=[f"vc{gi}", "PTc"], writes=[pok])
                    finalize_T(S, nc, C, po, pok, 4, osb, "osb", ptr, ptrk, otm[0], "otm0")
                    S.op("dve", lambda e: e.tensor_scalar(out=rden[:, 0, :], in0=otm[0][:, :, 64], scalar1=1e-30, scalar2=None, op0=ALU.max), reads=["otm0"], writes=["rden"])
                    S.op("dve", lambda e: e.reciprocal(out=rden[:, 0, :], in_=rden[:, 0, :]), reads=["rden"], writes=["rden"])
                    for pair in range(2):
                        for hp in range(2):
                            pc = pair * 2 + hp
                            for m in range(8):
                                S.op("pe", lambda e: e.matmul(pu[:, hp * 256:(hp + 1) * 256], lhsT=PTc[:, m, pc * 128:(pc + 1) * 128],
                                                              rhs=wimp[:, 2 * m:2 * m + 2, :].rearrange("p a b -> p (a b)"), start=(m == 0), stop=(m == 7)),
                                     reads=["PTc", "wimp"], writes=[puk])
                        for hp in range(2):
                            pc = pair * 2 + hp
                            if pc == 0:
                                S.op("dve", lambda e: e.scalar_tensor_tensor(out=imp[:], in0=pu[:, 0:256], scalar=rden[:, 0, 0:1], in1=f0[:],
                                                                             op0=ALU.mult, op1=ALU.add), reads=[puk, "rden", "f0"], writes=["imp"])
                            else:
                                S.op("dve", lambda e: e.scalar_tensor_tensor(out=imp[:], in0=pu[:, hp * 256:(hp + 1) * 256], scalar=rden[:, 0, pc:pc + 1],
                                                                             in1=imp[:], op0=ALU.mult, op1=ALU.add), reads=[puk, "rden", "imp"], writes=["imp"])
                    c0 = 224 + 2 * i
                    S.op("dve", lambda e: e.tensor_tensor(out=imp[:, c0:c0 + 2], in0=imp[:, c0:c0 + 2], in1=mkak[:, 0:2], op=ALU.mult),
                         reads=["imp", "mkak"], writes=["imp"])
                    S.op("dve", lambda e: e.tensor_tensor(out=imp[:, c0:c0 + 2], in0=imp[:, c0:c0 + 2], in1=mkak[:, 2:4], op=ALU.add),
                         reads=["imp", "mkak"], writes=["imp"])
                    if c0 + 2 < 256:
                        S.op("dve", lambda e: e.memset(imp[:, c0 + 2:256], -1.0), reads=["imp"], writes=["imp"])
                    S.op("dve", lambda e: e.max(out=mx[:, 0:8], in_=imp[:]), reads=["imp"], writes=["mx"])
                    S.op("dve", lambda e: e.match_replace(out=imp2[:], in_to_replace=mx[:, 0:8], in_values=imp[:], imm_value=-1e9),
                         reads=["imp", "mx"], writes=["imp2"])
                    S.op("dve", lambda e: e.max(out=mx[:, 8:16], in_=imp2[:]), reads=["imp2"], writes=["mx"])
                    S.op("dve", lambda e: e.tensor_scalar(out=negm[:], in0=imp[:], scalar1=mx[:, 15:16], scalar2=1.0, op0=ALU.is_ge, op1=ALU.subtract),
                         reads=["imp", "mx"], writes=["negm"])
                    for half in range(2):
                        S.op("pe", lambda e: e.matmul(pu[:, half * 128:(half + 1) * 128], lhsT=negm[:, half * 128:(half + 1) * 128], rhs=C["identf"][:, :],
                                                      start=True, stop=True), reads=["negm", "identf"], writes=[puk])
                    S.op("act", lambda e: e.activation(out=nmn[:].rearrange("p a b -> p (a b)"), in_=pu[:, 0:256], func=AF.Copy, scale=-NEG),
                         reads=[puk], writes=["nmn"])
                    for h in range(4):
                        pc = ptcol(h)
                        S.op("act", lambda e: e.activation(out=nmf[:, :, pc, :], in_=pu[:, 0:256].rearrange("p (a b) -> p a b", a=2), func=AF.Identity,
                                                           bias=farc[:, gi * 4 + h:gi * 4 + h + 1], scale=-NEG), reads=[puk, "farc"], writes=["nmf"])
                    units = []
                    nkt = 113 + i
                    for kt in range(int(os.environ.get("KT0", 0)), nkt):
                        delta = 112 + i - kt
                        half, r = (2 * kt) // 128, (2 * kt) % 128
                        if delta <= 12:
                            ext = lambda h, delta=delta, half=half, r=r: [(ew[:, 64 * r:64 * r + 128], nmn[:, half, :], ["ewide", "nmn"]),
                                                                           (C["identb"][:, :], selB[:, h * 13 + delta, :], ["identb", f"selB{gi}"])]
                        else:
                            banks = []
                            for par in range(2):
                                r0 = 64 * par
                                banks.append([(ksT[r0:r0 + 64, kt * 128:(kt + 1) * 128], qa[r0:r0 + 64, 2 * gi:2 * gi + 2, i * 128:(i + 1) * 128], [f"ksT{gi}", "qa"], True),
                                              (ew[:, 64 * r:64 * r + 128], nmf[:, half, 2 * par:2 * par + 2, :].rearrange("p a b -> p (a b)"), ["ewide", "nmf"], False)])
                            units.append(dict(banks=banks, bias=(g.kval[:, kt:kt + 1], ["kval"]), v=(vs[:, kt * VW:(kt + 1) * VW], [f"vs{gi}"])))
                            continue
                        units.append(dict(regions=head_regions(ksT, f"ksT{gi}", slice(kt * 128, (kt + 1) * 128), qa, "qa", gi, i, ext),
                                          bias=(g.kval[:, kt:kt + 1], ["kval"]), v=(vs[:, kt * VW:(kt + 1) * VW], [f"vs{gi}"])))
                    gqa_run(g, units, PT, po, pok)
                    finalize_T(S, nc, C, po, pok, 4, osb, "osb", ptr, ptrk, otm[1], "otm1")
                    S.op("dve", lambda e: e.tensor_scalar(out=rden[:, 1, :], in0=otm[1][:, :, 64], scalar1=1e-30, scalar2=None, op0=ALU.max), reads=["otm1"], writes=["rden"])
                    S.op("dve", lambda e: e.reciprocal(out=rden[:, 1, :], in_=rden[:, 1, :]), reads=["rden"], writes=["rden"])
                    units = []
                    for w in (4, 3, 2, 1, 0):
                        T = 4 + i - w
                        ext = lambda h, w=w: [(C["identb"][:, :], winB[:, h * 5 + w, :], ["identb", f"winB{gi}"])]
                        units.append(dict(regions=head_regions(kwT, f"kwT{gi}", slice(T * 128, (T + 1) * 128), qa, "qa", gi, i, ext),
                                          bias=(g.kval[:, 108 + T:109 + T], ["kval"]), v=(vw[:, T * VW:(T + 1) * VW], [f"vw{gi}"])))
                    gqa_run(g, units, PT, po, pok)
                    finalize_T(S, nc, C, po, pok, 4, osb, "osb", ptr, ptrk, otm[2], "otm2")
                    S.op("dve", lambda e: e.tensor_scalar(out=rden[:, 2, :], in0=otm[2][:, :, 64], scalar1=1e-30, scalar2=None, op0=ALU.max), reads=["otm2"], writes=["rden"])
                    S.op("dve", lambda e: e.reciprocal(out=rden[:, 2, :], in_=rden[:, 2, :]), reads=["rden"], writes=["rden"])
                    S.dma("sp", zt[:, 0:256], g.zs[i * 128:(i + 1) * 128, gi * 256:(gi + 1) * 256], reads=["zs_scr"], writes=["zt"])
                    for h in range(4):
                        pc = ptcol(h)
                        hh = 4 * gi + h
                        for br in range(3):
                            S.op("dve", lambda e: e.tensor_tensor(out=rden[:, br, pc:pc + 1], in0=rden[:, br, pc:pc + 1],
                                                                  in1=g.gates[:, i, hh * 3 + br:hh * 3 + br + 1], op=ALU.mult),
                                 reads=["rden", "gates"], writes=["rden"])
                        S.op("dve", lambda e: e.tensor_scalar(out=acc[:], in0=otm[0][:, pc, 0:64], scalar1=rden[:, 0, pc:pc + 1], scalar2=None, op0=ALU.mult),
                             reads=["otm0", "rden"], writes=["acc_a"])
                        for br in (1, 2):
                            S.op("dve", lambda e: e.scalar_tensor_tensor(out=acc[:], in0=otm[br][:, pc, 0:64], scalar=rden[:, br, pc:pc + 1], in1=acc[:],
                                                                         op0=ALU.mult, op1=ALU.add), reads=[f"otm{br}", "rden", "acc_a"], writes=["acc_a"])
                        S.op("dve", lambda e: e.tensor_tensor(out=mst[:, h * 64:(h + 1) * 64], in0=acc[:], in1=zt[:, h * 64:(h + 1) * 64], op=ALU.mult),
                             reads=["acc_a", "zt"], writes=["mst"])
                    S.dma("sp", g.mix[i * 128:(i + 1) * 128, gi * 256:(gi + 1) * 256], mst[:, 0:256], reads=["mst"], writes=["mix_scr"])


_HOST_CACHE = {}


def _host_A_static(rel_a):
    d = {}
    selB = np.zeros((8, 13, 128, 128), np.float32)
    winB = np.zeros((8, 5, 128, 128), np.float32)
    for h in range(8):
        for dl in range(13):
            dist = 128 * dl + _QI - _KI
            selB[h, dl] = toeplitz_tile(rel_a[:, h], dist, dist >= 0)
        for w in range(5):
            dist = 128 * w + _QI - _KI
            winB[h, w] = toeplitz_tile(rel_a[:, h], dist, (dist >= 0) & (dist < 512))
    d["A_selB"] = selB.reshape(104, 128, 128)
    d["A_winB"] = winB.reshape(40, 128, 128)
    cmpB = np.zeros((16, 8, 2, 128, 128), np.float32)
    for i in range(16):
        for mi, m in enumerate((6, 7)):
            dist = 14336 + 128 * i + _QI - 2048 * m - 16 * _KI - 15
            for h in range(8):
                cmpB[i, h, mi] = toeplitz_tile(rel_a[:, h], dist, dist >= 0)
    d["A_cmpB"] = cmpB.reshape(16, 16, 128, 128)
    d["A_farc"] = np.ascontiguousarray(np.broadcast_to(rel_a[31][None, :], (128, 8)))
    d["A_farrow"] = np.ascontiguousarray(np.broadcast_to(rel_a[31][None, :, None], (1, 8, 128))).astype(np.float32)
    wimp = np.zeros((1024, 256), np.float32)
    wt = {-1: 1.0, 0: 2.0, 1: 2.0, 2: 2.0, 3: 1.0}
    for p_ in range(1024):
        j = p_ - 1
        for b in range(256):
            k = j - 4 * b
            if k in wt:
                wimp[p_, b] = wt[k]
    d["A_wimp"] = np.ascontiguousarray(wimp.reshape(8, 128, 2, 128).transpose(0, 2, 1, 3).reshape(16, 128, 128))
    return d


def host_B_A_inputs(c, layer, P, G):
    rel_a = P["rel_bias"][:, :8]
    d = {}
    if "A_static" not in _HOST_CACHE:
        _HOST_CACHE["A_static"] = _host_A_static(rel_a)
    d.update(_HOST_CACHE["A_static"])
    d["A_ksT"] = np.stack([kT_window(G, 0, gi, c, 0) for gi in range(2)])
    d["A_vs"] = np.stack([v_window(G, 0 + 64 * gi, c, 0).reshape(128, -1) for gi in range(2)])
    d["A_kwT"] = np.stack([kT_window(G, 1, gi, c, 108) for gi in range(2)])
    d["A_vw"] = np.stack([v_window(G, 128 + 64 * gi, c, 108).reshape(128, -1) for gi in range(2)])
    sh = 128 * (7 - c)
    kc = np.zeros((2, 128, 1024), NPBF)
    vc = np.zeros((2, 128, 8, VW), NPBF)
    vc[:, :, :, 64] = 1.0
    n = 1024 - sh
    for gi in range(2):
        kc[gi, 0:64, sh:] = G["kcT"][64 * gi:64 * gi + 64, :n]
        kc[gi, 64:128, sh:] = G["kcT"][64 * gi:64 * gi + 64, :n]
        vfull = np.zeros((1024, 64), NPBF)
        vfull[sh:] = G["vc"][:n, 64 * gi:64 * gi + 64]
        vc[gi, :, :, 0:64] = vfull.reshape(8, 128, 64).transpose(1, 0, 2)
    d["A_kcT"] = kc
    d["A_vc"] = vc.reshape(2, 128, -1)
    kvalc = np.zeros((128, 8), np.float32)
    pos = (np.arange(8)[None, :] * 128 + np.arange(128)[:, None])
    kvalc[pos <= sh] = NEG
    d["A_kvalc"] = kvalc
    f0 = np.zeros((128, 256), np.float32)
    f0[:, 32 * (7 - c)] = 1e4
    d["A_f0"] = f0
    mkak = np.zeros((128, 4), np.float32)
    mkak[64:, 0] = 1.0
    mkak[:64, 2] = 1e4
    mkak[:64, 3] = -1.0
    mkak[64:, 3] = 1e4
    d["A_mkak"] = mkak
    ewide = np.zeros((128, 8256), NPBF)
    xx = np.arange(8256)
    for b in range(128):
        ewide[b, (xx // 64) == b] = 1.0
    d["A_ewide"] = ewide
    return d


def phase_D(g):
    S, nc, C = g.S, g.nc, g.C
    with Scope(S, nc) as es:
        DB = load_bias_tiles(g, es, "DB", g.D_bias, 48)
        PT = [sb(es, nc, f"PT{i}", [128, 256], BF16) for i in range(2)]
        OT = sb(es, nc, "OT", [128, 2, TPC], F32)
        osb = sb(es, nc, "osb", [128, 128], F32)
        otm = sb(es, nc, "otm", [128, 65], F32)
        rden = sb(es, nc, "rden", [128, 1], F32)
        zt = sb(es, nc, "zt", [128, 128], F32)
        mst = sb(es, nc, "mst", [128, 128], BF16)
        ptr, ptrk = g.ps[6], "ps6"
        g.ui = 0
        for t in range(4):
            with Scope(S, nc) as et:
                qd = sb(et, nc, f"qd{t}", [128, 3, TPC], BF16)
                wl = WLoader(S, nc, et, g.WB, 128, name=f"qdw{t}")
                pi = 0
                for p_ in range(3):
                    wb, wk = wl.load((FMT["d_q"] + p_ * 4 + t) * 128)
                    for (t0, w) in tok_groups(128, EXT):
                        pp, pk = g.ps[pi % 2], f"ps{pi % 2}"
                        pi += 1
                        mm_fm(S, pp, pk, wb, wk, g.hT, t0, w)
                        S.op("act", lambda e: e.activation(out=qd[:, p_, t0 - 128:t0 - 128 + w], in_=pp[:, :w], func=AF.Copy, scale=0.125),
                             reads=[pk], writes=[f"qd{t}"])
                kd = load_bf(g, et, f"kd{t}", g.D_kT[t, :, :], [128, 4096])
                vd = load_bf(g, et, f"vd{t}", g.D_v[t, :, :], [128, 69 * 2 * VW])

                def run_set(p_, qsl, keys, first):
                    n = len(keys)
                    for u, (ksl, vt, w, kvc) in enumerate(keys):
                        b = g.ui % 2
                        g.ui += 1
                        regions = []
                        for h in range(2):
                            r0 = 64 * h
                            regions.append([(kd[r0:r0 + 64, ksl], qd[r0:r0 + 64, p_, qsl], [f"kd{t}", f"qd{t}"]),
                                            (C["identb"][:, :], DB[:, (p_ * 8 + 2 * t + h) * 2 + w, :], ["identb", "DB"])])
                        attn_unit(S, g.ps[2 * b], f"ps{2 * b}", g.ps[2 * b + 1], f"ps{2 * b + 1}", regions, PT[b], f"PT{b}",
                                  g.kval[:, kvc:kvc + 1], ["kval"])
                        for h in range(2):
                            v0 = (vt * 2 + h) * VW
                            S.op("pe", lambda e: e.matmul(g.ps[4 + h][:, 0:128], lhsT=vd[:, v0:v0 + VW], rhs=PT[b][:, h * 128:(h + 1) * 128],
                                                          start=(u == 0), stop=(u == n - 1)), reads=[f"vd{t}", f"PT{b}"], writes=[f"ps{4 + h}"])
                    for h in range(2):
                        if first:
                            S.op("act", lambda e: e.copy(out=OT[:, h, qsl], in_=g.ps[4 + h][:, 0:128]), reads=[f"ps{4 + h}"], writes=["OT"])
                        else:
                            S.op("dve", lambda e: e.tensor_tensor(out=OT[:, h, qsl], in0=g.ps[4 + h][:, 0:128], in1=OT[:, h, qsl], op=ALU.add),
                                 reads=[f"ps{4 + h}", "OT"], writes=["OT"])

                for i in range(NT):
                    keys = [(slice((16 + i - w) * 128, (17 + i - w) * 128), 1 + i - w, w, 112 + i - w) for w in (1, 0)]
                    run_set(0, slice(i * 128, (i + 1) * 128), keys, True)
                for U in range(4):
                    for rho in range(4):
                        q0 = 512 * U + rho
                        keys = []
                        for w in (1, 0):
                            k0 = 2048 + 512 * (U - w) + rho
                            keys.append((slice(k0, k0 + 4 * 127 + 1, 4), 17 + (U - w + 1) * 4 + rho, w, 112 if U - w >= 0 else 111))
                        run_set(1, slice(q0, q0 + 4 * 127 + 1, 4), keys, False)
                for r in range(16):
                    keys = []
                    for w in (1, 0):
                        k0 = 2048 * (1 - w) + r
                        keys.append((slice(k0, k0 + 16 * 127 + 1, 16), 37 + (1 - w) * 16 + r, w, 112 if w == 0 else 111))
                    run_set(2, slice(r, r + 16 * 127 + 1, 16), keys, False)
                for i in range(NT):
                    S.dma("sp", zt[:], g.zs[i * 128:(i + 1) * 128, 1536 + t * 128:1536 + (t + 1) * 128], reads=["zs_scr"], writes=["zt"])
                    for h in range(2):
                        S.op("pe", lambda e: e.matmul(ptr[:, 0:65], lhsT=OT[:65, h, i * 128:(i + 1) * 128], rhs=C["identf"][:65, :65], start=True, stop=True),
                             reads=["OT", "identf"], writes=[ptrk])
                        S.op("dve", lambda e: e.tensor_copy(out=otm[:], in_=ptr[:, 0:65]), reads=[ptrk], writes=["otm"])
                        S.op("dve", lambda e: e.reciprocal(out=rden[:], in_=otm[:, 64:65]), reads=["otm"], writes=["rden"])
                        S.op("dve", lambda e: e.scalar_tensor_tensor(out=mst[:, h * 64:(h + 1) * 64], in0=otm[:, 0:64], scalar=rden[:, 0:1],
                                                                     in1=zt[:, h * 64:(h + 1) * 64], op0=ALU.mult, op1=ALU.mult),
                             reads=["otm", "rden", "zt"], writes=["mst"])
                    S.dma("sp", g.mix[i * 128:(i + 1) * 128, 1536 + t * 128:1536 + (t + 1) * 128], mst[:], reads=["mst"], writes=["mix_scr"])


def host_B_D_inputs(c, layer, P, G):
    rel_d = P["rel_bias"][:, 8:]
    d = {}
    if "D_bias" in _HOST_CACHE:
        d["D_bias"] = _HOST_CACHE["D_bias"]
    d["D_kT"] = np.stack([kT_window(G, 3 + t, None, c, 96, dup=False) for t in range(4)])
    base = c * TPC
    dv = G["tm"][:, 384:896]

    def vtile(tok):
        out = np.zeros((128, 8, VW), NPBF)
        out[:, :, 64] = 1.0
        ok = tok >= 0
        if ok.any():
            out[ok, :, 0:64] = dv[tok[ok]].reshape(-1, 8, 64)
        return out

    tiles = []
    for T in range(111, 128):
        tiles.append(vtile(base + (T - 112) * 128 + np.arange(128)))
    for U in range(-1, 4):
        for rho in range(4):
            tiles.append(vtile(base + 512 * U + rho + 4 * np.arange(128)))
    for cs in range(2):
        for r in range(16):
            tiles.append(vtile(base + (cs - 1) * 2048 + r + 16 * np.arange(128)))
    V = np.stack(tiles, axis=1)
    d["D_v"] = np.stack([np.ascontiguousarray(V[:, :, 2 * t:2 * t + 2, :]).reshape(128, -1) for t in range(4)])
    if "D_bias" in d:
        return d
    DB = np.zeros((3, 8, 2, 128, 128), np.float32)
    for p_, dil in enumerate((1, 4, 16)):
        for s_ in range(8):
            for w in range(2):
                dist = 128 * w + _QI - _KI
                DB[p_, s_, w] = toeplitz_tile(rel_d[:, p_ * 8 + s_], dist * dil, (dist >= 0) & (dist <= 128))
    d["D_bias"] = _HOST_CACHE["D_bias"] = DB.reshape(48, 128, 128)
    return d


def emit_ssd(g, es, W, fm_tile0, dtraw, full, Hin=None, out_S=None, out_D=None):
    S, nc, C = g.S, g.nc, g.C
    triu = sb(es, nc, "triu", [128, 128], F32)
    ones = sb(es, nc, "onesf", [128, 128], F32)
    S.op("pool", lambda e: e.memset(ones[:], 1.0), writes=["onesf"])
    S.op("pool", lambda e: e.memset(triu[:], 1.0), writes=["triu"])
    S.op("pool", lambda e: e.affine_select(out=triu[:], in_=triu[:], pattern=[[1, 128]], compare_op=ALU.is_ge, fill=0.0, base=0,
                                            channel_multiplier=-1), reads=["triu"], writes=["triu"])
    cw = sb(es, nc, "cw", [128, 8, 4], F32)
    cbias = sb(es, nc, "cbias", [128, 8], F32)
    S.dma("sp", cw[:], g.b_cw[:, :, :], writes=["cw"])
    S.dma("sp", cbias[:], g.b_cb[:, :], writes=["cbias"])
    par = sb(es, nc, "bpar", [128, 3, 8], F32)
    S.dma("sp", par[:], g.b_par[:, :, :], writes=["bpar"])
    xsT = sb(es, nc, "xsT", [128, 4, TPC], F32)
    BT = sb(es, nc, "BT", [128, 2, TPC], BF16)
    CT = sb(es, nc, "CT", [128, 2, TPC], BF16)
    with Scope(S, nc) as e1:
        wl = WLoader(S, nc, e1, W, 128, name="wssd")
        raw = sb(e1, nc, "craw", [128, EXT], F32)
        acc = sb(e1, nc, "cacc", [128, TPC], F32)
        tmp = sb(e1, nc, "ctmp", [128, TPC], F32)
        pi = 0
        for ti in range(8):
            wb, wk = wl.load((fm_tile0 + ti) * 128)
            for (t0, w) in tok_groups(0, EXT):
                p, pk = g.ps[pi % 2], f"ps{pi % 2}"
                pi += 1
                mm_fm(S, p, pk, wb, wk, g.hT, t0, w)
                S.op("act", lambda e: e.copy(out=raw[:, t0:t0 + w], in_=p[:, :w]), reads=[pk], writes=["craw"])
            S.op("dve", lambda e: e.tensor_scalar(out=acc[:], in0=raw[:, 125:125 + TPC], scalar1=cw[:, ti, 0:1], scalar2=cbias[:, ti:ti + 1],
                                                  op0=ALU.mult, op1=ALU.add), reads=["craw", "cw", "cbias"], writes=["cacc"])
            for k in range(1, 4):
                S.op("dve", lambda e: e.scalar_tensor_tensor(out=acc[:], in0=raw[:, 125 + k:125 + k + TPC], scalar=cw[:, ti, k:k + 1], in1=acc[:],
                                                             op0=ALU.mult, op1=ALU.add), reads=["craw", "cw", "cacc"], writes=["cacc"])
            S.op("act", lambda e: e.activation(out=tmp[:], in_=acc[:], func=AF.Tanh, scale=0.5), reads=["cacc"], writes=["ctmp"])
            S.op("dve", lambda e: e.tensor_scalar(out=tmp[:], in0=tmp[:], scalar1=0.5, scalar2=0.5, op0=ALU.mult, op1=ALU.add), reads=["ctmp"], writes=["ctmp"])
            if ti < 4:
                S.op("dve", lambda e: e.tensor_tensor(out=xsT[:, ti, :], in0=tmp[:], in1=acc[:], op=ALU.mult), reads=["ctmp", "cacc"], writes=["xsT"])
            elif ti < 6:
                S.op("dve", lambda e: e.tensor_tensor(out=BT[:, ti - 4, :], in0=tmp[:], in1=acc[:], op=ALU.mult), reads=["ctmp", "cacc"], writes=["BT"])
            else:
                S.op("dve", lambda e: e.tensor_tensor(out=CT[:, ti - 6, :], in0=tmp[:], in1=acc[:], op=ALU.mult), reads=["ctmp", "cacc"], writes=["CT"])
    dt = sb(es, nc, "dt", [128, NT, 8], F32)
    adt = sb(es, nc, "adt", [128, NT, 8], F32)
    aexp = sb(es, nc, "aexp", [128, 8], F32)
    S.op("act", lambda e: e.activation(out=aexp[:], in_=par[:, 1, :], func=AF.Exp), reads=["bpar"], writes=["aexp"])
    for n in range(NT):
        S.op("dve", lambda e: e.tensor_tensor(out=dt[:, n, :], in0=dtraw[:, n + 1, :], in1=par[:, 0, :], op=ALU.add), reads=["dtraw", "bpar"], writes=["dt"])
    S.op("act", lambda e: e.activation(out=dt[:], in_=dt[:], func=AF.Exp), reads=["dt"], writes=["dt"])
    S.op("dve", lambda e: e.tensor_scalar(out=dt[:], in0=dt[:], scalar1=1.0, scalar2=None, op0=ALU.add), reads=["dt"], writes=["dt"])
    S.op("act", lambda e: e.activation(out=dt[:], in_=dt[:], func=AF.Ln), reads=["dt"], writes=["dt"])
    for n in range(NT):
        S.op("dve", lambda e: e.scalar_tensor_tensor(out=adt[:, n, :], in0=dt[:, n, :], scalar=-1.0, in1=aexp[:], op0=ALU.mult, op1=ALU.mult),
             reads=["dt", "aexp"], writes=["adt"])
    H = sb(es, nc, "Hst", [128, 512], F32)
    Hb = sb(es, nc, "Hb", [128, 512], BF16)
    sumtot = sb(es, nc, "sumtot", [128, 8], F32)
    S.op("pool", lambda e: e.memset(H[:], 0.0), writes=["Hst"])
    S.op("pool", lambda e: e.memset(sumtot[:], 0.0), writes=["sumtot"])
    if Hin is not None:
        Sp, Dp = Hin
        sp_sb = sb(es, nc, "sprev", [128, 512], F32)
        dp_sb = sb(es, nc, "dprev", [128, 56], F32)
        S.dma("sp", dp_sb[:], Dp[:, :], writes=["dprev"])
        for s_ in range(7):
            S.dma("sp", sp_sb[:], Sp[:, s_ * 512:(s_ + 1) * 512], writes=["sprev"])
            for j in range(8):
                S.op("dve", lambda e: e.tensor_scalar(out=H[:, j * 64:(j + 1) * 64], in0=H[:, j * 64:(j + 1) * 64], scalar1=dp_sb[:, s_ * 8 + j:s_ * 8 + j + 1],
                                                      scalar2=None, op0=ALU.mult), reads=["Hst", "dprev"], writes=["Hst"])
            S.op("dve", lambda e: e.tensor_tensor(out=H[:], in0=H[:], in1=sp_sb[:], op=ALU.add), reads=["Hst", "sprev"], writes=["Hst"])
    S.op("dve", lambda e: e.tensor_copy(out=Hb[:], in_=H[:]), reads=["Hst"], writes=["Hb"])
    xs = sb(es, nc, "xs_tm", [128, 512], F32)
    xdt = sb(es, nc, "xdt", [128, 512], F32)
    xdtb = sb(es, nc, "xdtb", [128, 512], BF16)
    xdtd = sb(es, nc, "xdtd", [128, 512], BF16)
    Btm = sb(es, nc, "Btm", [128, 256], BF16)
    acs = sb(es, nc, "acs", [128, 16], F32)
    eacs = sb(es, nc, "eacs", [128, 8], F32)
    dend = sb(es, nc, "dend", [128, 8], F32)
    dch = sb(es, nc, "dch", [128, 8], F32)
    if full:
        cbm = sb(es, nc, "cbm", [128, 2, 128], F32)
        adtb = sb(es, nc, "adtb", [128, 128], F32)
        dsb = sb(es, nc, "dsb", [128, 128], F32)
        Mt = sb(es, nc, "Mt", [128, 128], BF16)
        ysb = sb(es, nc, "ysb", [128, 512], F32)
        y2 = sb(es, nc, "y2", [128, 512], F32)
        zt = sb(es, nc, "zt", [128, 512], F32)
        dsk = sb(es, nc, "dsk", [128, 512], F32)
        bnw = sb(es, nc, "bnw", [128, 512], F32)
        S.dma("sp", dsk[:], g.b_dskip[:, :], writes=["dsk"])
        S.dma("sp", bnw[:], g.b_norm[:, :], writes=["bnw"])
        ssq = sb(es, nc, "ssq", [128, 2], F32)
        mst = sb(es, nc, "mst", [128, 512], BF16)
    for n in range(NT):
        cs = slice(n * 128, (n + 1) * 128)
        for ti in range(4):
            S.op("pe", lambda e: e.matmul(g.ps[0][:, ti * 128:(ti + 1) * 128], lhsT=xsT[:, ti, cs], rhs=C["identf"][:, :], start=True, stop=True),
                 reads=["xsT", "identf"], writes=["ps0"])
        S.op("act", lambda e: e.copy(out=xs[:], in_=g.ps[0][:, :]), reads=["ps0"], writes=["xs_tm"])
        for gp in range(2):
            S.op("pe", lambda e: e.transpose(out=g.ps_bf[:, gp * 128:(gp + 1) * 128], in_=BT[:, gp, cs], identity=C["identb"][:]),
                 reads=["BT", "identb"], writes=["ps_bf"])
        S.op("act", lambda e: e.copy(out=Btm[:], in_=g.ps_bf[:, 0:256]), reads=["ps_bf"], writes=["Btm"])
        S.op("pe", lambda e: e.matmul(g.ps[1][:, 0:8], lhsT=triu[:, :], rhs=adt[:, n, :], start=True, stop=True), reads=["triu", "adt"], writes=["ps1"])
        S.op("pe", lambda e: e.matmul(g.ps[1][:, 8:16], lhsT=ones[:, :], rhs=adt[:, n, :], start=True, stop=True), reads=["onesf", "adt"], writes=["ps1"])
        S.op("dve", lambda e: e.tensor_copy(out=acs[:], in_=g.ps[1][:, 0:16]), reads=["ps1"], writes=["acs"])
        S.op("act", lambda e: e.activation(out=eacs[:], in_=acs[:, 0:8], func=AF.Exp), reads=["acs"], writes=["eacs"])
        S.op("act", lambda e: e.activation(out=dch[:], in_=acs[:, 8:16], func=AF.Exp), reads=["acs"], writes=["dch"])
        S.op("dve", lambda e: e.tensor_tensor(out=dend[:], in0=acs[:, 8:16], in1=acs[:, 0:8], op=ALU.subtract), reads=["acs"], writes=["dend"])
        S.op("act", lambda e: e.activation(out=dend[:], in_=dend[:], func=AF.Exp), reads=["dend"], writes=["dend"])
        S.op("dve", lambda e: e.tensor_tensor(out=sumtot[:], in0=sumtot[:], in1=acs[:, 8:16], op=ALU.add), reads=["sumtot", "acs"], writes=["sumtot"])
        for j in range(8):
            js = slice(j * 64, (j + 1) * 64)
            S.op("dve", lambda e: e.tensor_scalar(out=xdt[:, js], in0=xs[:, js], scalar1=dt[:, n, j:j + 1], scalar2=None, op0=ALU.mult),
                 reads=["xs_tm", "dt"], writes=["xdt"])
            S.op("pool", lambda e: e.tensor_scalar(out=xdtd[:, js], in0=xdt[:, js], scalar1=dend[:, j:j + 1], scalar2=None, op0=ALU.mult),
                 reads=["xdt", "dend"], writes=["xdtd"])
        S.op("act", lambda e: e.copy(out=xdtb[:], in_=xdt[:]), reads=["xdt"], writes=["xdtb"])
        if full:
            for gp in range(2):
                S.op("pe", lambda e: e.matmul(g.ps[2][:, gp * 128:(gp + 1) * 128], lhsT=BT[:, gp, cs], rhs=CT[:, gp, cs], start=True, stop=True),
                     reads=["BT", "CT"], writes=["ps2"])
            S.op("dve", lambda e: e.tensor_tensor(out=cbm[:], in0=g.ps[2][:, 0:256].rearrange("p (a b) -> p a b", a=2),
                                                  in1=triu[:, :].unsqueeze(1).to_broadcast([128, 2, 128]), op=ALU.mult), reads=["ps2", "triu"], writes=["cbm"])
            for j in range(8):
                gp = j // 4
                js = slice(j * 64, (j + 1) * 64)
                S.op("pool", lambda e: e.tensor_scalar(out=adtb[:], in0=ones[:], scalar1=adt[:, n, j:j + 1], scalar2=None, op0=ALU.mult),
                     reads=["onesf", "adt"], writes=["adtb"])
                S.op("pe", lambda e: e.matmul(g.ps[3][:, 0:128], lhsT=adtb[:, :], rhs=triu[:, :], start=True, stop=True), reads=["adtb", "triu"], writes=["ps3"])
                S.op("dve", lambda e: e.tensor_scalar(out=dsb[:], in0=g.ps[3][:, 0:128], scalar1=acs[:, j:j + 1], scalar2=0.0, op0=ALU.subtract, op1=ALU.min),
                     reads=["ps3", "acs"], writes=["dsb"])
                S.op("act", lambda e: e.activation(out=dsb[:], in_=dsb[:], func=AF.Exp), reads=["dsb"], writes=["dsb"])
                S.op("dve", lambda e: e.tensor_tensor(out=Mt[:], in0=dsb[:], in1=cbm[:, gp, :], op=ALU.mult), reads=["dsb", "cbm"], writes=["Mt"])
                S.op("pe", lambda e: e.matmul(g.ps[4][:, js], lhsT=Mt[:, :], rhs=xdtb[:, js], start=True, stop=True), reads=["Mt", "xdtb"], writes=["ps4"])
            for gp in range(2):
                S.op("pe", lambda e: e.matmul(g.ps[5][:, gp * 256:(gp + 1) * 256], lhsT=CT[:, gp, cs], rhs=Hb[:, gp * 256:(gp + 1) * 256], start=True, stop=True),
                     reads=["CT", "Hb"], writes=["ps5"])
            for j in range(8):
                js = slice(j * 64, (j + 1) * 64)
                S.op("dve", lambda e: e.tensor_scalar(out=ysb[:, js], in0=g.ps[5][:, js], scalar1=eacs[:, j:j + 1], scalar2=None, op0=ALU.mult),
                     reads=["ps5", "eacs"], writes=["ysb"])
            S.op("dve", lambda e: e.tensor_tensor(out=ysb[:], in0=g.ps[4][:, :], in1=ysb[:], op=ALU.add), reads=["ps4", "ysb"], writes=["ysb"])
            S.op("pool", lambda e: e.tensor_tensor(out=y2[:], in0=xs[:], in1=dsk[:], op=ALU.mult), reads=["xs_tm", "dsk"], writes=["y2"])
            S.op("dve", lambda e: e.tensor_tensor(out=ysb[:], in0=ysb[:], in1=y2[:], op=ALU.add), reads=["ysb", "y2"], writes=["ysb"])
            S.dma("sp", zt[:], g.zs[n * 128:(n + 1) * 128, 512:1024], reads=["zs_scr"], writes=["zt"])
            S.op("dve", lambda e: e.tensor_tensor(out=ysb[:], in0=ysb[:], in1=zt[:], op=ALU.mult), reads=["ysb", "zt"], writes=["ysb"])
            S.op("pool", lambda e: e.memset(ssq[:], 0.0), writes=["ssq"])
            for gp in range(2):
                S.op("act", lambda e: e.activation(out=y2[:, gp * 256:(gp + 1) * 256], in_=ysb[:, gp * 256:(gp + 1) * 256], func=AF.Square,
                                                   accum_out=ssq[:, gp:gp + 1]), reads=["ysb", "ssq"], writes=["y2", "ssq"])
            S.op("act", lambda e: e.activation(out=ssq[:], in_=ssq[:], func=AF.Sqrt, bias=C["epsc"][:, 0:1], scale=1.0 / 256), reads=["ssq"], writes=["ssq"])
            S.op("dve", lambda e: e.reciprocal(out=ssq[:], in_=ssq[:]), reads=["ssq"], writes=["ssq"])
            for gp in range(2):
                gs = slice(gp * 256, (gp + 1) * 256)
                S.op("dve", lambda e: e.scalar_tensor_tensor(out=mst[:, gs], in0=ysb[:, gs], scalar=ssq[:, gp:gp + 1], in1=bnw[:, gs], op0=ALU.mult, op1=ALU.mult),
                     reads=["ysb", "ssq", "bnw"], writes=["mst"])
            S.dma("sp", g.mix[n * 128:(n + 1) * 128, 512:1024], mst[:], reads=["mst"], writes=["mix_scr"])
        for gp in range(2):
            S.op("pe", lambda e: e.matmul(g.ps[6][:, gp * 256:(gp + 1) * 256], lhsT=Btm[:, gp * 128:(gp + 1) * 128], rhs=xdtd[:, gp * 256:(gp + 1) * 256],
                                          start=True, stop=True), reads=["Btm", "xdtd"], writes=["ps6"])
        for j in range(8):
            js = slice(j * 64, (j + 1) * 64)
            S.op("dve", lambda e: e.tensor_scalar(out=H[:, js], in0=H[:, js], scalar1=dch[:, j:j + 1], scalar2=None, op0=ALU.mult),
                 reads=["Hst", "dch"], writes=["Hst"])
        S.op("dve", lambda e: e.tensor_tensor(out=H[:], in0=g.ps[6][:, :], in1=H[:], op=ALU.add), reads=["ps6", "Hst"], writes=["Hst"])
        S.op("act", lambda e: e.copy(out=Hb[:], in_=H[:]), reads=["Hst"], writes=["Hb"])
    if out_S is not None:
        S.dma("sp", out_S[:, :], H[:], reads=["Hst"], writes=["o_S"])
        S.op("act", lambda e: e.activation(out=sumtot[:], in_=sumtot[:], func=AF.Exp), reads=["sumtot"], writes=["sumtot"])
        S.dma("sp", out_D[:, :], sumtot[:], reads=["sumtot"], writes=["o_D"])


def phase_B(g):
    with Scope(g.S, g.nc) as es:
        emit_ssd(g, es, g.WB, FMT["b_x"], g.dtraw, True, Hin=(g.b_Sprev, g.b_Dprev))


def host_ssd_params(layer, P):
    bc = lambda v: np.ascontiguousarray(np.broadcast_to(v[None, :], (128, v.shape[0]))).astype(np.float32)
    d = {}
    d["b_cw"] = np.ascontiguousarray(P["b_conv_w"][layer].reshape(4, 8, 128).transpose(2, 1, 0))
    d["b_cb"] = np.ascontiguousarray(P["b_conv_b"][layer].reshape(8, 128).T)
    par = np.zeros((128, 3, 8), np.float32)
    par[:, 0, :] = P["b_dt_bias"][layer][None, :]
    par[:, 1, :] = P["b_a_log"][layer][None, :]
    d["b_par"] = par
    d["b_dskip"] = bc(np.repeat(P["b_d"][layer], 64))
    d["b_norm"] = bc(P["b_norm"][layer])
    return d


def host_B_ssd_chain(c, G):
    Sp = np.zeros((128, 7 * 512), np.float32)
    Dp = np.ones((128, 56), np.float32)
    for s_ in range(7):
        src = c - 7 + s_
        if src >= 0:
            Sp[:, s_ * 512:(s_ + 1) * 512] = G["S"][src]
            Dp[:, s_ * 8:(s_ + 1) * 8] = G["D"][src]
    return {"b_Sprev": Sp, "b_Dprev": Dp}


def get_nc(name):
    if name not in _NC_CACHE:
        _NC_CACHE[name] = {"A": build_A, "B": build_B}[name]()
    return _NC_CACHE[name]


def kernel(**inputs):
    P = {k: np.asarray(v) for k, v in inputs.items()}
    _HOST_CACHE.clear()
    x = np.ascontiguousarray(P["x"][0], dtype=np.float32)
    cores = list(range(NCORE))
    for layer in range(4):
        resA = run_bass_kernel_spmd(get_nc("A"), [host_A_inputs(c, x, layer, P) for c in cores], core_ids=cores)
        G = gather_A(resA.results)
        resB = run_bass_kernel_spmd(get_nc("B"), [host_B_inputs(c, x, layer, P, G) for c in cores], core_ids=cores)
        x = np.concatenate([np.asarray(resB.results[c]["x_out"], dtype=np.float32) for c in cores], axis=0)
    return x[None].astype(np.float32)
```
